# Optimizing a Trainium2 kernel written in Bass

```python
import math
import jax, jax.numpy as jnp
from jax import lax
import numpy as np

D_MODEL = 2048
BATCH = 16
SEQ = 2048
DEPTH = 2

D_FF = 4 * D_MODEL
GMLP_CHUNK = 128
GMLP_WIDTH = 2 * D_MODEL
GMLP_GROUPS = 16
GMLP_GROUP_DIM = GMLP_WIDTH // GMLP_GROUPS
HEAD_DIM = 128
N_HEADS = D_MODEL // HEAD_DIM
N_KV_GROUPS = 4
HEADS_PER_GROUP = N_HEADS // N_KV_GROUPS
N_BRANCHES = 3
CMP_BLOCK = 32
CMP_STRIDE = 16
CMP_HIDDEN = 256
SEL_BLOCK = 64
N_SELECT = 8
SEL_QUERY_BLOCK = 32
WINDOW = 512
WIN_QUERY_BLOCK = 128
ROPE_THETA = 10000.0
EPS = 1e-6
NEG_INF = -1e30
FORCE_BONUS = 1e6

kernel_name = "yoco_gmlp_nsa_hybrid"


def rms_norm(x, g):
    x32 = x.astype(jnp.float32)
    y = x32 * lax.rsqrt(jnp.mean(x32 * x32, axis=-1, keepdims=True) + EPS)
    return y.astype(x.dtype) * g


def layer_norm(x, g, b):
    x32 = x.astype(jnp.float32)
    mu = jnp.mean(x32, axis=-1, keepdims=True)
    var = jnp.mean(jnp.square(x32 - mu), axis=-1, keepdims=True)
    return ((x32 - mu) * lax.rsqrt(var + EPS)).astype(x.dtype) * g + b


def modulate(h, shift, scale):
    return h * (1.0 + scale[:, None, :]) + shift[:, None, :]


def rope(x, pos):
    half = HEAD_DIM // 2
    freqs = ROPE_THETA ** (-jnp.arange(half, dtype=jnp.float32) / half)
    ang = pos.astype(jnp.float32)[:, None] * freqs[None, :]
    cos = jnp.cos(ang)[:, None, :].astype(x.dtype)
    sin = jnp.sin(ang)[:, None, :].astype(x.dtype)
    x1, x2 = x[..., :half], x[..., half:]
    return jnp.concatenate([x1 * cos - x2 * sin, x2 * cos + x1 * sin], axis=-1)


def masked_softmax(s, mask, axis):
    s = jnp.where(mask, s.astype(jnp.float32), NEG_INF)
    m = jnp.max(s, axis=axis, keepdims=True)
    e = jnp.where(mask, jnp.exp(s - m), 0.0)
    den = jnp.sum(e, axis=axis, keepdims=True)
    return e / jnp.maximum(den, 1e-30)


def sq_relu_mlp(h, w1, w2):
    return jnp.square(jax.nn.relu(h @ w1)) @ w2


def gmlp_mixer(h, w_in, b_in, ln_g, ln_b, w_s, b_s, w_out, b_out):
    B, S, _ = h.shape
    z = jax.nn.gelu(h @ w_in + b_in)
    u, v = jnp.split(z, 2, axis=-1)
    v = layer_norm(v, ln_g, ln_b)
    v = v.reshape(B, S // GMLP_CHUNK, GMLP_CHUNK, GMLP_GROUPS, GMLP_GROUP_DIM)
    causal = jnp.tril(jnp.ones((GMLP_CHUNK, GMLP_CHUNK), dtype=bool))
    w_causal = jnp.where(causal[None], w_s, 0.0)
    v = jnp.einsum('gts,bnsgc->bntgc', w_causal, v) + b_s.T[None, None, :, :, None]
    return (u * v.reshape(B, S, GMLP_WIDTH)) @ w_out + b_out


def compress_blocks(x, pos_emb, w1, w2):
    B, S, G, d = x.shape
    halves = x.reshape(B, S // CMP_STRIDE, CMP_STRIDE, G, d)
    blocks = jnp.concatenate([halves[:, :-1], halves[:, 1:]], axis=2)
    blocks = blocks + pos_emb[None, None, :, None, :]
    flat = jnp.moveaxis(blocks, 2, 3).reshape(B, blocks.shape[1], G, CMP_BLOCK * d)
    return jax.nn.gelu(flat @ w1) @ w2


def nsa_shared_kv(h, w_kv, cmp_pos_k, cmp_w1_k, cmp_w2_k, cmp_pos_v, cmp_w1_v, cmp_w2_v):
    B, S, _ = h.shape
    pos = jnp.arange(S)
    kv = (h @ w_kv).reshape(B, S, 2 * N_BRANCHES, N_KV_GROUPS, HEAD_DIM)
    k_cmp, v_cmp = kv[:, :, 0], kv[:, :, 1]
    k_slc, v_slc = kv[:, :, 2], kv[:, :, 3]
    k_win, v_win = kv[:, :, 4], kv[:, :, 5]
    k_cmp = compress_blocks(rope(k_cmp, pos), cmp_pos_k, cmp_w1_k, cmp_w2_k)
    v_cmp = compress_blocks(v_cmp, cmp_pos_v, cmp_w1_v, cmp_w2_v)
    return k_cmp, v_cmp, rope(k_slc, pos), v_slc, rope(k_win, pos), v_win


def nsa_mixer(h, k_cmp, v_cmp, k_slc, v_slc, k_win, v_win, w_qg, w_o):
    B, S, _ = h.shape
    G, HG, d = N_KV_GROUPS, HEADS_PER_GROUP, HEAD_DIM
    pos = jnp.arange(S)
    qg = h @ w_qg
    q = qg[..., :N_HEADS * d].reshape(B, S, N_HEADS, d)
    gates = jax.nn.sigmoid(qg[..., N_HEADS * d:].astype(jnp.float32))
    gates = gates.reshape(B, S, G, HG, N_BRANCHES).astype(h.dtype)
    q = (rope(q, pos) * (d ** -0.5)).reshape(B, S, G, HG, d)

    n_cmp = k_cmp.shape[1]
    s_cmp = jnp.einsum('btghd,bngd->bghtn', q, k_cmp)
    cmp_end = jnp.arange(n_cmp) * CMP_STRIDE + CMP_BLOCK - 1
    p_cmp = masked_softmax(s_cmp, cmp_end[None, :] <= pos[:, None], axis=-1)
    o_cmp = jnp.einsum('bghtn,bngd->btghd', p_cmp.astype(h.dtype), v_cmp)

    n_slc = S // SEL_BLOCK
    ci = jnp.arange(n_cmp)[:, None]
    sj = jnp.arange(n_slc)[None, :]
    overlap = ((ci * CMP_STRIDE <= sj * SEL_BLOCK + SEL_BLOCK - 1)
               & (ci * CMP_STRIDE + CMP_BLOCK - 1 >= sj * SEL_BLOCK)).astype(jnp.float32)
    imp = jnp.einsum('bghtn,nj->bgtj', p_cmp, overlap)
    blk = jnp.arange(n_slc)[None, :]
    cur = (pos // SEL_BLOCK)[:, None]
    forced = (blk == 0) | (blk == cur) | (blk == cur - 1)
    imp = jnp.where(blk * SEL_BLOCK <= pos[:, None],
                    imp + jnp.where(forced, FORCE_BONUS, 0.0), NEG_INF)
    n_sel = min(N_SELECT, n_slc)
    _, sel_idx = lax.top_k(imp, n_sel)

    k_blk = k_slc.reshape(B, n_slc, SEL_BLOCK, G, d).transpose(0, 3, 1, 2, 4)
    v_blk = v_slc.reshape(B, n_slc, SEL_BLOCK, G, d).transpose(0, 3, 1, 2, 4)
    nq = S // SEL_QUERY_BLOCK
    q_sel = jnp.moveaxis(q.reshape(B, nq, SEL_QUERY_BLOCK, G, HG, d), 1, 0)
    idx_sel = jnp.moveaxis(sel_idx.reshape(B, G, nq, SEL_QUERY_BLOCK, n_sel), 2, 0)
    b_ix = jnp.arange(B)[:, None, None, None]
    g_ix = jnp.arange(G)[None, :, None, None]

    def sel_block(args):
        qb, ib, start = args
        kg = k_blk[b_ix, g_ix, ib]
        vg = v_blk[b_ix, g_ix, ib]
        s = jnp.einsum('btghd,bgtsrd->bghtsr', qb, kg)
        tpos = start + jnp.arange(SEL_QUERY_BLOCK)
        kpos = ib[..., None] * SEL_BLOCK + jnp.arange(SEL_BLOCK)
        mask = kpos <= tpos[None, None, :, None, None]
        p = masked_softmax(s, mask[:, :, None], axis=(-2, -1))
        return jnp.einsum('bghtsr,bgtsrd->btghd', p.astype(qb.dtype), vg)

    o_slc = lax.map(sel_block, (q_sel, idx_sel, jnp.arange(nq) * SEL_QUERY_BLOCK))
    o_slc = jnp.moveaxis(o_slc, 0, 1).reshape(B, S, G, HG, d)

    nw = S // WIN_QUERY_BLOCK
    span = WINDOW + WIN_QUERY_BLOCK
    k_pad = jnp.pad(k_win, ((0, 0), (WINDOW, 0), (0, 0), (0, 0)))
    v_pad = jnp.pad(v_win, ((0, 0), (WINDOW, 0), (0, 0), (0, 0)))
    q_win = jnp.moveaxis(q.reshape(B, nw, WIN_QUERY_BLOCK, G, HG, d), 1, 0)

    def win_block(args):
        qb, start = args
        kb = lax.dynamic_slice_in_dim(k_pad, start, span, axis=1)
        vb = lax.dynamic_slice_in_dim(v_pad, start, span, axis=1)
        s = jnp.einsum('btghd,bkgd->bghtk', qb, kb)
        tpos = (start + jnp.arange(WIN_QUERY_BLOCK))[:, None]
        kpos = (start - WINDOW + jnp.arange(span))[None, :]
        mask = (kpos <= tpos) & (kpos > tpos - WINDOW) & (kpos >= 0)
        p = masked_softmax(s, mask, axis=-1)
        return jnp.einsum('bghtk,bkgd->btghd', p.astype(qb.dtype), vb)

    o_win = lax.map(win_block, (q_win, jnp.arange(nw) * WIN_QUERY_BLOCK))
    o_win = jnp.moveaxis(o_win, 0, 1).reshape(B, S, G, HG, d)

    o = (gates[..., 0:1] * o_cmp + gates[..., 1:2] * o_slc + gates[..., 2:3] * o_win)
    return o.reshape(B, S, N_HEADS * d) @ w_o


def setup_inputs(seed: int = 0) -> dict:
    key = jax.random.key(seed)
    ks = iter(jax.random.split(key, 40))

    def nrm(shape, scale):
        return jax.random.normal(next(ks), shape, jnp.float32) * scale

    D = D_MODEL
    n_a = DEPTH // 2
    n_b = DEPTH - n_a
    kv_cols = 2 * N_BRANCHES * N_KV_GROUPS * HEAD_DIM
    qg_cols = N_HEADS * HEAD_DIM + N_HEADS * N_BRANCHES
    return {
        "x": nrm((BATCH, SEQ, D), 1.0),
        "c": nrm((BATCH, D), 1.0),
        "mod_w": nrm((DEPTH, 2, D, 3 * D), 0.5 * D ** -0.5),
        "mod_b": nrm((DEPTH, 2, 3 * D), 0.02),
        "norm_g": 1.0 + nrm((DEPTH, 2, D), 0.02),
        "mlp_w1": nrm((DEPTH, D, D_FF), D ** -0.5),
        "mlp_w2": nrm((DEPTH, D_FF, D), D_FF ** -0.5),
        "a_w_in": nrm((n_a, D, 2 * GMLP_WIDTH), D ** -0.5),
        "a_b_in": nrm((n_a, 2 * GMLP_WIDTH), 0.02),
        "a_ln_g": 1.0 + nrm((n_a, GMLP_WIDTH), 0.02),
        "a_ln_b": nrm((n_a, GMLP_WIDTH), 0.02),
        "a_w_s": nrm((n_a, GMLP_GROUPS, GMLP_CHUNK, GMLP_CHUNK), GMLP_CHUNK ** -0.5),
        "a_b_s": 1.0 + nrm((n_a, GMLP_GROUPS, GMLP_CHUNK), 0.02),
        "a_w_out": nrm((n_a, GMLP_WIDTH, D), GMLP_WIDTH ** -0.5),
        "a_b_out": nrm((n_a, D), 0.02),
        "kv_norm_g": 1.0 + nrm((D,), 0.02),
        "kv_mod_w": nrm((D, 2 * D), 0.5 * D ** -0.5),
        "kv_mod_b": nrm((2 * D,), 0.02),
        "w_kv": nrm((D, kv_cols), D ** -0.5),
        "cmp_pos_k": nrm((CMP_BLOCK, HEAD_DIM), 0.5),
        "cmp_w1_k": nrm((CMP_BLOCK * HEAD_DIM, CMP_HIDDEN), (CMP_BLOCK * HEAD_DIM) ** -0.5),
        "cmp_w2_k": nrm((CMP_HIDDEN, HEAD_DIM), CMP_HIDDEN ** -0.5),
        "cmp_pos_v": nrm((CMP_BLOCK, HEAD_DIM), 0.5),
        "cmp_w1_v": nrm((CMP_BLOCK * HEAD_DIM, CMP_HIDDEN), (CMP_BLOCK * HEAD_DIM) ** -0.5),
        "cmp_w2_v": nrm((CMP_HIDDEN, HEAD_DIM), CMP_HIDDEN ** -0.5),
        "b_w_qg": nrm((n_b, D, qg_cols), D ** -0.5),
        "b_w_o": nrm((n_b, N_HEADS * HEAD_DIM, D), (N_HEADS * HEAD_DIM) ** -0.5),
        "final_g": 1.0 + nrm((D,), 0.02),
    }


def reference(x, c, mod_w, mod_b, norm_g, mlp_w1, mlp_w2,
              a_w_in, a_b_in, a_ln_g, a_ln_b, a_w_s, a_b_s, a_w_out, a_b_out,
              kv_norm_g, kv_mod_w, kv_mod_b, w_kv,
              cmp_pos_k, cmp_w1_k, cmp_w2_k, cmp_pos_v, cmp_w1_v, cmp_w2_v,
              b_w_qg, b_w_o, final_g):
    n_a = a_w_in.shape[0]
    cond = jax.nn.silu(c)
    shared = None
    for layer in range(DEPTH):
        if layer == n_a:
            kv_shift, kv_scale = jnp.split(cond @ kv_mod_w + kv_mod_b, 2, axis=-1)
            h_kv = modulate(rms_norm(x, kv_norm_g), kv_shift, kv_scale)
            shared = nsa_shared_kv(h_kv, w_kv, cmp_pos_k, cmp_w1_k, cmp_w2_k,
                                   cmp_pos_v, cmp_w1_v, cmp_w2_v)
        mod = jnp.einsum('bd,sde->sbe', cond, mod_w[layer]) + mod_b[layer][:, None, :]

        shift, scale, gate = jnp.split(mod[0], 3, axis=-1)
        h = modulate(rms_norm(x, norm_g[layer, 0]), shift, scale)
        if layer < n_a:
            y = gmlp_mixer(h, a_w_in[layer], a_b_in[layer], a_ln_g[layer], a_ln_b[layer],
                           a_w_s[layer], a_b_s[layer], a_w_out[layer], a_b_out[layer])
        else:
            j = layer - n_a
            y = nsa_mixer(h, shared[0], shared[1], shared[2], shared[3], shared[4], shared[5],
                          b_w_qg[j], b_w_o[j])
        x = x + gate[:, None, :] * y

        shift, scale, gate = jnp.split(mod[1], 3, axis=-1)
        h = modulate(rms_norm(x, norm_g[layer, 1]), shift, scale)
        x = x + gate[:, None, :] * sq_relu_mlp(h, mlp_w1[layer], mlp_w2[layer])
    return rms_norm(x, final_g)
```

```python
import math
from contextlib import ExitStack

import numpy as np
import concourse.bass as bass
import concourse.mybir as mybir
from concourse.bass_utils import run_bass_kernel_spmd

F32 = mybir.dt.float32
BF16 = mybir.dt.bfloat16
AF = mybir.ActivationFunctionType
ALU = mybir.AluOpType
AX = mybir.AxisListType

D = 2048
S = 2048
NSEQ = 2
TB = 512
NBLK = S // TB
KC = D // 128
DFF = 8192
GW = 4096
EPS = 1e-6
NEGM = -30000.0


class Tok:
    __slots__ = ("sem", "val")

    def __init__(self, sem, val):
        self.sem = sem
        self.val = val


class T:
    def __init__(self, ap):
        self.ap = ap
        self.w = None
        self.r = {}

    def __getitem__(self, k):
        return self.ap[k]


class Eng:
    def __init__(self, nc, name, eng, stack):
        self.name = name
        self.eng = eng
        self.sem = stack.enter_context(nc.semaphore("s_" + name))
        self.cnt = 0
        self.waited = {}

    def wait(self, tok, raw):
        if tok is None:
            return
        if tok.sem is self.sem and not raw:
            return
        k = id(tok.sem)
        if self.waited.get(k, 0) >= tok.val:
            return
        self.eng.wait_ge(tok.sem, tok.val)
        self.waited[k] = tok.val


class Sched:
    NDMASEM = 16

    def __init__(self, nc, stack):
        self.nc = nc
        self.pe = Eng(nc, "pe", nc.tensor, stack)
        self.act = Eng(nc, "act", nc.scalar, stack)
        self.dve = Eng(nc, "dve", nc.vector, stack)
        self.pool = Eng(nc, "pool", nc.gpsimd, stack)
        self.sp = Eng(nc, "sp", nc.sync, stack)
        self.engs = [self.pe, self.act, self.dve, self.pool, self.sp]
        self.dsems = {id(q): [stack.enter_context(nc.semaphore("dma%s%d" % (q.name, i))) for i in range(self.NDMASEM)]
                      for q in (self.sp, self.pool)}
        self.dn = {id(q): 0 for q in (self.sp, self.pool)}
        self.dma_toks = {}

    def _deps(self, E, reads, writes):
        for t in reads:
            E.wait(t.w, True)
        for t in writes:
            E.wait(t.w, False)
            for tok in t.r.values():
                E.wait(tok, False)

    def _upd(self, tok, reads, writes):
        for t in writes:
            t.w = tok
            t.r = {}
        for t in reads:
            t.r[id(tok.sem)] = tok

    def op(self, E, inst_fn, reads=(), writes=()):
        self._deps(E, reads, writes)
        inst = inst_fn(E.eng)
        E.cnt += 1
        inst.then_inc(E.sem, 1)
        self._upd(Tok(E.sem, E.cnt), reads, writes)

    def dma(self, Q, out_ap, in_ap, reads=(), writes=()):
        n = self.dn[id(Q)]
        slot = n % self.NDMASEM
        sem = self.dsems[id(Q)][slot]
        val = 16 * (n // self.NDMASEM + 1)
        self.dn[id(Q)] = n + 1
        if val > 16:
            Q.wait(Tok(sem, val - 16), True)
        self._deps(Q, reads, writes)
        Q.eng.dma_start(out=out_ap, in_=in_ap).then_inc(sem, 16)
        tok = Tok(sem, val)
        self.dma_toks[(id(Q), slot)] = tok
        self._upd(tok, reads, writes)

    def barrier(self):
        toks = [Tok(e.sem, e.cnt) for e in self.engs if e.cnt > 0] + list(self.dma_toks.values())
        for e in self.engs:
            for tok in toks:
                e.wait(tok, True)

    def finish(self):
        for tok in self.dma_toks.values():
            self.sp.wait(tok, True)


VOFF = {}
_o = 0
for _n, _w in [("mod_b00", 48), ("mod_b01", 48), ("mod_b10", 48), ("mod_b11", 48), ("kv_mod_b", 32),
               ("g00", 16), ("g01", 16), ("g10", 16), ("g11", 16), ("gkv", 16), ("gfin", 16),
               ("b_in_u", 32), ("ln_g", 32), ("ln_b", 32), ("b_out", 16)]:
    VOFF[_n] = (_o, _w)
    _o += _w
NV = _o


def fm(v):
    v = np.asarray(v, np.float32)
    return np.ascontiguousarray(v.reshape(-1, 128).T)


class Ctx:
    pass


def build_program(debug=False, stop_after=None):
    nc = bass.Bass("TRN2", target_bir_lowering=False)
    C = Ctx()
    C.nc = nc

    def din(name, shape, dt=F32):
        return nc.dram_tensor(name, list(shape), dt, kind="ExternalInput").ap()

    dbgkind = "ExternalOutput" if debug else "Internal"
    x_d = din("x", [NSEQ, S, D])
    cT_d = din("cT", [128, KC, NSEQ])
    vecs_d = din("vecs", [128, NV])
    modw_d = din("mod_w", [2, 2, D, 3 * D])
    kvmodw_d = din("kv_mod_w", [D, 2 * D])
    w1_d = din("mlp_w1", [2, D, DFF])
    w2_d = din("mlp_w2", [2, DFF, D])
    win_d = din("a_w_in", [D, 2 * GW])
    wout_d = din("a_w_out", [GW, D])
    binv_d = din("b_in_v", [1, GW])
    wsT_d = din("w_sT", [128, 16, 128])
    bs_d = din("b_s", [1, 2048])
    wkv_d = din("w_kv", [D, 3072])
    cpos_d = din("cmp_posT", [2, 128, 32])
    cw1_d = din("cmp_w1", [2, 4096, 256])
    cw2_d = din("cmp_w2", [2, 256, 128])
    wqg_d = din("w_qg", [D, 2096])
    wo_d = din("w_o", [D, D])
    cos_d = din("cos2", [128, S])
    sin_d = din("sinS", [128, S])
    ident_d = din("ident", [128, 128])
    mcmp_d = din("m_cmp", [128, 16, 128])
    mtri_d = din("m_tri", [128, 2, 128])
    addtab_d = din("addtab", [128, 16, 32])
    emat_d = din("emat", [32, S])
    ovl_d = din("ovl", [128, 33])
    out_d = nc.dram_tensor("out", [NSEQ, S, D], F32, kind="ExternalOutput").ap()

    x2T_d = nc.dram_tensor("x2T", [NSEQ * NBLK, 128, KC, TB], F32, kind=dbgkind).ap()
    kT_d = nc.dram_tensor("kT", [4, NSEQ, 128, 4, S], BF16, kind=dbgkind).ap()
    vtm_d = nc.dram_tensor("vtm", [2, NSEQ, S, 512], BF16, kind=dbgkind).ap()
    dbg_d = nc.dram_tensor("dbg", [8, 128, KC, TB], F32, kind=dbgkind).ap()
    dbgc_d = nc.dram_tensor("dbgc", [NSEQ, 2, 128, 4, 128], BF16, kind=dbgkind).ap()

    with ExitStack() as st:
        sc = Sched(nc, st)
        C.sc = sc
        PE, ACT, DVE, POOL, SP = sc.pe, sc.act, sc.dve, sc.pool, sc.sp

        sbn = [0]
        grave = {}

        def retire(t):
            toks = list(t.r.values())
            if t.w is not None:
                toks.append(t.w)
            for tok in toks:
                k = id(tok.sem)
                if k not in grave or grave[k].val < tok.val:
                    grave[k] = tok

        def sb(name, shape, dt, stack=st):
            sbn[0] += 1
            t = T(stack.enter_context(nc.sbuf_tensor("sb%d_%s" % (sbn[0], name), list(shape), dt)))
            t.r = dict(grave)
            stack.callback(retire, t)
            return t

        banks = [T(st.enter_context(nc.psum_tensor("bank%d" % i, [128, 512], F32))) for i in range(7)]
        pbf = T(st.enter_context(nc.psum_tensor("pbf", [128, 1024], BF16)))
        bank_i = [0]

        def nb():
            b = banks[bank_i[0] % 6]
            bank_i[0] += 1
            return b

        xT = [sb("xT%d" % c, [128, TB], F32) for c in range(KC)]
        hT = sb("hT", [128, KC, TB], BF16)
        hTc = [T(hT.ap[:, c, :]) for c in range(KC)]
        slots = [sb("slot%d" % i, [128, 16, 512], BF16) for i in range(3)]
        slot_i = [0]
        vecs = sb("vecs", [128, NV], F32)
        identf = sb("identf", [128, 128], F32)
        identb = sb("identb", [128, 128], BF16)
        onesD = sb("onesD", [128, 128], BF16)
        ones1 = sb("ones1", [128, 128], BF16)
        condT = sb("condT", [128, KC, NSEQ], BF16)
        cTs = sb("cTs", [128, KC, NSEQ], F32)
        modv = {n: sb("modv_" + n, [128, w, NSEQ], F32) for n, w in
                [("00", 48), ("01", 48), ("10", 48), ("11", 48), ("kv", 32)]}
        Avec = {n: sb("A_" + n, [128, KC, NSEQ], F32) for n in ["00", "01", "10", "11", "kv"]}
        gbo = sb("gbo", [128, KC, NSEQ], F32)
        rstd = sb("rstd", [128, TB], F32)
        epsc = sb("epsc", [128, 1], F32)
        sqt = [sb("sqt%d" % i, [128, TB], BF16) for i in range(2)]
        tmpf = [sb("tmpf%d" % i, [128, TB], F32) for i in range(3)]
        KcT = [sb("KcT%d" % s_, [128, 4, 128], BF16) for s_ in range(NSEQ)]
        Vc = [sb("Vc%d" % s_, [128, 4, 128], BF16) for s_ in range(NSEQ)]
        rr = {"sq": 0, "tf": 0}

        def nsq():
            rr["sq"] += 1
            return sqt[rr["sq"] % 2]

        def ntf():
            rr["tf"] += 1
            return tmpf[rr["tf"] % 3]

        def vcol(name, c=None):
            o, w = VOFF[name]
            if c is None:
                return vecs.ap[:, o:o + w]
            return vecs.ap[:, o + c:o + c + 1]

        NPANEL = 102
        wcache_d = nc.dram_tensor("wcache", [NPANEL, 128, 16 * 512], BF16, kind="Internal").ap()
        wcache = {}

        def panel_src(w2d, r0, c0, cacheable=True):
            key = (str(w2d), r0, c0)
            return (key, w2d[r0:r0 + 2048, c0:c0 + 512].rearrange("(kc p) n -> p kc n", p=128), cacheable)

        pref = []

        def load_panel(p):
            key, ap, cacheable = p
            s_ = slots[slot_i[0] % 3]
            slot_i[0] += 1
            if cacheable and key in wcache:
                idx, ct = wcache[key]
                sc.dma(POOL, s_.ap[:], wcache_d[idx].rearrange("p (k n) -> p k n", k=16), reads=[ct], writes=[s_])
                return s_
            sc.dma(POOL, s_.ap[:], ap, writes=[s_])
            if cacheable:
                idx = len(wcache)
                assert idx < NPANEL
                ct = T(None)
                wcache[key] = (idx, ct)
                sc.dma(SP, wcache_d[idx].rearrange("p (k n) -> p k n", k=16), s_.ap[:], reads=[s_], writes=[ct])
            return s_

        def run_steps(steps, nxt=()):
            n = len(steps)
            assigned = {}
            pidx = [i for i in range(n) if steps[i][0] is not None]
            pos = {i: j for j, i in enumerate(pidx)}

            def issue(j):
                if j < len(pidx):
                    i = pidx[j]
                    if pref:
                        key, s_ = pref.pop(0)
                        assert key == steps[i][0][0], (key, steps[i][0][0])
                        assigned[i] = s_
                    else:
                        assigned[i] = load_panel(steps[i][0])
            issue(0)
            issue(1)
            for i in range(n):
                if i in pos:
                    issue(pos[i] + 2)
                steps[i][1](assigned.get(i))
            assert not pref
            for p in nxt:
                pref.append((p[0], load_panel(p)))

        P_gmlp = [panel_src(win_d, 0, GW), panel_src(win_d, 0, GW + 512)]
        P_mlp = [[panel_src(w1_d[l], 0, 0), panel_src(w1_d[l], 0, 512)] for l in range(2)]
        P_kv = [panel_src(wkv_d, 0, 0), panel_src(wkv_d, 0, 512)]
        P_q = [panel_src(wqg_d, 0, 0), panel_src(wqg_d, 0, 512)]
        P_wo = [panel_src(wo_d, 0, 0), panel_src(wo_d, 0, 512)]

        sc.dma(SP, vecs.ap[:], vecs_d, writes=[vecs])
        sc.dma(SP, identf.ap[:], ident_d, writes=[identf])
        sc.dma(SP, cTs.ap[:], cT_d, writes=[cTs])
        sc.op(DVE, lambda e: e.tensor_copy(out=identb.ap[:], in_=identf.ap[:]), reads=[identf], writes=[identb])
        sc.op(DVE, lambda e: e.memset(onesD.ap[:], 1.0 / D), writes=[onesD])
        sc.op(DVE, lambda e: e.memset(ones1.ap[:], 1.0), writes=[ones1])
        sc.op(DVE, lambda e: e.memset(epsc.ap[:], EPS), writes=[epsc])
        sc.op(ACT, lambda e: e.activation(out=condT.ap[:], in_=cTs.ap[:], func=AF.Silu), reads=[cTs], writes=[condT])

        def mod_steps(name, w2d, ncols, bname):
            steps = []
            for p in range(ncols // 512):
                def fn(slot, p=p):
                    bk = nb()
                    for fc in range(4):
                        for k in range(KC):
                            sc.op(PE, lambda e, fc=fc, k=k: e.matmul(
                                bk.ap[:, fc * NSEQ:(fc + 1) * NSEQ], lhsT=slot.ap[:, k, fc * 128:(fc + 1) * 128],
                                rhs=condT.ap[:, k, :], start=(k == 0), stop=(k == KC - 1)),
                                reads=[slot, condT], writes=[bk])
                    o, _ = VOFF[bname]
                    sc.op(DVE, lambda e: e.tensor_tensor(
                        out=modv[name].ap[:, p * 4:(p + 1) * 4, :],
                        in0=bk.ap[:, 0:4 * NSEQ].rearrange("p (a b) -> p a b", b=NSEQ),
                        in1=vecs.ap[:, o + p * 4:o + (p + 1) * 4].unsqueeze(2).to_broadcast([128, 4, NSEQ]),
                        op=ALU.add), reads=[bk, vecs], writes=[modv[name]])
                steps.append((panel_src(w2d, 0, p * 512, False), fn))
            return steps

        def derive(name, gname, has_gate_bias=False):
            mv = modv[name]
            A = Avec[name]
            o, _ = VOFF[gname]
            sc.op(DVE, lambda e: e.tensor_scalar(out=A.ap[:], in0=mv.ap[:, 16:32, :], scalar1=1.0, scalar2=None,
                                                 op0=ALU.add), reads=[mv], writes=[A])
            sc.op(DVE, lambda e: e.tensor_tensor(out=A.ap[:], in0=A.ap[:],
                                                 in1=vecs.ap[:, o:o + 16].unsqueeze(2).to_broadcast([128, 16, NSEQ]),
                                                 op=ALU.mult), reads=[A, vecs], writes=[A])

        steps = []
        steps += mod_steps("00", modw_d[0, 0], 3 * D, "mod_b00")
        steps += mod_steps("01", modw_d[0, 1], 3 * D, "mod_b01")
        steps += mod_steps("kv", kvmodw_d, 2 * D, "kv_mod_b")
        steps += mod_steps("10", modw_d[1, 0], 3 * D, "mod_b10")
        steps += mod_steps("11", modw_d[1, 1], 3 * D, "mod_b11")
        run_steps(steps, nxt=P_gmlp)
        derive("00", "g00")
        derive("01", "g01")
        derive("kv", "gkv")
        derive("10", "g10")
        derive("11", "g11")
        ob, _ = VOFF["b_out"]
        sc.op(DVE, lambda e: e.tensor_tensor(out=gbo.ap[:], in0=modv["00"].ap[:, 32:48, :],
                                             in1=vecs.ap[:, ob:ob + 16].unsqueeze(2).to_broadcast([128, 16, NSEQ]),
                                             op=ALU.mult), reads=[modv["00"], vecs], writes=[gbo])

        sqst = {"done": 0, "pend": []}

        def flush_sq():
            bk = banks[6]
            for c in sqst["pend"]:
                assert c == sqst["done"]
                q = nsq()
                sc.op(ACT, lambda e, c=c, q=q: e.activation(out=q.ap[:], in_=xT[c].ap[:], func=AF.Square),
                      reads=[xT[c]], writes=[q])
                sc.op(PE, lambda e, c=c, q=q: e.matmul(bk.ap[:], lhsT=onesD.ap[:], rhs=q.ap[:],
                                                       start=(c == 0), stop=(c == KC - 1)),
                      reads=[q, onesD], writes=[bk])
                sqst["done"] += 1
            sqst["pend"] = []

        def x_final(c):
            sqst["pend"].append(c)

        def norm_to_hT(name, b):
            bk = banks[6]
            flush_sq()
            for c in range(sqst["done"], KC):
                sqst["pend"].append(c)
            flush_sq()
            assert sqst["done"] == KC
            sqst["done"] = 0
            sc.op(ACT, lambda e: e.activation(out=rstd.ap[:], in_=bk.ap[:], func=AF.Sqrt, bias=epsc.ap[:, 0:1]),
                  reads=[bk, epsc], writes=[rstd])
            sc.op(DVE, lambda e: e.reciprocal(out=rstd.ap[:], in_=rstd.ap[:]), reads=[rstd], writes=[rstd])
            if name is None:
                return
            A = Avec[name]
            mv = modv[name]
            for c in range(KC):
                t_ = ntf()
                sc.op(DVE, lambda e, c=c, t_=t_: e.tensor_tensor(out=t_.ap[:], in0=xT[c].ap[:], in1=rstd.ap[:],
                                                                 op=ALU.mult), reads=[xT[c], rstd], writes=[t_])
                sc.op(ACT, lambda e, c=c, t_=t_: e.activation(out=hT.ap[:, c, :], in_=t_.ap[:], func=AF.Identity,
                                                              bias=mv.ap[:, c, b:b + 1], scale=A.ap[:, c, b:b + 1]),
                      reads=[t_, A, mv], writes=[hTc[c]])

        def resid_add(bk, c, gate_ap, bias_ap=None):
            t_ = ntf()
            if bias_ap is None:
                sc.op(ACT, lambda e: e.activation(out=t_.ap[:], in_=bk.ap[:], func=AF.Copy, scale=gate_ap),
                      reads=[bk], writes=[t_])
            else:
                sc.op(ACT, lambda e: e.activation(out=t_.ap[:], in_=bk.ap[:], func=AF.Identity, scale=gate_ap,
                                                  bias=bias_ap), reads=[bk], writes=[t_])
            sc.op(DVE, lambda e: e.tensor_tensor(out=xT[c].ap[:], in0=xT[c].ap[:], in1=t_.ap[:], op=ALU.add),
                  reads=[xT[c], t_], writes=[xT[c]])
            x_final(c)

        def mlp_phase(l, b, nxt=()):
            name = "%d1" % l
            norm_to_hT(name, b)
            with ExitStack() as ps:
                hid = [sb("hid%d" % i, [128, TB], BF16, ps) for i in range(64)]
                steps = []
                for fp in range(16):
                    def fn(slot, fp=fp):
                        for j in range(4):
                            fc = fp * 4 + j
                            bk = nb()
                            for k in range(KC):
                                sc.op(PE, lambda e, k=k, j=j: e.matmul(
                                    bk.ap[:], lhsT=slot.ap[:, k, j * 128:(j + 1) * 128], rhs=hT.ap[:, k, :],
                                    start=(k == 0), stop=(k == KC - 1)), reads=[slot, hTc[k]], writes=[bk])
                            t_ = ntf()
                            sc.op(ACT, lambda e: e.activation(out=t_.ap[:], in_=bk.ap[:], func=AF.Relu),
                                  reads=[bk], writes=[t_])
                            sc.op(DVE, lambda e, fc=fc: e.tensor_tensor(out=hid[fc].ap[:], in0=t_.ap[:], in1=t_.ap[:],
                                                                        op=ALU.mult), reads=[t_], writes=[hid[fc]])
                    steps.append((panel_src(w1_d[l], 0, fp * 512), fn))
                acc = {}
                for og in range(4):
                    for fp in range(4):
                        def fn(slot, og=og, fp=fp):
                            if fp == 0:
                                acc[og] = [nb() for _ in range(4)]
                            for oc in range(4):
                                bk = acc[og][oc]
                                for k in range(16):
                                    sc.op(PE, lambda e, k=k, oc=oc: e.matmul(
                                        bk.ap[:], lhsT=slot.ap[:, k, oc * 128:(oc + 1) * 128],
                                        rhs=hid[fp * 16 + k].ap[:], start=(fp == 0 and k == 0),
                                        stop=(fp == 3 and k == 15)), reads=[slot, hid[fp * 16 + k]], writes=[bk])
                            if fp == 1:
                                flush_sq()
                            if fp == 3:
                                for oc in range(4):
                                    c = og * 4 + oc
                                    resid_add(acc[og][oc], c, modv[name].ap[:, 32 + c, b:b + 1])
                        steps.append((panel_src(w2_d[l], fp * 2048, og * 512), fn))
                run_steps(steps, nxt=nxt)

        C.__dict__.update(locals())
        passA(C)
        if stop_after != "A":
            passB(C)
        sc.barrier()
        sc.finish()
    return nc


def rope_from_bank(C, bk, out_t, out_ap, extra_reads=()):
    sc, ACT, DVE = C.sc, C.sc.act, C.sc.dve
    xs = C.ntf()
    sw = C.ntf()
    sc.op(ACT, lambda e: e.activation(out=xs.ap[:], in_=bk.ap[:], func=AF.Copy), reads=[bk], writes=[xs])
    sc.op(DVE, lambda e: e.tensor_copy(out=sw.ap[0:64, :], in_=xs.ap[64:128, :]), reads=[xs], writes=[sw])
    sc.op(DVE, lambda e: e.tensor_copy(out=sw.ap[64:128, :], in_=xs.ap[0:64, :]), reads=[xs], writes=[sw])
    sc.op(DVE, lambda e: e.tensor_tensor(out=xs.ap[:], in0=xs.ap[:], in1=C.cosb.ap[:], op=ALU.mult),
          reads=[xs, C.cosb, sw], writes=[xs])
    sc.op(DVE, lambda e: e.tensor_tensor(out=sw.ap[:], in0=sw.ap[:], in1=C.sinb.ap[:], op=ALU.mult),
          reads=[sw, C.sinb], writes=[sw])
    sc.op(DVE, lambda e: e.tensor_tensor(out=out_ap, in0=xs.ap[:], in1=sw.ap[:], op=ALU.add),
          reads=[xs, sw] + list(extra_reads), writes=[out_t])


def passA(C):
    nc, sc = C.nc, C.sc
    PE, ACT, DVE, POOL, SP = sc.pe, sc.act, sc.dve, sc.pool, sc.sp
    nb, ntf, nsq, xT, hT, vecs, VO = C.nb, C.ntf, C.nsq, C.xT, C.hT, C.vecs, VOFF
    sb, run_steps, panel_src = C.sb, C.run_steps, C.panel_src
    modv, Avec = C.modv, C.Avec
    C_hTc = C.hTc
    with ExitStack() as pa:
        WcT = sb("WcT", [128, 16, 128], BF16, pa)
        BiasT = sb("BiasT", [128, 32, 128], F32, pa)
        binv = sb("binv", [128, 1536], BF16, pa)
        s1 = sb("s1", [128, 32], F32, pa)
        s2 = sb("s2", [128, 32], F32, pa)
        st4 = sb("st4", [128, 4, 4], F32, pa)
        junk = sb("junk", [128, TB], BF16, pa)
        ko_i = [0]
        with ExitStack() as ps:
            rs_rep = sb("rs_rep", [128, 16, 128], F32, ps)
            bs_rep = sb("bs_rep", [128, 16, 128], F32, ps)
            sc.dma(POOL, WcT.ap[:], C.wsT_d, writes=[WcT])
            for vp in range(8):
                sc.dma(POOL, binv.ap[32 * (vp % 3):32 * (vp % 3) + 1, (vp // 3) * 512:(vp // 3 + 1) * 512],
                       C.binv_d[0:1, vp * 512:(vp + 1) * 512], writes=[binv])
            sc.dma(SP, bs_rep.ap[:].rearrange("p g t -> p (g t)"), C.bs_d.partition_broadcast(128), writes=[bs_rep])
            sc.op(POOL, lambda e: e.affine_select(out=WcT.ap[:], in_=WcT.ap[:], pattern=[[0, 16], [1, 128]],
                                                  compare_op=ALU.is_ge, fill=0.0, base=0, channel_multiplier=-1),
                  reads=[WcT], writes=[WcT])
            for q in range(4):
                bk = nb()
                sc.op(PE, lambda e, q=q: e.matmul(bk.ap[:], lhsT=C.ones1.ap[:],
                                                  rhs=WcT.ap[:, q * 4:(q + 1) * 4, :], start=True, stop=True),
                      reads=[WcT, C.ones1], writes=[bk])
                sc.op(ACT, lambda e, q=q: e.activation(out=rs_rep.ap[:, q * 4:(q + 1) * 4, :].rearrange("p a b -> p (a b)"),
                                                       in_=bk.ap[:], func=AF.Copy), reads=[bk], writes=[rs_rep])
            olb, _ = VO["ln_b"]
            for fc in range(32):
                g = fc // 2
                sc.op(DVE, lambda e, fc=fc, g=g: e.scalar_tensor_tensor(
                    out=BiasT.ap[:, fc, :], in0=rs_rep.ap[:, g, :], scalar=vecs.ap[:, olb + fc:olb + fc + 1],
                    in1=bs_rep.ap[:, g, :], op0=ALU.mult, op1=ALU.add), reads=[rs_rep, bs_rep, vecs], writes=[BiasT])

        olg, _ = VO["ln_g"]
        obu, _ = VO["b_in_u"]
        for seq in range(NSEQ):
            for blk in range(NBLK):
                t0 = blk * TB
                b = seq
                with ExitStack() as ps:
                    xtm = [sb("xtm%d" % i, [128, D], F32, ps) for i in range(4)]
                    for tt in range(4):
                        sc.dma(SP, xtm[tt].ap[:], C.x_d[seq, t0 + tt * 128:t0 + (tt + 1) * 128, :], writes=[xtm[tt]])
                    for c in range(KC):
                        bk = nb()
                        for tt in range(4):
                            sc.op(PE, lambda e, tt=tt, c=c: e.transpose(bk.ap[:, tt * 128:(tt + 1) * 128],
                                                                        xtm[tt].ap[:, c * 128:(c + 1) * 128],
                                                                        C.identf.ap[:]),
                                  reads=[xtm[tt], C.identf], writes=[bk])
                        if c >= 2:
                            C.flush_sq()
                        if c >= 1:
                            C.x_final(c - 1)
                        if c % 2:
                            sc.op(ACT, lambda e, c=c: e.activation(out=xT[c].ap[:], in_=bk.ap[:], func=AF.Copy),
                                  reads=[bk], writes=[xT[c]])
                        else:
                            sc.op(DVE, lambda e, c=c: e.tensor_copy(out=xT[c].ap[:], in_=bk.ap[:]),
                                  reads=[bk], writes=[xT[c]])
                    C.x_final(KC - 1)

                C.norm_to_hT("00", b)
                with ExitStack() as ps:
                    sT = [sb("sT%d" % i, [128, TB], BF16, ps) for i in range(32)]
                    v = [sb("v%d" % i, [128, GW], BF16, ps) for i in range(4)]
                    sc.op(DVE, lambda e: e.memset(s1.ap[:], 0.0), writes=[s1])
                    sc.op(DVE, lambda e: e.memset(s2.ap[:], 0.0), writes=[s2])
                    steps = []
                    for vp in range(8):
                        def fn(slot, vp=vp):
                            for tt in range(4):
                                bk = nb()
                                for k in range(KC):
                                    sc.op(PE, lambda e, k=k, tt=tt: e.matmul(
                                        bk.ap[:], lhsT=hT.ap[:, k, tt * 128:(tt + 1) * 128], rhs=slot.ap[:, k, :],
                                        start=(k == 0), stop=False), reads=[slot, C_hTc[k]], writes=[bk])
                                pb = 32 * (vp % 3)
                                sc.op(PE, lambda e: e.matmul(bk.ap[:], lhsT=C.ones1.ap[pb:pb + 1, :],
                                                             rhs=binv.ap[pb:pb + 1, (vp // 3) * 512:(vp // 3 + 1) * 512],
                                                             start=False, stop=True),
                                      reads=[C.ones1, binv], writes=[bk])
                                col = tt * 8 + vp
                                sc.op(ACT, lambda e, tt=tt, col=col: e.activation(
                                    out=v[tt].ap[:, vp * 512:(vp + 1) * 512], in_=bk.ap[:], func=AF.Gelu,
                                    accum_out=s1.ap[:, col:col + 1]), reads=[bk], writes=[v[tt], s1])
                                sc.op(ACT, lambda e, tt=tt, col=col: e.activation(
                                    out=junk.ap[:], in_=v[tt].ap[:, vp * 512:(vp + 1) * 512], func=AF.Square,
                                    accum_out=s2.ap[:, col:col + 1]), reads=[v[tt]], writes=[junk, s2])
                        steps.append((panel_src(C.win_d, 0, GW + vp * 512), fn))

                    def ln_and_spatial(slot):
                        sc.op(DVE, lambda e: e.tensor_reduce(out=st4.ap[:, 0, :], in_=s1.ap[:].rearrange("p (a b) -> p a b", b=8),
                                                             axis=AX.X, op=ALU.add), reads=[s1], writes=[st4])
                        sc.op(DVE, lambda e: e.tensor_reduce(out=st4.ap[:, 3, :], in_=s2.ap[:].rearrange("p (a b) -> p a b", b=8),
                                                             axis=AX.X, op=ALU.add), reads=[s2, st4], writes=[st4])
                        sc.op(DVE, lambda e: e.tensor_scalar(out=st4.ap[:, 0, :], in0=st4.ap[:, 0, :], scalar1=1.0 / GW,
                                                             scalar2=None, op0=ALU.mult), reads=[st4], writes=[st4])
                        sc.op(DVE, lambda e: e.tensor_tensor(out=st4.ap[:, 1, :], in0=st4.ap[:, 0, :], in1=st4.ap[:, 0, :],
                                                             op=ALU.mult), reads=[st4], writes=[st4])
                        sc.op(DVE, lambda e: e.scalar_tensor_tensor(out=st4.ap[:, 1, :], in0=st4.ap[:, 3, :], scalar=1.0 / GW,
                                                                    in1=st4.ap[:, 1, :], op0=ALU.mult, op1=ALU.subtract),
                              reads=[st4], writes=[st4])
                        sc.op(ACT, lambda e: e.activation(out=st4.ap[:, 2, :], in_=st4.ap[:, 1, :], func=AF.Sqrt,
                                                          bias=C.epsc.ap[:, 0:1]), reads=[st4, C.epsc], writes=[st4])
                        sc.op(DVE, lambda e: e.reciprocal(out=st4.ap[:, 2, :], in_=st4.ap[:, 2, :]), reads=[st4], writes=[st4])
                        for tt in range(4):
                            sc.op(DVE, lambda e, tt=tt: e.tensor_scalar(
                                out=v[tt].ap[:], in0=v[tt].ap[:], scalar1=st4.ap[:, 0, tt:tt + 1],
                                scalar2=st4.ap[:, 2, tt:tt + 1], op0=ALU.subtract, op1=ALU.mult),
                                reads=[v[tt], st4], writes=[v[tt]])
                        for tt in range(4):
                            for f4 in range(8):
                                bk = nb()
                                for j in range(4):
                                    fc = f4 * 4 + j
                                    sc.op(PE, lambda e, j=j, fc=fc, tt=tt: e.matmul(
                                        bk.ap[:, j * 128:(j + 1) * 128], lhsT=v[tt].ap[:, fc * 128:(fc + 1) * 128],
                                        rhs=WcT.ap[:, fc // 2, :], start=True, stop=True),
                                        reads=[v[tt], WcT], writes=[bk])
                                for j in range(4):
                                    fc = f4 * 4 + j
                                    sc.op(DVE, lambda e, j=j, fc=fc, tt=tt: e.scalar_tensor_tensor(
                                        out=sT[fc].ap[:, tt * 128:(tt + 1) * 128], in0=bk.ap[:, j * 128:(j + 1) * 128],
                                        scalar=vecs.ap[:, olg + fc:olg + fc + 1], in1=BiasT.ap[:, fc, :],
                                        op0=ALU.mult, op1=ALU.add), reads=[bk, vecs, BiasT], writes=[sT[fc]])
                    steps.append((None, ln_and_spatial))
                    for up in range(8):
                        def fn(slot, up=up):
                            for j in range(4):
                                fc = up * 4 + j
                                bk = nb()
                                for k in range(KC):
                                    sc.op(PE, lambda e, k=k, j=j: e.matmul(
                                        bk.ap[:], lhsT=slot.ap[:, k, j * 128:(j + 1) * 128], rhs=hT.ap[:, k, :],
                                        start=(k == 0), stop=(k == KC - 1)), reads=[slot, C_hTc[k]], writes=[bk])
                                t_ = ntf()
                                sc.op(ACT, lambda e, fc=fc: e.activation(out=t_.ap[:], in_=bk.ap[:], func=AF.Gelu,
                                                                         bias=vecs.ap[:, obu + fc:obu + fc + 1]),
                                      reads=[bk, vecs], writes=[t_])
                                sc.op(DVE, lambda e, fc=fc: e.tensor_tensor(out=sT[fc].ap[:], in0=t_.ap[:], in1=sT[fc].ap[:],
                                                                            op=ALU.mult), reads=[t_, sT[fc]], writes=[sT[fc]])
                        steps.append((panel_src(C.win_d, 0, up * 512), fn))
                    acc = {}
                    for og in range(4):
                        for fp in range(2):
                            def fn(slot, og=og, fp=fp):
                                if fp == 0:
                                    acc[og] = [nb() for _ in range(4)]
                                for oc in range(4):
                                    bk = acc[og][oc]
                                    for k in range(16):
                                        sc.op(PE, lambda e, k=k, oc=oc: e.matmul(
                                            bk.ap[:], lhsT=slot.ap[:, k, oc * 128:(oc + 1) * 128],
                                            rhs=sT[fp * 16 + k].ap[:], start=(fp == 0 and k == 0),
                                            stop=(fp == 1 and k == 15)), reads=[slot, sT[fp * 16 + k]], writes=[bk])
                                if fp == 1:
                                    C.flush_sq()
                                if fp == 1:
                                    for oc in range(4):
                                        c = og * 4 + oc
                                        C.resid_add(acc[og][oc], c, modv["00"].ap[:, 32 + c, b:b + 1],
                                                    C.gbo.ap[:, c, b:b + 1])
                            steps.append((panel_src(C.wout_d, fp * 2048, og * 512), fn))
                    run_steps(steps, nxt=C.P_mlp[0])
                if C.debug and seq == 0 and blk == 0:
                    for c in range(KC):
                        sc.dma(SP, C.dbg_d[0, :, c, :], xT[c].ap[:], reads=[xT[c]])
                if C.stop_after == "gmlp":
                    return

                C.mlp_phase(0, b, nxt=C.P_kv)

                for c in range(KC):
                    sc.dma(SP, C.x2T_d[seq * NBLK + blk, :, c, :], xT[c].ap[:], reads=[xT[c]])
                C.norm_to_hT("kv", b)
                kvs = ExitStack()
                kout = [sb("kout%d" % i, [128, TB], BF16, kvs) for i in range(4)]
                C.cosb = sb("cosb", [128, TB], F32, kvs)
                C.sinb = sb("sinb", [128, TB], F32, kvs)
                sc.dma(SP, C.cosb.ap[:], C.cos_d[:, t0:t0 + TB], writes=[C.cosb])
                sc.dma(SP, C.sinb.ap[:], C.sin_d[:, t0:t0 + TB], writes=[C.sinb])
                steps = []
                for br in range(6):
                    def fn(slot, br=br):
                        if br in (0, 1, 2, 4):
                            ki = {0: 0, 1: 1, 2: 2, 4: 3}[br]
                            for g in range(4):
                                bk = nb()
                                for k in range(KC):
                                    sc.op(PE, lambda e, k=k, g=g: e.matmul(
                                        bk.ap[:], lhsT=slot.ap[:, k, g * 128:(g + 1) * 128], rhs=hT.ap[:, k, :],
                                        start=(k == 0), stop=(k == KC - 1)), reads=[slot, C_hTc[k]], writes=[bk])
                                ko = kout[ko_i[0] % 4]
                                ko_i[0] += 1
                                if br == 1:
                                    sc.op(ACT, lambda e: e.activation(out=ko.ap[:], in_=bk.ap[:], func=AF.Copy),
                                          reads=[bk], writes=[ko])
                                else:
                                    rope_from_bank(C, bk, ko, ko.ap[:])
                                sc.dma(SP, C.kT_d[ki, seq, :, g, t0:t0 + TB], ko.ap[:], reads=[ko])
                        else:
                            vi = {3: 0, 5: 1}[br]
                            for tt in range(4):
                                bk = nb()
                                for k in range(KC):
                                    sc.op(PE, lambda e, k=k, tt=tt: e.matmul(
                                        bk.ap[:], lhsT=hT.ap[:, k, tt * 128:(tt + 1) * 128], rhs=slot.ap[:, k, :],
                                        start=(k == 0), stop=(k == KC - 1)), reads=[slot, C_hTc[k]], writes=[bk])
                                ko = kout[ko_i[0] % 4]
                                ko_i[0] += 1
                                sc.op(ACT, lambda e: e.activation(out=ko.ap[:], in_=bk.ap[:], func=AF.Copy),
                                      reads=[bk], writes=[ko])
                                sc.dma(SP, C.vtm_d[vi, seq, t0 + tt * 128:t0 + (tt + 1) * 128, :], ko.ap[:], reads=[ko])
                    steps.append((panel_src(C.wkv_d, 0, br * 512), fn))
                run_steps(steps, nxt=(C.P_q if (seq == NSEQ - 1 and blk == NBLK - 1) else C.P_gmlp))
                kvs.close()
                if C.stop_after == "blk0":
                    return
            sc.barrier()
            compress_seq(C, seq)


def compress_seq(C, seq):
    nc, sc = C.nc, C.sc
    PE, ACT, DVE, POOL, SP = sc.pe, sc.act, sc.dve, sc.pool, sc.sp
    nb, sb = C.nb, C.sb
    with ExitStack() as ps:
        w1s = sb("cw1s", [128, 32, 256], BF16, ps)
        w2s = sb("cw2s", [128, 2, 128], BF16, ps)
        posT = sb("cposT", [128, 32], BF16, ps)
        cvec = sb("ccvec", [128, 2], F32, ps)
        src = sb("csrc", [128, S], BF16, ps)
        hidc = sb("chid", [128, 2, 128], BF16, ps)
        for kv in range(2):
            sc.dma(POOL, w1s.ap[:], C.cw1_d[kv].rearrange("(r p) j -> p r j", p=128), writes=[w1s])
            sc.dma(POOL, w2s.ap[:], C.cw2_d[kv].rearrange("(a p) d -> p a d", p=128), writes=[w2s])
            sc.dma(POOL, posT.ap[:], C.cpos_d[kv], writes=[posT])
            bk = nb()
            for jc in range(2):
                for r in range(32):
                    sc.op(PE, lambda e, jc=jc, r=r: e.matmul(bk.ap[:, jc:jc + 1], lhsT=w1s.ap[:, r, jc * 128:(jc + 1) * 128],
                                                             rhs=posT.ap[:, r:r + 1], start=(r == 0), stop=(r == 31)),
                          reads=[w1s, posT], writes=[bk])
            sc.op(DVE, lambda e: e.tensor_copy(out=cvec.ap[:], in_=bk.ap[:, 0:2]), reads=[bk], writes=[cvec])
            for g in range(4):
                sc.dma(SP, src.ap[:], C.kT_d[kv, seq, :, g, :], writes=[src])
                bk = nb()
                for jc in range(2):
                    for r in range(32):
                        sc.op(PE, lambda e, jc=jc, r=r: e.matmul(
                            bk.ap[:, jc * 128:jc * 128 + 127], lhsT=w1s.ap[:, r, jc * 128:(jc + 1) * 128],
                            rhs=src.ap[:, r:r + 16 * 126 + 1:16], start=(r == 0), stop=(r == 31)),
                            reads=[w1s, src], writes=[bk])
                for jc in range(2):
                    sc.op(ACT, lambda e, jc=jc: e.activation(out=hidc.ap[:, jc, 0:127], in_=bk.ap[:, jc * 128:jc * 128 + 127],
                                                             func=AF.Gelu, bias=cvec.ap[:, jc:jc + 1]),
                          reads=[bk, cvec], writes=[hidc])
                bk2 = nb()
                if kv == 0:
                    for jc in range(2):
                        sc.op(PE, lambda e, jc=jc: e.matmul(bk2.ap[:, 0:127], lhsT=w2s.ap[:, jc, :], rhs=hidc.ap[:, jc, 0:127],
                                                            start=(jc == 0), stop=(jc == 1)), reads=[w2s, hidc], writes=[bk2])
                    sc.op(DVE, lambda e, g=g: e.tensor_copy(out=C.KcT[seq].ap[:, g, 0:127], in_=bk2.ap[:, 0:127]),
                          reads=[bk2], writes=[C.KcT[seq]])
                else:
                    for jc in range(2):
                        sc.op(PE, lambda e, jc=jc: e.matmul(bk2.ap[0:127, 0:128], lhsT=hidc.ap[:, jc, 0:127], rhs=w2s.ap[:, jc, :],
                                                            start=(jc == 0), stop=(jc == 1)), reads=[w2s, hidc], writes=[bk2])
                    sc.op(DVE, lambda e, g=g: e.tensor_copy(out=C.Vc[seq].ap[0:127, g, :], in_=bk2.ap[0:127, 0:128]),
                          reads=[bk2], writes=[C.Vc[seq]])
        sc.barrier()
    if C.debug:
        sc.dma(SP, C.dbgc_d[seq, 0], C.KcT[seq].ap[:], reads=[C.KcT[seq]])
        sc.dma(SP, C.dbgc_d[seq, 1], C.Vc[seq].ap[:], reads=[C.Vc[seq]])


def passB(C):
    nc, sc = C.nc, C.sc
    PE, ACT, DVE, POOL, SP = sc.pe, sc.act, sc.dve, sc.pool, sc.sp
    nb, ntf, nsq, xT, hT, vecs, VO = C.nb, C.ntf, C.nsq, C.xT, C.hT, C.vecs, VOFF
    sb, run_steps, panel_src = C.sb, C.run_steps, C.panel_src
    modv, banks, pbf, identb, ones1 = C.modv, C.banks, C.pbf, C.identb, C.ones1
    SCL = 1.0 / math.sqrt(128.0)
    C_hTc = C.hTc
    with ExitStack() as pb:
        m_cmp = sb("m_cmp", [128, 16, 128], BF16, pb)
        m_tri = sb("m_tri", [128, 2, 128], BF16, pb)
        addtab = sb("addtab", [128, 16, 32], F32, pb)
        emat = sb("emat", [32, S], BF16, pb)
        ovl = sb("ovl", [128, 33], BF16, pb)
        wg = sb("wg", [128, KC, 48], BF16, pb)
        gates = sb("gates", [128, 4, 48], F32, pb)
        Pt = [sb("Pt%d" % i, [128, 512], BF16, pb) for i in range(3)]
        oacc = sb("oacc", [128, 4, 128], F32, pb)
        otmp = sb("otmp", [128, 4, 128], F32, pb)
        sm = sb("sm", [128, 64], F32, pb)
        tmpi = sb("tmpi", [128, 4, 32], F32, pb)
        selb = sb("selb", [128, 32], F32, pb)
        selbb = sb("selbb", [128, 32], BF16, pb)
        selT = sb("selT", [32, 128], BF16, pb)
        sc.dma(POOL, m_cmp.ap[:], C.mcmp_d, writes=[m_cmp])
        sc.dma(POOL, m_tri.ap[:], C.mtri_d, writes=[m_tri])
        sc.dma(SP, addtab.ap[:], C.addtab_d, writes=[addtab])
        sc.dma(POOL, emat.ap[:], C.emat_d, writes=[emat])
        sc.dma(POOL, ovl.ap[:], C.ovl_d, writes=[ovl])
        sc.dma(POOL, wg.ap[:], C.wqg_d[:, 2048:2096].rearrange("(kc p) n -> p kc n", p=128), writes=[wg])
        pt_i = [0]
        sb_i = [0]
        o_i = [0]
        otm = hT.ap[:].rearrange("p c t -> p (c t)").rearrange("p (a f) -> p a f", a=4)

        def npt():
            pt_i[0] += 1
            return Pt[pt_i[0] % 3]

        def nbs():
            sb_i[0] += 1
            return banks[sb_i[0] % 3]

        def nbo():
            o_i[0] += 1
            return banks[4 + o_i[0] % 2]
        bkD = banks[6]

        def bc4(ap2d, k):
            return ap2d.unsqueeze(1).to_broadcast([k, 4, 128])

        def v3(ap2d):
            return ap2d.rearrange("p (a b) -> p a b", a=4)

        for seq in range(NSEQ):
            for blk in range(NBLK):
                t0 = blk * TB
                b = seq
                for c in range(KC):
                    sc.dma(SP, xT[c].ap[:], C.x2T_d[seq * NBLK + blk, :, c, :], writes=[xT[c]])
                C.norm_to_hT("10", b)
                qs = ExitStack()
                QT = sb("QT", [128, 16, TB], BF16, qs)
                nt = blk * 4 + 4
                wlo = max(0, blk * 4 - 4)
                Ks = sb("Ks", [128, 4, S], BF16, qs)
                Vs = sb("Vs", [128, 16, 512], BF16, qs)
                Kw = sb("Kw", [128, 4, 1024], BF16, qs)
                Vw = sb("Vw", [128, 8, 512], BF16, qs)
                te = nt * 128
                Ksg = [T(Ks.ap[:, g, :]) for g in range(4)]
                Kwg = [T(Kw.ap[:, g, :]) for g in range(4)]
                for t_ in Ksg:
                    t_.r = dict(Ks.r)
                    qs.callback(C.retire, t_)
                for t_ in Kwg:
                    t_.r = dict(Kw.r)
                    qs.callback(C.retire, t_)
                for g in range(4):
                    sc.dma(SP, Ks.ap[:, g, 0:te], C.kT_d[2, seq, :, g, 0:te], writes=[Ksg[g]])
                    sc.dma(SP, Kw.ap[:, g, 0:te - wlo * 128], C.kT_d[3, seq, :, g, wlo * 128:te], writes=[Kwg[g]])
                sc.dma(SP, Vs.ap[:, 0:nt, :], C.vtm_d[0, seq, 0:te, :].rearrange("(n p) c -> p n c", p=128), writes=[Vs])
                sc.dma(SP, Vw.ap[:, 0:nt - wlo, :], C.vtm_d[1, seq, wlo * 128:te, :].rearrange("(n p) c -> p n c", p=128),
                       writes=[Vw])

                with ExitStack() as ps:
                    C.cosb = sb("cosb", [128, TB], F32, ps)
                    C.sinb = sb("sinb", [128, TB], F32, ps)
                    sc.dma(SP, C.cosb.ap[:], C.cos_d[:, t0:t0 + TB], writes=[C.cosb])
                    sc.dma(SP, C.sinb.ap[:], C.sin_d[:, t0:t0 + TB], writes=[C.sinb])
                    steps = []
                    for g in range(4):
                        def fn(slot, g=g):
                            for j in range(4):
                                h = g * 4 + j
                                bk = nb()
                                for k in range(KC):
                                    sc.op(PE, lambda e, k=k, j=j: e.matmul(
                                        bk.ap[:], lhsT=slot.ap[:, k, j * 128:(j + 1) * 128], rhs=hT.ap[:, k, :],
                                        start=(k == 0), stop=(k == KC - 1)), reads=[slot, C_hTc[k]], writes=[bk])
                                rope_from_bank(C, bk, QT, QT.ap[:, h, :])
                        steps.append((panel_src(C.wqg_d, 0, g * 512), fn))

                    def gate_fn(slot):
                        for tt in range(4):
                            bk = nb()
                            for k in range(KC):
                                sc.op(PE, lambda e, k=k, tt=tt: e.matmul(
                                    bk.ap[:, 0:48], lhsT=hT.ap[:, k, tt * 128:(tt + 1) * 128], rhs=wg.ap[:, k, :],
                                    start=(k == 0), stop=(k == KC - 1)), reads=[wg, C_hTc[k]], writes=[bk])
                            sc.op(ACT, lambda e, tt=tt: e.activation(out=gates.ap[:, tt, :], in_=bk.ap[:, 0:48],
                                                                     func=AF.Sigmoid), reads=[bk], writes=[gates])
                    steps.append((None, gate_fn))
                    run_steps(steps, nxt=C.P_wo)

                with ExitStack() as ps:
                    def finish_branch(bkO, bkDv, den_ap, br, tt, g, first, last):
                        sc.op(DVE, lambda e: e.tensor_scalar(out=sm.ap[:, 0:4], in0=den_ap, scalar1=1e-30, scalar2=None,
                                                             op0=ALU.max), reads=[bkDv], writes=[sm])
                        sc.op(DVE, lambda e: e.reciprocal(out=sm.ap[:, 0:4], in_=sm.ap[:, 0:4]), reads=[sm], writes=[sm])
                        gsl = gates.ap[:, tt, 12 * g:12 * g + 12].rearrange("p (h r) -> p h r", r=3)[:, :, br]
                        sc.op(DVE, lambda e: e.tensor_tensor(out=sm.ap[:, 4:8], in0=sm.ap[:, 0:4], in1=gsl, op=ALU.mult),
                              reads=[sm, gates], writes=[sm])
                        fb = sm.ap[:, 4:8].unsqueeze(2).to_broadcast([128, 4, 128])
                        if first:
                            sc.op(DVE, lambda e: e.tensor_tensor(out=oacc.ap[:], in0=v3(bkO.ap[:]), in1=fb, op=ALU.mult),
                                  reads=[bkO, sm], writes=[oacc])
                        else:
                            sc.op(DVE, lambda e: e.tensor_tensor(out=otmp.ap[:], in0=v3(bkO.ap[:]), in1=fb, op=ALU.mult),
                                  reads=[bkO, sm], writes=[otmp])
                            dst = v3(otm[:, tt, g * 512:(g + 1) * 512]) if last else oacc.ap[:]
                            sc.op(DVE, lambda e: e.tensor_tensor(out=dst, in0=oacc.ap[:], in1=otmp.ap[:], op=ALU.add),
                                  reads=[oacc, otmp], writes=(C_hTc[4 * tt:4 * tt + 4] if last else [oacc]))

                    tiles = []
                    brn = [0]

                    def add_branch(kind, tt, g, T_):
                        q0 = tt * 128
                        Qg = QT.ap[:, 4 * g:4 * g + 4, q0:q0 + 128]
                        st_ = {}
                        bi = brn[0]
                        brn[0] += 1
                        if kind == 0:
                            kts = [0]
                        elif kind == 2:
                            kts = list(range(max(0, T_ - 4), T_ + 1))
                        else:
                            kts = list(range(T_ + 1))
                        for idx, kt in enumerate(kts):
                            first = (idx == 0)
                            lastk = (idx == len(kts) - 1)
                            tl = {}

                            def S(kt=kt, tl=tl):
                                bkS = nbs()
                                P = npt()
                                tl["P"] = P
                                if kind == 0:
                                    sc.op(PE, lambda e: e.matmul(v3(bkS.ap[0:127, :]), lhsT=C.KcT[seq].ap[:, g, 0:127], rhs=Qg,
                                                                 start=True, stop=False), reads=[C.KcT[seq], QT], writes=[bkS])
                                    sc.op(PE, lambda e: e.matmul(v3(bkS.ap[0:127, :]), lhsT=identb.ap[0:127, 0:127],
                                                                 rhs=bc4(m_cmp.ap[0:127, T_, :], 127), start=False, stop=True),
                                          reads=[identb, m_cmp], writes=[bkS])
                                    sc.op(ACT, lambda e: e.activation(out=P.ap[0:127, :], in_=bkS.ap[0:127, :], func=AF.Exp,
                                                                      scale=SCL), reads=[bkS], writes=[P])
                                    return
                                diag = (kt == T_)
                                if kind == 1:
                                    sc.op(PE, lambda e: e.matmul(v3(bkS.ap[:]), lhsT=Ks.ap[:, g, kt * 128:(kt + 1) * 128], rhs=Qg,
                                                                 start=True, stop=False), reads=[Ksg[g], QT], writes=[bkS])
                                    sc.op(PE, lambda e: e.matmul(v3(bkS.ap[:]), lhsT=emat.ap[0:32, kt * 128:(kt + 1) * 128],
                                                                 rhs=bc4(selT.ap[0:32, :], 32), start=False, stop=not diag),
                                          reads=[emat, selT], writes=[bkS])
                                    if diag:
                                        sc.op(PE, lambda e: e.matmul(v3(bkS.ap[:]), lhsT=identb.ap[:],
                                                                     rhs=bc4(m_tri.ap[:, 0, :], 128), start=False, stop=True),
                                              reads=[identb, m_tri], writes=[bkS])
                                else:
                                    lw = kt - wlo
                                    lower = (kt == T_ - 4)
                                    sc.op(PE, lambda e: e.matmul(v3(bkS.ap[:]), lhsT=Kw.ap[:, g, lw * 128:(lw + 1) * 128], rhs=Qg,
                                                                 start=True, stop=not (lower or diag)), reads=[Kwg[g], QT], writes=[bkS])
                                    if lower or diag:
                                        mi = 0 if diag else 1
                                        sc.op(PE, lambda e: e.matmul(v3(bkS.ap[:]), lhsT=identb.ap[:],
                                                                     rhs=bc4(m_tri.ap[:, mi, :], 128), start=False, stop=True),
                                              reads=[identb, m_tri], writes=[bkS])
                                sc.op(ACT, lambda e: e.activation(out=P.ap[:], in_=bkS.ap[:], func=AF.Exp, scale=SCL),
                                      reads=[bkS], writes=[P])

                            def PV(kt=kt, tl=tl, first=first, lastk=lastk):
                                if first:
                                    st_["O"] = banks[3 + bi % 2]
                                    st_["D"] = banks[5 + bi % 2]
                                bkO, bkDv, P = st_["O"], st_["D"], tl["P"]
                                for h in range(4):
                                    s0 = (first and h == 0)
                                    if kind == 0:
                                        sc.op(PE, lambda e, h=h: e.matmul(bkO.ap[:, h * 128:(h + 1) * 128],
                                                                          lhsT=P.ap[0:127, h * 128:(h + 1) * 128],
                                                                          rhs=C.Vc[seq].ap[0:127, g, :], start=s0, stop=True),
                                              reads=[P, C.Vc[seq]], writes=[bkO])
                                        sc.op(PE, lambda e, h=h: e.matmul(bkDv.ap[:, h * 33:(h + 1) * 33],
                                                                          lhsT=P.ap[0:127, h * 128:(h + 1) * 128],
                                                                          rhs=ovl.ap[0:127, :], start=s0, stop=True),
                                              reads=[P, ovl], writes=[bkDv])
                                    else:
                                        if kind == 1:
                                            vsrc, vt = Vs.ap[:, kt, g * 128:(g + 1) * 128], Vs
                                        else:
                                            vsrc, vt = Vw.ap[:, kt - wlo, g * 128:(g + 1) * 128], Vw
                                        sc.op(PE, lambda e, h=h: e.matmul(bkO.ap[:, h * 128:(h + 1) * 128],
                                                                          lhsT=P.ap[:, h * 128:(h + 1) * 128], rhs=vsrc,
                                                                          start=s0, stop=lastk), reads=[P, vt], writes=[bkO])
                                        sc.op(PE, lambda e, h=h: e.matmul(bkDv.ap[:, h:h + 1],
                                                                          lhsT=P.ap[:, h * 128:(h + 1) * 128], rhs=ones1.ap[:, 0:1],
                                                                          start=s0, stop=lastk), reads=[P, ones1], writes=[bkDv])
                                if not lastk:
                                    return
                                if kind == 0:
                                    bd3 = bkDv.ap[:, 0:132].rearrange("p (h c) -> p h c", c=33)
                                    finish_branch(bkO, bkDv, bd3[:, :, 32], 0, tt, g, True, False)
                                    sc.op(DVE, lambda e: e.tensor_tensor(out=tmpi.ap[:], in0=bd3[:, :, 0:32],
                                                                         in1=sm.ap[:, 0:4].unsqueeze(2).to_broadcast([128, 4, 32]),
                                                                         op=ALU.mult), reads=[bkDv, sm], writes=[tmpi])
                                    sc.op(DVE, lambda e: e.tensor_reduce(out=sm.ap[:, 16:48],
                                                                         in_=tmpi.ap[:].rearrange("p h j -> p j h"),
                                                                         axis=AX.X, op=ALU.add), reads=[tmpi], writes=[sm])
                                    sc.op(DVE, lambda e: e.tensor_tensor(out=sm.ap[:, 16:48], in0=sm.ap[:, 16:48],
                                                                         in1=addtab.ap[:, T_, :], op=ALU.add),
                                          reads=[sm, addtab], writes=[sm])
                                    sc.op(DVE, lambda e: e.max(out=sm.ap[:, 8:16], in_=sm.ap[:, 16:48]), reads=[sm], writes=[sm])
                                    sc.op(DVE, lambda e: e.tensor_scalar(out=selb.ap[:], in0=sm.ap[:, 16:48],
                                                                         scalar1=sm.ap[:, 15:16], scalar2=None, op0=ALU.is_ge),
                                          reads=[sm], writes=[selb])
                                    sc.op(DVE, lambda e: e.tensor_scalar(out=selbb.ap[:], in0=selb.ap[:], scalar1=-NEGM,
                                                                         scalar2=NEGM, op0=ALU.mult, op1=ALU.add),
                                          reads=[selb], writes=[selbb])
                                elif kind == 2:
                                    finish_branch(bkO, bkDv, bkDv.ap[:, 0:4], 2, tt, g, False, False)
                                else:
                                    finish_branch(bkO, bkDv, bkDv.ap[:, 0:4], 1, tt, g, False, True)

                            def pre(first=first):
                                if kind == 1 and first:
                                    sc.op(PE, lambda e: e.transpose(pbf.ap[0:32, 0:128], selbb.ap[:], identb.ap[:]),
                                          reads=[selbb, identb], writes=[pbf])
                                    sc.op(ACT, lambda e: e.activation(out=selT.ap[:], in_=pbf.ap[0:32, 0:128], func=AF.Copy),
                                          reads=[pbf], writes=[selT])
                            tiles.append((pre, S, PV))

                    for tt in range(4):
                        for g in range(4):
                            T_ = blk * 4 + tt
                            add_branch(0, tt, g, T_)
                            add_branch(2, tt, g, T_)
                            add_branch(1, tt, g, T_)
                    for i_, (pre, S_, PV_) in enumerate(tiles):
                        if i_ == 0:
                            pre()
                            S_()
                        if i_ + 1 < len(tiles):
                            tiles[i_ + 1][0]()
                            tiles[i_ + 1][1]()
                        PV_()
                    for tt in range(4):
                        for h8 in range(2):
                            for j in range(8):
                                hh = h8 * 8 + j
                                sc.op(PE, lambda e, j=j, hh=hh: e.transpose(pbf.ap[:, j * 128:(j + 1) * 128],
                                                                            otm[:, tt, hh * 128:(hh + 1) * 128], identb.ap[:]),
                                      reads=C_hTc[4 * tt:4 * tt + 4] + [identb], writes=[pbf])
                            sc.op(ACT if h8 else DVE,
                                  (lambda e: e.activation(out=QT.ap[:, h8 * 8:(h8 + 1) * 8, tt * 128:(tt + 1) * 128],
                                                          in_=pbf.ap[:].rearrange("p (a b) -> p a b", a=8), func=AF.Copy)) if h8 else
                                  (lambda e: e.tensor_copy(out=QT.ap[:, h8 * 8:(h8 + 1) * 8, tt * 128:(tt + 1) * 128],
                                                           in_=pbf.ap[:].rearrange("p (a b) -> p a b", a=8))),
                                  reads=[pbf], writes=[QT])
                steps = []
                for og in range(4):
                    def fn(slot, og=og):
                        for oc in range(4):
                            c = og * 4 + oc
                            bk = nb()
                            for h in range(16):
                                sc.op(PE, lambda e, h=h, oc=oc: e.matmul(
                                    bk.ap[:], lhsT=slot.ap[:, h, oc * 128:(oc + 1) * 128], rhs=QT.ap[:, h, :],
                                    start=(h == 0), stop=(h == 15)), reads=[slot, QT], writes=[bk])
                            if oc == 0:
                                C.flush_sq()
                            C.resid_add(bk, c, modv["10"].ap[:, 32 + c, b:b + 1])
                    steps.append((panel_src(C.wo_d, 0, og * 512), fn))
                run_steps(steps, nxt=C.P_mlp[1])
                qs.close()
                if C.debug and blk == 0 and seq == 0:
                    for c in range(KC):
                        sc.dma(SP, C.dbg_d[1, :, c, :], xT[c].ap[:], reads=[xT[c]])
                if C.stop_after == "nsa0":
                    return
                C.mlp_phase(1, b, nxt=([] if (seq == NSEQ - 1 and blk == NBLK - 1) else C.P_q))
                C.norm_to_hT(None, b)
                with ExitStack() as ps:
                    ot = sb("ot", [128, 4, D], F32, ps)
                    ogf, _ = VO["gfin"]
                    for c in range(KC):
                        t_ = ntf()
                        sc.op(DVE, lambda e, c=c: e.scalar_tensor_tensor(out=t_.ap[:], in0=xT[c].ap[:],
                                                                         scalar=vecs.ap[:, ogf + c:ogf + c + 1],
                                                                         in1=C.rstd.ap[:], op0=ALU.mult, op1=ALU.mult),
                              reads=[xT[c], vecs, C.rstd], writes=[t_])
                        bk = nb()
                        for tt in range(4):
                            sc.op(PE, lambda e, tt=tt: e.transpose(bk.ap[:, tt * 128:(tt + 1) * 128],
                                                                   t_.ap[:, tt * 128:(tt + 1) * 128], C.identf.ap[:]),
                                  reads=[t_, C.identf], writes=[bk])
                        if c % 2:
                            sc.op(ACT, lambda e, c=c: e.activation(out=ot.ap[:, :, c * 128:(c + 1) * 128],
                                                                   in_=bk.ap[:].rearrange("p (a b) -> p a b", a=4), func=AF.Copy),
                                  reads=[bk], writes=[ot])
                        else:
                            sc.op(DVE, lambda e, c=c: e.tensor_copy(out=ot.ap[:, :, c * 128:(c + 1) * 128],
                                                                    in_=bk.ap[:].rearrange("p (a b) -> p a b", a=4)),
                                  reads=[bk], writes=[ot])
                    sc.dma(SP, C.out_d[seq, t0:t0 + TB, :].rearrange("(a p) f -> p a f", p=128), ot.ap[:], reads=[ot])


def host_tables():
    half = 64
    freqs = (10000.0 ** (-np.arange(half, dtype=np.float32) / half)).astype(np.float32)
    ang = np.arange(S, dtype=np.float32)[None, :] * freqs[:, None]
    cos = np.cos(ang).astype(np.float32)
    sin = np.sin(ang).astype(np.float32)
    cos2 = np.concatenate([cos, cos], 0)
    sinS = np.concatenate([-sin, sin], 0)
    return {"cos2": np.ascontiguousarray(cos2), "sinS": np.ascontiguousarray(sinS),
            "ident": np.eye(128, dtype=np.float32)}


def make_in_maps(inputs, ncores):
    f32 = lambda a: np.ascontiguousarray(np.asarray(a, np.float32))
    I = {k: np.asarray(v) for k, v in inputs.items()}
    vecs = np.zeros((128, NV), np.float32)

    def put(name, v):
        o, w = VOFF[name]
        vecs[:, o:o + w] = fm(v)
    for l in range(2):
        for s_ in range(2):
            put("mod_b%d%d" % (l, s_), I["mod_b"][l, s_])
            put("g%d%d" % (l, s_), I["norm_g"][l, s_])
    put("kv_mod_b", I["kv_mod_b"])
    put("gkv", I["kv_norm_g"])
    put("gfin", I["final_g"])
    put("b_in_u", I["a_b_in"][0, :GW])
    put("ln_g", I["a_ln_g"][0])
    put("ln_b", I["a_ln_b"][0])
    put("b_out", I["a_b_out"][0])
    shared = {
        "vecs": vecs,
        "mod_w": f32(I["mod_w"]), "kv_mod_w": f32(I["kv_mod_w"]),
        "mlp_w1": f32(I["mlp_w1"]), "mlp_w2": f32(I["mlp_w2"]),
        "a_w_in": f32(I["a_w_in"][0]), "a_w_out": f32(I["a_w_out"][0]),
        "b_in_v": f32(I["a_b_in"][0, GW:][None, :]),
        "w_sT": f32(np.transpose(I["a_w_s"][0], (2, 0, 1))),
        "b_s": f32(I["a_b_s"][0].reshape(1, 2048)),
        "w_kv": f32(I["w_kv"]),
        "cmp_posT": f32(np.stack([I["cmp_pos_k"].T, I["cmp_pos_v"].T])),
        "cmp_w1": f32(np.stack([I["cmp_w1_k"], I["cmp_w1_v"]])),
        "cmp_w2": f32(np.stack([I["cmp_w2_k"], I["cmp_w2_v"]])),
        "w_qg": f32(I["b_w_qg"][0]), "w_o": f32(I["b_w_o"][0]),
    }
    shared.update(host_tables())
    shared.update(host_masks())
    maps = []
    for i in range(ncores):
        m = dict(shared)
        m["x"] = f32(I["x"][NSEQ * i:NSEQ * (i + 1)])
        c = I["c"][NSEQ * i:NSEQ * (i + 1)]
        m["cT"] = f32(np.transpose(c.reshape(NSEQ, KC, 128), (2, 1, 0)))
        maps.append(m)
    return maps


def host_masks():
    n = np.arange(128)[:, None]
    m_cmp = np.zeros((128, 16, 128), np.float32)
    for T_ in range(16):
        t = T_ * 128 + np.arange(128)[None, :]
        m_cmp[:, T_, :] = np.where((16 * n + 31 <= t) & (n < 127), 0.0, NEGM)
    r = np.arange(128)[:, None]
    c = np.arange(128)[None, :]
    m_tri = np.zeros((128, 2, 128), np.float32)
    m_tri[:, 0, :] = np.where(r <= c, 0.0, NEGM)
    m_tri[:, 1, :] = np.where(r > c, 0.0, NEGM)
    addtab = np.zeros((128, 16, 32), np.float32)
    for T_ in range(16):
        t = T_ * 128 + np.arange(128)[:, None]
        j = np.arange(32)[None, :]
        cur = t // 64
        forced = (j == 0) | (j == cur) | (j == cur - 1)
        addtab[:, T_, :] = np.where(j * 64 <= t, np.where(forced, 1e6, 0.0), -1e30)
    emat = np.zeros((32, S), np.float32)
    emat[np.arange(S) // 64, np.arange(S)] = 1.0
    ci = np.arange(128)[:, None]
    sj = np.arange(32)[None, :]
    ovl = np.zeros((128, 33), np.float32)
    ovl[:, :32] = ((ci * 16 <= sj * 64 + 63) & (ci * 16 + 31 >= sj * 64) & (ci < 127)).astype(np.float32)
    ovl[:127, 32] = 1.0
    return {"m_cmp": m_cmp, "m_tri": m_tri, "addtab": addtab, "emat": emat, "ovl": ovl}


_NC_CACHE = {}


def kernel(**inputs):
    ncores = 8
    if "prog" not in _NC_CACHE:
        _NC_CACHE["prog"] = build_program()
    nc = _NC_CACHE["prog"]
    maps = make_in_maps(inputs, ncores)
    res = run_bass_kernel_spmd(nc, maps, core_ids=list(range(ncores)))
    out = np.concatenate([np.asarray(r["out"]) for r in res.results], axis=0)
    return out.astype(np.float32)
```

```python
import math
from contextlib import ExitStack

import numpy as np
import concourse.bass as bass
import concourse.mybir as mybir
from concourse.bass_utils import run_bass_kernel_spmd

F32 = mybir.dt.float32
BF16 = mybir.dt.bfloat16
AF = mybir.ActivationFunctionType
ALU = mybir.AluOpType
AX = mybir.AxisListType

D = 2048
S = 2048
NSEQ = 2
TB = 512
NBLK = S // TB
KC = D // 128
DFF = 8192
GW = 4096
EPS = 1e-6
NEGM = -30000.0


class Tok:
    __slots__ = ("sem", "val")

    def __init__(self, sem, val):
        self.sem = sem
        self.val = val


class T:
    def __init__(self, ap):
        self.ap = ap
        self.w = None
        self.r = {}

    def __getitem__(self, k):
        return self.ap[k]


class Eng:
    def __init__(self, nc, name, eng, stack):
        self.name = name
        self.eng = eng
        self.sem = stack.enter_context(nc.semaphore("s_" + name))
        self.cnt = 0
        self.waited = {}

    def wait(self, tok, raw):
        if tok is None:
            return
        if tok.sem is self.sem and not raw and self.name == "pe":
            return
        k = id(tok.sem)
        if self.waited.get(k, 0) >= tok.val:
            return
        self.eng.wait_ge(tok.sem, tok.val)
        self.waited[k] = tok.val


class Sched:
    NDMASEM = 16

    def __init__(self, nc, stack):
        self.nc = nc
        self.pe = Eng(nc, "pe", nc.tensor, stack)
        self.act = Eng(nc, "act", nc.scalar, stack)
        self.dve = Eng(nc, "dve", nc.vector, stack)
        self.pool = Eng(nc, "pool", nc.gpsimd, stack)
        self.sp = Eng(nc, "sp", nc.sync, stack)
        self.engs = [self.pe, self.act, self.dve, self.pool, self.sp]
        self.dsems = {id(q): [stack.enter_context(nc.semaphore("dma%s%d" % (q.name, i))) for i in range(self.NDMASEM)]
                      for q in (self.sp, self.pool)}
        self.dn = {id(q): 0 for q in (self.sp, self.pool)}
        self.dma_toks = {}

    def _deps(self, E, reads, writes):
        for t in reads:
            E.wait(t.w, True)
        for t in writes:
            E.wait(t.w, False)
            for tok in t.r.values():
                E.wait(tok, False)

    def _upd(self, tok, reads, writes):
        for t in writes:
            t.w = tok
            t.r = {}
        for t in reads:
            t.r[id(tok.sem)] = tok

    def op(self, E, inst_fn, reads=(), writes=()):
        self._deps(E, reads, writes)
        inst = inst_fn(E.eng)
        E.cnt += 1
        inst.then_inc(E.sem, 1)
        self._upd(Tok(E.sem, E.cnt), reads, writes)

    def dma(self, Q, out_ap, in_ap, reads=(), writes=()):
        n = self.dn[id(Q)]
        slot = n % self.NDMASEM
        sem = self.dsems[id(Q)][slot]
        val = 16 * (n // self.NDMASEM + 1)
        self.dn[id(Q)] = n + 1
        if val > 16:
            Q.wait(Tok(sem, val - 16), True)
        self._deps(Q, reads, writes)
        Q.eng.dma_start(out=out_ap, in_=in_ap).then_inc(sem, 16)
        tok = Tok(sem, val)
        self.dma_toks[(id(Q), slot)] = tok
        self._upd(tok, reads, writes)

    def barrier(self):
        toks = [Tok(e.sem, e.cnt) for e in self.engs if e.cnt > 0] + list(self.dma_toks.values())
        for e in self.engs:
            for tok in toks:
                e.wait(tok, True)

    def finish(self):
        for tok in self.dma_toks.values():
            self.sp.wait(tok, True)


VOFF = {}
_o = 0
for _n, _w in [("mod_b00", 48), ("mod_b01", 48), ("mod_b10", 48), ("mod_b11", 48), ("kv_mod_b", 32),
               ("g00", 16), ("g01", 16), ("g10", 16), ("g11", 16), ("gkv", 16), ("gfin", 16),
               ("b_in_u", 32), ("ln_g", 32), ("ln_b", 32), ("b_out", 16)]:
    VOFF[_n] = (_o, _w)
    _o += _w
NV = _o


def fm(v):
    v = np.asarray(v, np.float32)
    return np.ascontiguousarray(v.reshape(-1, 128).T)


class Ctx:
    pass


def build_program(debug=False, stop_after=None):
    nc = bass.Bass("TRN2", target_bir_lowering=False)
    C = Ctx()
    C.nc = nc

    def din(name, shape, dt=F32):
        return nc.dram_tensor(name, list(shape), dt, kind="ExternalInput").ap()

    dbgkind = "ExternalOutput" if debug else "Internal"
    x_d = din("x", [NSEQ, S, D])
    cT_d = din("cT", [128, KC, NSEQ])
    vecs_d = din("vecs", [128, NV])
    modw_d = din("mod_w", [2, 2, D, 3 * D])
    kvmodw_d = din("kv_mod_w", [D, 2 * D])
    w1_d = din("mlp_w1", [2, D, DFF])
    w2_d = din("mlp_w2", [2, DFF, D])
    win_d = din("a_w_in", [D, 2 * GW])
    wout_d = din("a_w_out", [GW, D])
    binv_d = din("b_in_v", [1, GW])
    wsT_d = din("w_sT", [128, 16, 128])
    bs_d = din("b_s", [1, 2048])
    wkv_d = din("w_kv", [D, 3072])
    cpos_d = din("cmp_posT", [2, 128, 32])
    cw1_d = din("cmp_w1", [2, 4096, 256])
    cw2_d = din("cmp_w2", [2, 256, 128])
    wqg_d = din("w_qg", [D, 2096])
    wo_d = din("w_o", [D, D])
    cos_d = din("cos2", [128, S])
    sin_d = din("sinS", [128, S])
    ident_d = din("ident", [128, 128])
    mcmp_d = din("m_cmp", [128, 16, 128])
    mtri_d = din("m_tri", [128, 2, 128])
    addtab_d = din("addtab", [128, 16, 32])
    emat_d = din("emat", [32, S])
    ovl_d = din("ovl", [128, 33])
    out_d = nc.dram_tensor("out", [NSEQ, S, D], F32, kind="ExternalOutput").ap()

    x2T_d = nc.dram_tensor("x2T", [NSEQ * NBLK, 128, KC, TB], F32, kind=dbgkind).ap()
    kT_d = nc.dram_tensor("kT", [4, NSEQ, 128, 4, S], BF16, kind=dbgkind).ap()
    vtm_d = nc.dram_tensor("vtm", [2, NSEQ, S, 512], BF16, kind=dbgkind).ap()
    dbg_d = nc.dram_tensor("dbg", [8, 128, KC, TB], F32, kind=dbgkind).ap()
    dbgc_d = nc.dram_tensor("dbgc", [NSEQ, 2, 128, 4, 128], BF16, kind=dbgkind).ap()

    with ExitStack() as st:
        sc = Sched(nc, st)
        C.sc = sc
        PE, ACT, DVE, POOL, SP = sc.pe, sc.act, sc.dve, sc.pool, sc.sp

        sbn = [0]
        grave = {}

        def retire(t):
            toks = list(t.r.values())
            if t.w is not None:
                toks.append(t.w)
            for tok in toks:
                k = id(tok.sem)
                if k not in grave or grave[k].val < tok.val:
                    grave[k] = tok

        def sb(name, shape, dt, stack=st):
            sbn[0] += 1
            t = T(stack.enter_context(nc.sbuf_tensor("sb%d_%s" % (sbn[0], name), list(shape), dt)))
            t.r = dict(grave)
            stack.callback(retire, t)
            return t

        banks = [T(st.enter_context(nc.psum_tensor("bank%d" % i, [128, 512], F32))) for i in range(7)]
        pbf = T(st.enter_context(nc.psum_tensor("pbf", [128, 1024], BF16)))
        bank_i = [0]

        def nb():
            b = banks[bank_i[0] % 6]
            bank_i[0] += 1
            return b

        xT = [sb("xT%d" % c, [128, TB], F32) for c in range(KC)]
        hT = sb("hT", [128, KC, TB], BF16)
        hTc = [T(hT.ap[:, c, :]) for c in range(KC)]
        slots = [sb("slot%d" % i, [128, 16, 512], BF16) for i in range(3)]
        slot_i = [0]
        vecs = sb("vecs", [128, NV], F32)
        identf = sb("identf", [128, 128], F32)
        identb = sb("identb", [128, 128], BF16)
        onesD = sb("onesD", [128, 128], BF16)
        ones1 = sb("ones1", [128, 128], BF16)
        condT = sb("condT", [128, KC, NSEQ], BF16)
        cTs = sb("cTs", [128, KC, NSEQ], F32)
        modv = {n: sb("modv_" + n, [128, w, NSEQ], F32) for n, w in
                [("00", 48), ("01", 48), ("10", 48), ("11", 48), ("kv", 32)]}
        Avec = {n: sb("A_" + n, [128, KC, NSEQ], F32) for n in ["00", "01", "10", "11", "kv"]}
        gbo = sb("gbo", [128, KC, NSEQ], F32)
        rstd = sb("rstd", [128, TB], F32)
        epsc = sb("epsc", [128, 1], F32)
        sqt = [sb("sqt%d" % i, [128, TB], BF16) for i in range(2)]
        tmpf = [sb("tmpf%d" % i, [128, TB], F32) for i in range(3)]
        KcT = [sb("KcT%d" % s_, [128, 4, 128], BF16) for s_ in range(NSEQ)]
        Vc = [sb("Vc%d" % s_, [128, 4, 128], BF16) for s_ in range(NSEQ)]
        rr = {"sq": 0, "tf": 0}

        def nsq():
            rr["sq"] += 1
            return sqt[rr["sq"] % 2]

        def ntf():
            rr["tf"] += 1
            return tmpf[rr["tf"] % 3]

        def vcol(name, c=None):
            o, w = VOFF[name]
            if c is None:
                return vecs.ap[:, o:o + w]
            return vecs.ap[:, o + c:o + c + 1]

        def panel_src(w2d, r0, c0):
            key = (str(w2d), r0, c0)
            return (key, w2d[r0:r0 + 2048, c0:c0 + 512].rearrange("(kc p) n -> p kc n", p=128))

        pref = []

        def load_panel(p):
            s_ = slots[slot_i[0] % 3]
            slot_i[0] += 1
            sc.dma(POOL, s_.ap[:], p[1], writes=[s_])
            return s_

        def run_steps(steps, nxt=()):
            n = len(steps)
            assigned = {}
            pidx = [i for i in range(n) if steps[i][0] is not None]
            pos = {i: j for j, i in enumerate(pidx)}

            def issue(j):
                if j < len(pidx):
                    i = pidx[j]
                    if pref:
                        key, s_ = pref.pop(0)
                        assert key == steps[i][0][0], (key, steps[i][0][0])
                        assigned[i] = s_
                    else:
                        assigned[i] = load_panel(steps[i][0])
            issue(0)
            issue(1)
            for i in range(n):
                if i in pos:
                    issue(pos[i] + 2)
                steps[i][1](assigned.get(i))
            assert not pref
            for p in nxt:
                pref.append((p[0], load_panel(p)))

        P_gmlp = [panel_src(win_d, 0, GW), panel_src(win_d, 0, GW + 512)]
        P_mlp = [[panel_src(w1_d[l], 0, 0), panel_src(w1_d[l], 0, 512)] for l in range(2)]
        P_kv = [panel_src(wkv_d, 0, 0), panel_src(wkv_d, 0, 512)]
        P_q = [panel_src(wqg_d, 0, 0), panel_src(wqg_d, 0, 512)]
        P_wo = [panel_src(wo_d, 0, 0), panel_src(wo_d, 0, 512)]

        sc.dma(SP, vecs.ap[:], vecs_d, writes=[vecs])
        sc.dma(SP, identf.ap[:], ident_d, writes=[identf])
        sc.dma(SP, cTs.ap[:], cT_d, writes=[cTs])
        sc.op(DVE, lambda e: e.tensor_copy(out=identb.ap[:], in_=identf.ap[:]), reads=[identf], writes=[identb])
        sc.op(DVE, lambda e: e.memset(onesD.ap[:], 1.0 / D), writes=[onesD])
        sc.op(DVE, lambda e: e.memset(ones1.ap[:], 1.0), writes=[ones1])
        sc.op(DVE, lambda e: e.memset(epsc.ap[:], EPS), writes=[epsc])
        sc.op(ACT, lambda e: e.activation(out=condT.ap[:], in_=cTs.ap[:], func=AF.Silu), reads=[cTs], writes=[condT])

        def mod_steps(name, w2d, ncols, bname):
            steps = []
            for p in range(ncols // 512):
                def fn(slot, p=p):
                    bk = nb()
                    for fc in range(4):
                        for k in range(KC):
                            sc.op(PE, lambda e, fc=fc, k=k: e.matmul(
                                bk.ap[:, fc * NSEQ:(fc + 1) * NSEQ], lhsT=slot.ap[:, k, fc * 128:(fc + 1) * 128],
                                rhs=condT.ap[:, k, :], start=(k == 0), stop=(k == KC - 1)),
                                reads=[slot, condT], writes=[bk])
                    o, _ = VOFF[bname]
                    sc.op(DVE, lambda e: e.tensor_tensor(
                        out=modv[name].ap[:, p * 4:(p + 1) * 4, :],
                        in0=bk.ap[:, 0:4 * NSEQ].rearrange("p (a b) -> p a b", b=NSEQ),
                        in1=vecs.ap[:, o + p * 4:o + (p + 1) * 4].unsqueeze(2).to_broadcast([128, 4, NSEQ]),
                        op=ALU.add), reads=[bk, vecs], writes=[modv[name]])
                steps.append((panel_src(w2d, 0, p * 512), fn))
            return steps

        def derive(name, gname, has_gate_bias=False):
            mv = modv[name]
            A = Avec[name]
            o, _ = VOFF[gname]
            sc.op(DVE, lambda e: e.tensor_scalar(out=A.ap[:], in0=mv.ap[:, 16:32, :], scalar1=1.0, scalar2=None,
                                                 op0=ALU.add), reads=[mv], writes=[A])
            sc.op(DVE, lambda e: e.tensor_tensor(out=A.ap[:], in0=A.ap[:],
                                                 in1=vecs.ap[:, o:o + 16].unsqueeze(2).to_broadcast([128, 16, NSEQ]),
                                                 op=ALU.mult), reads=[A, vecs], writes=[A])

        steps = []
        steps += mod_steps("00", modw_d[0, 0], 3 * D, "mod_b00")
        steps += mod_steps("01", modw_d[0, 1], 3 * D, "mod_b01")
        steps += mod_steps("kv", kvmodw_d, 2 * D, "kv_mod_b")
        steps += mod_steps("10", modw_d[1, 0], 3 * D, "mod_b10")
        steps += mod_steps("11", modw_d[1, 1], 3 * D, "mod_b11")
        run_steps(steps, nxt=P_gmlp)
        derive("00", "g00")
        derive("01", "g01")
        derive("kv", "gkv")
        derive("10", "g10")
        derive("11", "g11")
        ob, _ = VOFF["b_out"]
        sc.op(DVE, lambda e: e.tensor_tensor(out=gbo.ap[:], in0=modv["00"].ap[:, 32:48, :],
                                             in1=vecs.ap[:, ob:ob + 16].unsqueeze(2).to_broadcast([128, 16, NSEQ]),
                                             op=ALU.mult), reads=[modv["00"], vecs], writes=[gbo])

        sqst = {"done": 0, "pend": []}

        def flush_sq():
            bk = banks[6]
            for c in sqst["pend"]:
                assert c == sqst["done"]
                q = nsq()
                sc.op(ACT, lambda e, c=c, q=q: e.activation(out=q.ap[:], in_=xT[c].ap[:], func=AF.Square),
                      reads=[xT[c]], writes=[q])
                sc.op(PE, lambda e, c=c, q=q: e.matmul(bk.ap[:], lhsT=onesD.ap[:], rhs=q.ap[:],
                                                       start=(c == 0), stop=(c == KC - 1)),
                      reads=[q, onesD], writes=[bk])
                sqst["done"] += 1
            sqst["pend"] = []

        def x_final(c):
            sqst["pend"].append(c)

        def norm_to_hT(name, b):
            bk = banks[6]
            flush_sq()
            for c in range(sqst["done"], KC):
                sqst["pend"].append(c)
            flush_sq()
            assert sqst["done"] == KC
            sqst["done"] = 0
            sc.op(ACT, lambda e: e.activation(out=rstd.ap[:], in_=bk.ap[:], func=AF.Sqrt, bias=epsc.ap[:, 0:1]),
                  reads=[bk, epsc], writes=[rstd])
            sc.op(DVE, lambda e: e.reciprocal(out=rstd.ap[:], in_=rstd.ap[:]), reads=[rstd], writes=[rstd])
            if name is None:
                return
            A = Avec[name]
            mv = modv[name]
            for c in range(KC):
                t_ = ntf()
                sc.op(DVE, lambda e, c=c, t_=t_: e.tensor_tensor(out=t_.ap[:], in0=xT[c].ap[:], in1=rstd.ap[:],
                                                                 op=ALU.mult), reads=[xT[c], rstd], writes=[t_])
                sc.op(ACT, lambda e, c=c, t_=t_: e.activation(out=hT.ap[:, c, :], in_=t_.ap[:], func=AF.Identity,
                                                              bias=mv.ap[:, c, b:b + 1], scale=A.ap[:, c, b:b + 1]),
                      reads=[t_, A, mv], writes=[hTc[c]])

        def resid_add(bk, c, gate_ap, bias_ap=None):
            t_ = ntf()
            if bias_ap is None:
                sc.op(ACT, lambda e: e.activation(out=t_.ap[:], in_=bk.ap[:], func=AF.Copy, scale=gate_ap),
                      reads=[bk], writes=[t_])
            else:
                sc.op(ACT, lambda e: e.activation(out=t_.ap[:], in_=bk.ap[:], func=AF.Identity, scale=gate_ap,
                                                  bias=bias_ap), reads=[bk], writes=[t_])
            sc.op(DVE, lambda e: e.tensor_tensor(out=xT[c].ap[:], in0=xT[c].ap[:], in1=t_.ap[:], op=ALU.add),
                  reads=[xT[c], t_], writes=[xT[c]])
            x_final(c)

        def mlp_phase(l, b, nxt=()):
            name = "%d1" % l
            norm_to_hT(name, b)
            with ExitStack() as ps:
                hid = [sb("hid%d" % i, [128, TB], BF16, ps) for i in range(64)]
                steps = []
                for fp in range(16):
                    def fn(slot, fp=fp):
                        for j in range(4):
                            fc = fp * 4 + j
                            bk = nb()
                            for k in range(KC):
                                sc.op(PE, lambda e, k=k, j=j: e.matmul(
                                    bk.ap[:], lhsT=slot.ap[:, k, j * 128:(j + 1) * 128], rhs=hT.ap[:, k, :],
                                    start=(k == 0), stop=(k == KC - 1)), reads=[slot, hTc[k]], writes=[bk])
                            t_ = ntf()
                            sc.op(ACT, lambda e: e.activation(out=t_.ap[:], in_=bk.ap[:], func=AF.Relu),
                                  reads=[bk], writes=[t_])
                            sc.op(DVE, lambda e, fc=fc: e.tensor_tensor(out=hid[fc].ap[:], in0=t_.ap[:], in1=t_.ap[:],
                                                                        op=ALU.mult), reads=[t_], writes=[hid[fc]])
                    steps.append((panel_src(w1_d[l], 0, fp * 512), fn))
                acc = {}
                for og in range(4):
                    for fp in range(4):
                        def fn(slot, og=og, fp=fp):
                            if fp == 0:
                                acc[og] = [nb() for _ in range(4)]
                            for oc in range(4):
                                bk = acc[og][oc]
                                for k in range(16):
                                    sc.op(PE, lambda e, k=k, oc=oc: e.matmul(
                                        bk.ap[:], lhsT=slot.ap[:, k, oc * 128:(oc + 1) * 128],
                                        rhs=hid[fp * 16 + k].ap[:], start=(fp == 0 and k == 0),
                                        stop=(fp == 3 and k == 15)), reads=[slot, hid[fp * 16 + k]], writes=[bk])
                            if fp == 1:
                                flush_sq()
                            if fp == 3:
                                for oc in range(4):
                                    c = og * 4 + oc
                                    resid_add(acc[og][oc], c, modv[name].ap[:, 32 + c, b:b + 1])
                        steps.append((panel_src(w2_d[l], fp * 2048, og * 512), fn))
                run_steps(steps, nxt=nxt)

        C.__dict__.update(locals())
        passA(C)
        if stop_after != "A":
            passB(C)
        sc.barrier()
        sc.finish()
    return nc


def rope_from_bank(C, bk, out_t, out_ap, extra_reads=()):
    sc, ACT, DVE = C.sc, C.sc.act, C.sc.dve
    xs = C.ntf()
    sw = C.ntf()
    sc.op(ACT, lambda e: e.activation(out=xs.ap[:], in_=bk.ap[:], func=AF.Copy), reads=[bk], writes=[xs])
    sc.op(DVE, lambda e: e.tensor_copy(out=sw.ap[0:64, :], in_=xs.ap[64:128, :]), reads=[xs], writes=[sw])
    sc.op(DVE, lambda e: e.tensor_copy(out=sw.ap[64:128, :], in_=xs.ap[0:64, :]), reads=[xs], writes=[sw])
    sc.op(DVE, lambda e: e.tensor_tensor(out=xs.ap[:], in0=xs.ap[:], in1=C.cosb.ap[:], op=ALU.mult),
          reads=[xs, C.cosb, sw], writes=[xs])
    sc.op(DVE, lambda e: e.tensor_tensor(out=sw.ap[:], in0=sw.ap[:], in1=C.sinb.ap[:], op=ALU.mult),
          reads=[sw, C.sinb], writes=[sw])
    sc.op(DVE, lambda e: e.tensor_tensor(out=out_ap, in0=xs.ap[:], in1=sw.ap[:], op=ALU.add),
          reads=[xs, sw] + list(extra_reads), writes=[out_t])


def passA(C):
    nc, sc = C.nc, C.sc
    PE, ACT, DVE, POOL, SP = sc.pe, sc.act, sc.dve, sc.pool, sc.sp
    nb, ntf, nsq, xT, hT, vecs, VO = C.nb, C.ntf, C.nsq, C.xT, C.hT, C.vecs, VOFF
    sb, run_steps, panel_src = C.sb, C.run_steps, C.panel_src
    modv, Avec = C.modv, C.Avec
    C_hTc = C.hTc
    with ExitStack() as pa:
        WcT = sb("WcT", [128, 16, 128], BF16, pa)
        BiasT = sb("BiasT", [128, 32, 128], F32, pa)
        binv = sb("binv", [128, 1536], BF16, pa)
        s1 = sb("s1", [128, 32], F32, pa)
        s2 = sb("s2", [128, 32], F32, pa)
        st4 = sb("st4", [128, 4, 4], F32, pa)
        junk = sb("junk", [128, TB], BF16, pa)
        ko_i = [0]
        with ExitStack() as ps:
            rs_rep = sb("rs_rep", [128, 16, 128], F32, ps)
            bs_rep = sb("bs_rep", [128, 16, 128], F32, ps)
            sc.dma(POOL, WcT.ap[:], C.wsT_d, writes=[WcT])
            for vp in range(8):
                sc.dma(POOL, binv.ap[32 * (vp % 3):32 * (vp % 3) + 1, (vp // 3) * 512:(vp // 3 + 1) * 512],
                       C.binv_d[0:1, vp * 512:(vp + 1) * 512], writes=[binv])
            sc.dma(SP, bs_rep.ap[:].rearrange("p g t -> p (g t)"), C.bs_d.partition_broadcast(128), writes=[bs_rep])
            sc.op(POOL, lambda e: e.affine_select(out=WcT.ap[:], in_=WcT.ap[:], pattern=[[0, 16], [1, 128]],
                                                  compare_op=ALU.is_ge, fill=0.0, base=0, channel_multiplier=-1),
                  reads=[WcT], writes=[WcT])
            for q in range(4):
                bk = nb()
                sc.op(PE, lambda e, q=q: e.matmul(bk.ap[:], lhsT=C.ones1.ap[:],
                                                  rhs=WcT.ap[:, q * 4:(q + 1) * 4, :], start=True, stop=True),
                      reads=[WcT, C.ones1], writes=[bk])
                sc.op(ACT, lambda e, q=q: e.activation(out=rs_rep.ap[:, q * 4:(q + 1) * 4, :].rearrange("p a b -> p (a b)"),
                                                       in_=bk.ap[:], func=AF.Copy), reads=[bk], writes=[rs_rep])
            olb, _ = VO["ln_b"]
            for fc in range(32):
                g = fc // 2
                sc.op(DVE, lambda e, fc=fc, g=g: e.scalar_tensor_tensor(
                    out=BiasT.ap[:, fc, :], in0=rs_rep.ap[:, g, :], scalar=vecs.ap[:, olb + fc:olb + fc + 1],
                    in1=bs_rep.ap[:, g, :], op0=ALU.mult, op1=ALU.add), reads=[rs_rep, bs_rep, vecs], writes=[BiasT])

        olg, _ = VO["ln_g"]
        obu, _ = VO["b_in_u"]
        for seq in range(NSEQ):
            for blk in range(NBLK):
                t0 = blk * TB
                b = seq
                with ExitStack() as ps:
                    xtm = [sb("xtm%d" % i, [128, D], F32, ps) for i in range(4)]
                    for tt in range(4):
                        sc.dma(SP, xtm[tt].ap[:], C.x_d[seq, t0 + tt * 128:t0 + (tt + 1) * 128, :], writes=[xtm[tt]])
                    for c in range(KC):
                        bk = nb()
                        for tt in range(4):
                            sc.op(PE, lambda e, tt=tt, c=c: e.transpose(bk.ap[:, tt * 128:(tt + 1) * 128],
                                                                        xtm[tt].ap[:, c * 128:(c + 1) * 128],
                                                                        C.identf.ap[:]),
                                  reads=[xtm[tt], C.identf], writes=[bk])
                        if c >= 2:
                            C.flush_sq()
                        if c >= 1:
                            C.x_final(c - 1)
                        if c % 2:
                            sc.op(ACT, lambda e, c=c: e.activation(out=xT[c].ap[:], in_=bk.ap[:], func=AF.Copy),
                                  reads=[bk], writes=[xT[c]])
                        else:
                            sc.op(DVE, lambda e, c=c: e.tensor_copy(out=xT[c].ap[:], in_=bk.ap[:]),
                                  reads=[bk], writes=[xT[c]])
                    C.x_final(KC - 1)

                C.norm_to_hT("00", b)
                with ExitStack() as ps:
                    sT = [sb("sT%d" % i, [128, TB], BF16, ps) for i in range(32)]
                    v = [sb("v%d" % i, [128, GW], BF16, ps) for i in range(4)]
                    sc.op(DVE, lambda e: e.memset(s1.ap[:], 0.0), writes=[s1])
                    sc.op(DVE, lambda e: e.memset(s2.ap[:], 0.0), writes=[s2])
                    steps = []
                    for vp in range(8):
                        def fn(slot, vp=vp):
                            for tt in range(4):
                                bk = nb()
                                for k in range(KC):
                                    sc.op(PE, lambda e, k=k, tt=tt: e.matmul(
                                        bk.ap[:], lhsT=hT.ap[:, k, tt * 128:(tt + 1) * 128], rhs=slot.ap[:, k, :],
                                        start=(k == 0), stop=False), reads=[slot, C_hTc[k]], writes=[bk])
                                pb = 32 * (vp % 3)
                                sc.op(PE, lambda e: e.matmul(bk.ap[:], lhsT=C.ones1.ap[pb:pb + 1, :],
                                                             rhs=binv.ap[pb:pb + 1, (vp // 3) * 512:(vp // 3 + 1) * 512],
                                                             start=False, stop=True),
                                      reads=[C.ones1, binv], writes=[bk])
                                col = tt * 8 + vp
                                sc.op(ACT, lambda e, tt=tt, col=col: e.activation(
                                    out=v[tt].ap[:, vp * 512:(vp + 1) * 512], in_=bk.ap[:], func=AF.Gelu,
                                    accum_out=s1.ap[:, col:col + 1]), reads=[bk], writes=[v[tt], s1])
                                sc.op(ACT, lambda e, tt=tt, col=col: e.activation(
                                    out=junk.ap[:], in_=v[tt].ap[:, vp * 512:(vp + 1) * 512], func=AF.Square,
                                    accum_out=s2.ap[:, col:col + 1]), reads=[v[tt]], writes=[junk, s2])
                        steps.append((panel_src(C.win_d, 0, GW + vp * 512), fn))

                    def ln_and_spatial(slot):
                        sc.op(DVE, lambda e: e.tensor_reduce(out=st4.ap[:, 0, :], in_=s1.ap[:].rearrange("p (a b) -> p a b", b=8),
                                                             axis=AX.X, op=ALU.add), reads=[s1], writes=[st4])
                        sc.op(DVE, lambda e: e.tensor_reduce(out=st4.ap[:, 3, :], in_=s2.ap[:].rearrange("p (a b) -> p a b", b=8),
                                                             axis=AX.X, op=ALU.add), reads=[s2, st4], writes=[st4])
                        sc.op(DVE, lambda e: e.tensor_scalar(out=st4.ap[:, 0, :], in0=st4.ap[:, 0, :], scalar1=1.0 / GW,
                                                             scalar2=None, op0=ALU.mult), reads=[st4], writes=[st4])
                        sc.op(DVE, lambda e: e.tensor_tensor(out=st4.ap[:, 1, :], in0=st4.ap[:, 0, :], in1=st4.ap[:, 0, :],
                                                             op=ALU.mult), reads=[st4], writes=[st4])
                        sc.op(DVE, lambda e: e.scalar_tensor_tensor(out=st4.ap[:, 1, :], in0=st4.ap[:, 3, :], scalar=1.0 / GW,
                                                                    in1=st4.ap[:, 1, :], op0=ALU.mult, op1=ALU.subtract),
                              reads=[st4], writes=[st4])
                        sc.op(ACT, lambda e: e.activation(out=st4.ap[:, 2, :], in_=st4.ap[:, 1, :], func=AF.Sqrt,
                                                          bias=C.epsc.ap[:, 0:1]), reads=[st4, C.epsc], writes=[st4])
                        sc.op(DVE, lambda e: e.reciprocal(out=st4.ap[:, 2, :], in_=st4.ap[:, 2, :]), reads=[st4], writes=[st4])
                        for tt in range(4):
                            sc.op(DVE, lambda e, tt=tt: e.tensor_scalar(
                                out=v[tt].ap[:], in0=v[tt].ap[:], scalar1=st4.ap[:, 0, tt:tt + 1],
                                scalar2=st4.ap[:, 2, tt:tt + 1], op0=ALU.subtract, op1=ALU.mult),
                                reads=[v[tt], st4], writes=[v[tt]])
                        for tt in range(4):
                            for f4 in range(8):
                                bk = nb()
                                for j in range(4):
                                    fc = f4 * 4 + j
                                    sc.op(PE, lambda e, j=j, fc=fc, tt=tt: e.matmul(
                                        bk.ap[:, j * 128:(j + 1) * 128], lhsT=v[tt].ap[:, fc * 128:(fc + 1) * 128],
                                        rhs=WcT.ap[:, fc // 2, :], start=True, stop=True),
                                        reads=[v[tt], WcT], writes=[bk])
                                for j in range(4):
                                    fc = f4 * 4 + j
                                    sc.op(DVE, lambda e, j=j, fc=fc, tt=tt: e.scalar_tensor_tensor(
                                        out=sT[fc].ap[:, tt * 128:(tt + 1) * 128], in0=bk.ap[:, j * 128:(j + 1) * 128],
                                        scalar=vecs.ap[:, olg + fc:olg + fc + 1], in1=BiasT.ap[:, fc, :],
                                        op0=ALU.mult, op1=ALU.add), reads=[bk, vecs, BiasT], writes=[sT[fc]])
                    steps.append((None, ln_and_spatial))
                    for up in range(8):
                        def fn(slot, up=up):
                            for j in range(4):
                                fc = up * 4 + j
                                bk = nb()
                                for k in range(KC):
                                    sc.op(PE, lambda e, k=k, j=j: e.matmul(
                                        bk.ap[:], lhsT=slot.ap[:, k, j * 128:(j + 1) * 128], rhs=hT.ap[:, k, :],
                                        start=(k == 0), stop=(k == KC - 1)), reads=[slot, C_hTc[k]], writes=[bk])
                                t_ = ntf()
                                sc.op(ACT, lambda e, fc=fc: e.activation(out=t_.ap[:], in_=bk.ap[:], func=AF.Gelu,
                                                                         bias=vecs.ap[:, obu + fc:obu + fc + 1]),
                                      reads=[bk, vecs], writes=[t_])
                                sc.op(DVE, lambda e, fc=fc: e.tensor_tensor(out=sT[fc].ap[:], in0=t_.ap[:], in1=sT[fc].ap[:],
                                                                            op=ALU.mult), reads=[t_, sT[fc]], writes=[sT[fc]])
                        steps.append((panel_src(C.win_d, 0, up * 512), fn))
                    acc = {}
                    for og in range(4):
                        for fp in range(2):
                            def fn(slot, og=og, fp=fp):
                                if fp == 0:
                                    acc[og] = [nb() for _ in range(4)]
                                for oc in range(4):
                                    bk = acc[og][oc]
                                    for k in range(16):
                                        sc.op(PE, lambda e, k=k, oc=oc: e.matmul(
                                            bk.ap[:], lhsT=slot.ap[:, k, oc * 128:(oc + 1) * 128],
                                            rhs=sT[fp * 16 + k].ap[:], start=(fp == 0 and k == 0),
                                            stop=(fp == 1 and k == 15)), reads=[slot, sT[fp * 16 + k]], writes=[bk])
                                if fp == 1:
                                    C.flush_sq()
                                if fp == 1:
                                    for oc in range(4):
                                        c = og * 4 + oc
                                        C.resid_add(acc[og][oc], c, modv["00"].ap[:, 32 + c, b:b + 1],
                                                    C.gbo.ap[:, c, b:b + 1])
                            steps.append((panel_src(C.wout_d, fp * 2048, og * 512), fn))
                    run_steps(steps, nxt=C.P_mlp[0])
                if C.debug and seq == 0 and blk == 0:
                    for c in range(KC):
                        sc.dma(SP, C.dbg_d[0, :, c, :], xT[c].ap[:], reads=[xT[c]])
                if C.stop_after == "gmlp":
                    return

                C.mlp_phase(0, b, nxt=C.P_kv)

                for c in range(KC):
                    sc.dma(SP, C.x2T_d[seq * NBLK + blk, :, c, :], xT[c].ap[:], reads=[xT[c]])
                C.norm_to_hT("kv", b)
                kvs = ExitStack()
                kout = [sb("kout%d" % i, [128, TB], BF16, kvs) for i in range(4)]
                C.cosb = sb("cosb", [128, TB], F32, kvs)
                C.sinb = sb("sinb", [128, TB], F32, kvs)
                sc.dma(SP, C.cosb.ap[:], C.cos_d[:, t0:t0 + TB], writes=[C.cosb])
                sc.dma(SP, C.sinb.ap[:], C.sin_d[:, t0:t0 + TB], writes=[C.sinb])
                steps = []
                for br in range(6):
                    def fn(slot, br=br):
                        if br in (0, 1, 2, 4):
                            ki = {0: 0, 1: 1, 2: 2, 4: 3}[br]
                            for g in range(4):
                                bk = nb()
                                for k in range(KC):
                                    sc.op(PE, lambda e, k=k, g=g: e.matmul(
                                        bk.ap[:], lhsT=slot.ap[:, k, g * 128:(g + 1) * 128], rhs=hT.ap[:, k, :],
                                        start=(k == 0), stop=(k == KC - 1)), reads=[slot, C_hTc[k]], writes=[bk])
                                ko = kout[ko_i[0] % 4]
                                ko_i[0] += 1
                                if br == 1:
                                    sc.op(ACT, lambda e: e.activation(out=ko.ap[:], in_=bk.ap[:], func=AF.Copy),
                                          reads=[bk], writes=[ko])
                                else:
                                    rope_from_bank(C, bk, ko, ko.ap[:])
                                sc.dma(SP, C.kT_d[ki, seq, :, g, t0:t0 + TB], ko.ap[:], reads=[ko])
                        else:
                            vi = {3: 0, 5: 1}[br]
                            for tt in range(4):
                                bk = nb()
                                for k in range(KC):
                                    sc.op(PE, lambda e, k=k, tt=tt: e.matmul(
                                        bk.ap[:], lhsT=hT.ap[:, k, tt * 128:(tt + 1) * 128], rhs=slot.ap[:, k, :],
                                        start=(k == 0), stop=(k == KC - 1)), reads=[slot, C_hTc[k]], writes=[bk])
                                ko = kout[ko_i[0] % 4]
                                ko_i[0] += 1
                                sc.op(ACT, lambda e: e.activation(out=ko.ap[:], in_=bk.ap[:], func=AF.Copy),
                                      reads=[bk], writes=[ko])
                                sc.dma(SP, C.vtm_d[vi, seq, t0 + tt * 128:t0 + (tt + 1) * 128, :], ko.ap[:], reads=[ko])
                    steps.append((panel_src(C.wkv_d, 0, br * 512), fn))
                run_steps(steps, nxt=(C.P_q if (seq == NSEQ - 1 and blk == NBLK - 1) else C.P_gmlp))
                kvs.close()
                if C.stop_after == "blk0":
                    return
            sc.barrier()
            compress_seq(C, seq)


def compress_seq(C, seq):
    nc, sc = C.nc, C.sc
    PE, ACT, DVE, POOL, SP = sc.pe, sc.act, sc.dve, sc.pool, sc.sp
    nb, sb = C.nb, C.sb
    with ExitStack() as ps:
        w1s = sb("cw1s", [128, 32, 256], BF16, ps)
        w2s = sb("cw2s", [128, 2, 128], BF16, ps)
        posT = sb("cposT", [128, 32], BF16, ps)
        cvec = sb("ccvec", [128, 2], F32, ps)
        src = sb("csrc", [128, S], BF16, ps)
        zde = sb("czde", [128, 16, 128], BF16, ps)
        hidc = sb("chid", [128, 2, 128], BF16, ps)
        for kv in range(2):
            sc.dma(POOL, w1s.ap[:], C.cw1_d[kv].rearrange("(r p) j -> p r j", p=128), writes=[w1s])
            sc.dma(POOL, w2s.ap[:], C.cw2_d[kv].rearrange("(a p) d -> p a d", p=128), writes=[w2s])
            sc.dma(POOL, posT.ap[:], C.cpos_d[kv], writes=[posT])
            bk = nb()
            for jc in range(2):
                for r in range(32):
                    sc.op(PE, lambda e, jc=jc, r=r: e.matmul(bk.ap[:, jc:jc + 1], lhsT=w1s.ap[:, r, jc * 128:(jc + 1) * 128],
                                                             rhs=posT.ap[:, r:r + 1], start=(r == 0), stop=(r == 31)),
                          reads=[w1s, posT], writes=[bk])
            sc.op(DVE, lambda e: e.tensor_copy(out=cvec.ap[:], in_=bk.ap[:, 0:2]), reads=[bk], writes=[cvec])
            for g in range(4):
                sc.dma(SP, src.ap[:], C.kT_d[kv, seq, :, g, :], writes=[src])
                sc.op(DVE, lambda e: e.tensor_copy(out=zde.ap[:], in_=src.ap[:].rearrange("p (m rr) -> p rr m", rr=16)),
                      reads=[src], writes=[zde])
                bk = nb()
                for jc in range(2):
                    for r in range(32):
                        sc.op(PE, lambda e, jc=jc, r=r: e.matmul(
                            bk.ap[:, jc * 128:jc * 128 + 127], lhsT=w1s.ap[:, r, jc * 128:(jc + 1) * 128],
                            rhs=zde.ap[:, r % 16, (r // 16):(r // 16) + 127], start=(r == 0), stop=(r == 31)),
                            reads=[w1s, zde], writes=[bk])
                for jc in range(2):
                    sc.op(ACT, lambda e, jc=jc: e.activation(out=hidc.ap[:, jc, 0:127], in_=bk.ap[:, jc * 128:jc * 128 + 127],
                                                             func=AF.Gelu, bias=cvec.ap[:, jc:jc + 1]),
                          reads=[bk, cvec], writes=[hidc])
                bk2 = nb()
                if kv == 0:
                    for jc in range(2):
                        sc.op(PE, lambda e, jc=jc: e.matmul(bk2.ap[:, 0:127], lhsT=w2s.ap[:, jc, :], rhs=hidc.ap[:, jc, 0:127],
                                                            start=(jc == 0), stop=(jc == 1)), reads=[w2s, hidc], writes=[bk2])
                    sc.op(DVE, lambda e, g=g: e.tensor_copy(out=C.KcT[seq].ap[:, g, 0:127], in_=bk2.ap[:, 0:127]),
                          reads=[bk2], writes=[C.KcT[seq]])
                else:
                    for jc in range(2):
                        sc.op(PE, lambda e, jc=jc: e.matmul(bk2.ap[0:127, 0:128], lhsT=hidc.ap[:, jc, 0:127], rhs=w2s.ap[:, jc, :],
                                                            start=(jc == 0), stop=(jc == 1)), reads=[w2s, hidc], writes=[bk2])
                    sc.op(DVE, lambda e, g=g: e.tensor_copy(out=C.Vc[seq].ap[0:127, g, :], in_=bk2.ap[0:127, 0:128]),
                          reads=[bk2], writes=[C.Vc[seq]])
        sc.barrier()
    if C.debug:
        sc.dma(SP, C.dbgc_d[seq, 0], C.KcT[seq].ap[:], reads=[C.KcT[seq]])
        sc.dma(SP, C.dbgc_d[seq, 1], C.Vc[seq].ap[:], reads=[C.Vc[seq]])


def passB(C):
    nc, sc = C.nc, C.sc
    PE, ACT, DVE, POOL, SP = sc.pe, sc.act, sc.dve, sc.pool, sc.sp
    nb, ntf, nsq, xT, hT, vecs, VO = C.nb, C.ntf, C.nsq, C.xT, C.hT, C.vecs, VOFF
    sb, run_steps, panel_src = C.sb, C.run_steps, C.panel_src
    modv, banks, pbf, identb, ones1 = C.modv, C.banks, C.pbf, C.identb, C.ones1
    SCL = 1.0 / math.sqrt(128.0)
    C_hTc = C.hTc
    with ExitStack() as pb:
        m_cmp = sb("m_cmp", [128, 16, 128], BF16, pb)
        m_tri = sb("m_tri", [128, 2, 128], BF16, pb)
        addtab = sb("addtab", [128, 16, 32], F32, pb)
        emat = sb("emat", [32, S], BF16, pb)
        ovl = sb("ovl", [128, 33], BF16, pb)
        wg = sb("wg", [128, KC, 48], BF16, pb)
        gates = sb("gates", [128, 4, 48], F32, pb)
        Pt = [sb("Pt%d" % i, [128, 512], BF16, pb) for i in range(3)]
        oacc = sb("oacc", [128, 4, 128], F32, pb)
        otmp = sb("otmp", [128, 4, 128], F32, pb)
        sm = sb("sm", [128, 64], F32, pb)
        tmpi = sb("tmpi", [128, 4, 32], F32, pb)
        selb = sb("selb", [128, 32], F32, pb)
        selbb = sb("selbb", [128, 32], BF16, pb)
        selT = sb("selT", [32, 128], BF16, pb)
        sc.dma(POOL, m_cmp.ap[:], C.mcmp_d, writes=[m_cmp])
        sc.dma(POOL, m_tri.ap[:], C.mtri_d, writes=[m_tri])
        sc.dma(SP, addtab.ap[:], C.addtab_d, writes=[addtab])
        sc.dma(POOL, emat.ap[:], C.emat_d, writes=[emat])
        sc.dma(POOL, ovl.ap[:], C.ovl_d, writes=[ovl])
        sc.dma(POOL, wg.ap[:], C.wqg_d[:, 2048:2096].rearrange("(kc p) n -> p kc n", p=128), writes=[wg])
        pt_i = [0]
        sb_i = [0]
        o_i = [0]
        otm = hT.ap[:].rearrange("p c t -> p (c t)").rearrange("p (a f) -> p a f", a=4)

        def npt():
            pt_i[0] += 1
            return Pt[pt_i[0] % 3]

        def nbs():
            sb_i[0] += 1
            return banks[sb_i[0] % 3]

        def nbo():
            o_i[0] += 1
            return banks[4 + o_i[0] % 2]
        bkD = banks[6]

        def bc4(ap2d, k):
            return ap2d.unsqueeze(1).to_broadcast([k, 4, 128])

        def v3(ap2d):
            return ap2d.rearrange("p (a b) -> p a b", a=4)

        for seq in range(NSEQ):
            for blk in range(NBLK):
                t0 = blk * TB
                b = seq
                for c in range(KC):
                    sc.dma(SP, xT[c].ap[:], C.x2T_d[seq * NBLK + blk, :, c, :], writes=[xT[c]])
                C.norm_to_hT("10", b)
                qs = ExitStack()
                QT = sb("QT", [128, 16, TB], BF16, qs)
                nt = blk * 4 + 4
                wlo = max(0, blk * 4 - 4)
                Ks = sb("Ks", [128, 4, S], BF16, qs)
                Vs = sb("Vs", [128, 16, 512], BF16, qs)
                Kw = sb("Kw", [128, 4, 1024], BF16, qs)
                Vw = sb("Vw", [128, 8, 512], BF16, qs)
                te = nt * 128
                Ksg = [T(Ks.ap[:, g, :]) for g in range(4)]
                Kwg = [T(Kw.ap[:, g, :]) for g in range(4)]
                for t_ in Ksg:
                    t_.r = dict(Ks.r)
                    qs.callback(C.retire, t_)
                for t_ in Kwg:
                    t_.r = dict(Kw.r)
                    qs.callback(C.retire, t_)
                for g in range(4):
                    sc.dma(SP, Ks.ap[:, g, 0:te], C.kT_d[2, seq, :, g, 0:te], writes=[Ksg[g]])
                    sc.dma(SP, Kw.ap[:, g, 0:te - wlo * 128], C.kT_d[3, seq, :, g, wlo * 128:te], writes=[Kwg[g]])
                sc.dma(SP, Vs.ap[:, 0:nt, :], C.vtm_d[0, seq, 0:te, :].rearrange("(n p) c -> p n c", p=128), writes=[Vs])
                sc.dma(SP, Vw.ap[:, 0:nt - wlo, :], C.vtm_d[1, seq, wlo * 128:te, :].rearrange("(n p) c -> p n c", p=128),
                       writes=[Vw])

                with ExitStack() as ps:
                    C.cosb = sb("cosb", [128, TB], F32, ps)
                    C.sinb = sb("sinb", [128, TB], F32, ps)
                    sc.dma(SP, C.cosb.ap[:], C.cos_d[:, t0:t0 + TB], writes=[C.cosb])
                    sc.dma(SP, C.sinb.ap[:], C.sin_d[:, t0:t0 + TB], writes=[C.sinb])
                    steps = []
                    for g in range(4):
                        def fn(slot, g=g):
                            for j in range(4):
                                h = g * 4 + j
                                bk = nb()
                                for k in range(KC):
                                    sc.op(PE, lambda e, k=k, j=j: e.matmul(
                                        bk.ap[:], lhsT=slot.ap[:, k, j * 128:(j + 1) * 128], rhs=hT.ap[:, k, :],
                                        start=(k == 0), stop=(k == KC - 1)), reads=[slot, C_hTc[k]], writes=[bk])
                                rope_from_bank(C, bk, QT, QT.ap[:, h, :])
                        steps.append((panel_src(C.wqg_d, 0, g * 512), fn))

                    def gate_fn(slot):
                        for tt in range(4):
                            bk = nb()
                            for k in range(KC):
                                sc.op(PE, lambda e, k=k, tt=tt: e.matmul(
                                    bk.ap[:, 0:48], lhsT=hT.ap[:, k, tt * 128:(tt + 1) * 128], rhs=wg.ap[:, k, :],
                                    start=(k == 0), stop=(k == KC - 1)), reads=[wg, C_hTc[k]], writes=[bk])
                            sc.op(ACT, lambda e, tt=tt: e.activation(out=gates.ap[:, tt, :], in_=bk.ap[:, 0:48],
                                                                     func=AF.Sigmoid), reads=[bk], writes=[gates])
                    steps.append((None, gate_fn))
                    run_steps(steps, nxt=C.P_wo)

                with ExitStack() as ps:
                    def finish_branch(bkO, bkDv, den_ap, br, tt, g, first, last):
                        sc.op(DVE, lambda e: e.tensor_scalar(out=sm.ap[:, 0:4], in0=den_ap, scalar1=1e-30, scalar2=None,
                                                             op0=ALU.max), reads=[bkDv], writes=[sm])
                        sc.op(DVE, lambda e: e.reciprocal(out=sm.ap[:, 0:4], in_=sm.ap[:, 0:4]), reads=[sm], writes=[sm])
                        gsl = gates.ap[:, tt, 12 * g:12 * g + 12].rearrange("p (h r) -> p h r", r=3)[:, :, br]
                        sc.op(DVE, lambda e: e.tensor_tensor(out=sm.ap[:, 4:8], in0=sm.ap[:, 0:4], in1=gsl, op=ALU.mult),
                              reads=[sm, gates], writes=[sm])
                        fb = sm.ap[:, 4:8].unsqueeze(2).to_broadcast([128, 4, 128])
                        if first:
                            sc.op(DVE, lambda e: e.tensor_tensor(out=oacc.ap[:], in0=v3(bkO.ap[:]), in1=fb, op=ALU.mult),
                                  reads=[bkO, sm], writes=[oacc])
                        else:
                            sc.op(DVE, lambda e: e.tensor_tensor(out=otmp.ap[:], in0=v3(bkO.ap[:]), in1=fb, op=ALU.mult),
                                  reads=[bkO, sm], writes=[otmp])
                            dst = v3(otm[:, tt, g * 512:(g + 1) * 512]) if last else oacc.ap[:]
                            sc.op(DVE, lambda e: e.tensor_tensor(out=dst, in0=oacc.ap[:], in1=otmp.ap[:], op=ALU.add),
                                  reads=[oacc, otmp], writes=(C_hTc[4 * tt:4 * tt + 4] if last else [oacc]))

                    tiles = []
                    brn = [0]

                    def add_branch(kind, tt, g, T_):
                        q0 = tt * 128
                        Qg = QT.ap[:, 4 * g:4 * g + 4, q0:q0 + 128]
                        st_ = {}
                        bi = brn[0]
                        brn[0] += 1
                        if kind == 0:
                            kts = [0]
                        elif kind == 2:
                            kts = list(range(max(0, T_ - 4), T_ + 1))
                        else:
                            kts = list(range(T_ + 1))
                        for idx, kt in enumerate(kts):
                            first = (idx == 0)
                            lastk = (idx == len(kts) - 1)
                            tl = {}

                            def S(kt=kt, tl=tl):
                                bkS = nbs()
                                P = npt()
                                tl["P"] = P
                                if kind == 0:
                                    sc.op(PE, lambda e: e.matmul(v3(bkS.ap[0:127, :]), lhsT=C.KcT[seq].ap[:, g, 0:127], rhs=Qg,
                                                                 start=True, stop=False), reads=[C.KcT[seq], QT], writes=[bkS])
                                    sc.op(PE, lambda e: e.matmul(v3(bkS.ap[0:127, :]), lhsT=identb.ap[0:127, 0:127],
                                                                 rhs=bc4(m_cmp.ap[0:127, T_, :], 127), start=False, stop=True),
                                          reads=[identb, m_cmp], writes=[bkS])
                                    sc.op(ACT, lambda e: e.activation(out=P.ap[0:127, :], in_=bkS.ap[0:127, :], func=AF.Exp,
                                                                      scale=SCL), reads=[bkS], writes=[P])
                                    return
                                diag = (kt == T_)
                                if kind == 1:
                                    sc.op(PE, lambda e: e.matmul(v3(bkS.ap[:]), lhsT=Ks.ap[:, g, kt * 128:(kt + 1) * 128], rhs=Qg,
                                                                 start=True, stop=False), reads=[Ksg[g], QT], writes=[bkS])
                                    sc.op(PE, lambda e: e.matmul(v3(bkS.ap[:]), lhsT=emat.ap[0:32, kt * 128:(kt + 1) * 128],
                                                                 rhs=bc4(selT.ap[0:32, :], 32), start=False, stop=not diag),
                                          reads=[emat, selT], writes=[bkS])
                                    if diag:
                                        sc.op(PE, lambda e: e.matmul(v3(bkS.ap[:]), lhsT=identb.ap[:],
                                                                     rhs=bc4(m_tri.ap[:, 0, :], 128), start=False, stop=True),
                                              reads=[identb, m_tri], writes=[bkS])
                                else:
                                    lw = kt - wlo
                                    lower = (kt == T_ - 4)
                                    sc.op(PE, lambda e: e.matmul(v3(bkS.ap[:]), lhsT=Kw.ap[:, g, lw * 128:(lw + 1) * 128], rhs=Qg,
                                                                 start=True, stop=not (lower or diag)), reads=[Kwg[g], QT], writes=[bkS])
                                    if lower or diag:
                                        mi = 0 if diag else 1
                                        sc.op(PE, lambda e: e.matmul(v3(bkS.ap[:]), lhsT=identb.ap[:],
                                                                     rhs=bc4(m_tri.ap[:, mi, :], 128), start=False, stop=True),
                                              reads=[identb, m_tri], writes=[bkS])
                                sc.op(ACT, lambda e: e.activation(out=P.ap[:], in_=bkS.ap[:], func=AF.Exp, scale=SCL),
                                      reads=[bkS], writes=[P])

                            def PV(kt=kt, tl=tl, first=first, lastk=lastk):
                                if first:
                                    st_["O"] = banks[3 + bi % 2]
                                    st_["D"] = banks[5 + bi % 2]
                                bkO, bkDv, P = st_["O"], st_["D"], tl["P"]
                                for h in range(4):
                                    s0 = (first and h == 0)
                                    if kind == 0:
                                        sc.op(PE, lambda e, h=h: e.matmul(bkO.ap[:, h * 128:(h + 1) * 128],
                                                                          lhsT=P.ap[0:127, h * 128:(h + 1) * 128],
                                                                          rhs=C.Vc[seq].ap[0:127, g, :], start=s0, stop=True),
                                              reads=[P, C.Vc[seq]], writes=[bkO])
                                        sc.op(PE, lambda e, h=h: e.matmul(bkDv.ap[:, h * 33:(h + 1) * 33],
                                                                          lhsT=P.ap[0:127, h * 128:(h + 1) * 128],
                                                                          rhs=ovl.ap[0:127, :], start=s0, stop=True),
                                              reads=[P, ovl], writes=[bkDv])
                                    else:
                                        if kind == 1:
                                            vsrc, vt = Vs.ap[:, kt, g * 128:(g + 1) * 128], Vs
                                        else:
                                            vsrc, vt = Vw.ap[:, kt - wlo, g * 128:(g + 1) * 128], Vw
                                        sc.op(PE, lambda e, h=h: e.matmul(bkO.ap[:, h * 128:(h + 1) * 128],
                                                                          lhsT=P.ap[:, h * 128:(h + 1) * 128], rhs=vsrc,
                                                                          start=s0, stop=lastk), reads=[P, vt], writes=[bkO])
                                        sc.op(PE, lambda e, h=h: e.matmul(bkDv.ap[:, h:h + 1],
                                                                          lhsT=P.ap[:, h * 128:(h + 1) * 128], rhs=ones1.ap[:, 0:1],
                                                                          start=s0, stop=lastk), reads=[P, ones1], writes=[bkDv])
                                if not lastk:
                                    return
                                if kind == 0:
                                    bd3 = bkDv.ap[:, 0:132].rearrange("p (h c) -> p h c", c=33)
                                    finish_branch(bkO, bkDv, bd3[:, :, 32], 0, tt, g, True, False)
                                    sc.op(DVE, lambda e: e.tensor_tensor(out=tmpi.ap[:], in0=bd3[:, :, 0:32],
                                                                         in1=sm.ap[:, 0:4].unsqueeze(2).to_broadcast([128, 4, 32]),
                                                                         op=ALU.mult), reads=[bkDv, sm], writes=[tmpi])
                                    sc.op(DVE, lambda e: e.tensor_reduce(out=sm.ap[:, 16:48],
                                                                         in_=tmpi.ap[:].rearrange("p h j -> p j h"),
                                                                         axis=AX.X, op=ALU.add), reads=[tmpi], writes=[sm])
                                    sc.op(DVE, lambda e: e.tensor_tensor(out=sm.ap[:, 16:48], in0=sm.ap[:, 16:48],
                                                                         in1=addtab.ap[:, T_, :], op=ALU.add),
                                          reads=[sm, addtab], writes=[sm])
                                    sc.op(DVE, lambda e: e.max(out=sm.ap[:, 8:16], in_=sm.ap[:, 16:48]), reads=[sm], writes=[sm])
                                    sc.op(DVE, lambda e: e.tensor_scalar(out=selb.ap[:], in0=sm.ap[:, 16:48],
                                                                         scalar1=sm.ap[:, 15:16], scalar2=None, op0=ALU.is_ge),
                                          reads=[sm], writes=[selb])
                                    sc.op(DVE, lambda e: e.tensor_scalar(out=selbb.ap[:], in0=selb.ap[:], scalar1=-NEGM,
                                                                         scalar2=NEGM, op0=ALU.mult, op1=ALU.add),
                                          reads=[selb], writes=[selbb])
                                elif kind == 2:
                                    finish_branch(bkO, bkDv, bkDv.ap[:, 0:4], 2, tt, g, False, False)
                                else:
                                    finish_branch(bkO, bkDv, bkDv.ap[:, 0:4], 1, tt, g, False, True)

                            def pre(first=first):
                                if kind == 1 and first:
                                    sc.op(PE, lambda e: e.transpose(pbf.ap[0:32, 0:128], selbb.ap[:], identb.ap[:]),
                                          reads=[selbb, identb], writes=[pbf])
                                    sc.op(ACT, lambda e: e.activation(out=selT.ap[:], in_=pbf.ap[0:32, 0:128], func=AF.Copy),
                                          reads=[pbf], writes=[selT])
                            tiles.append((pre, S, PV))

                    for tt in range(4):
                        for g in range(4):
                            T_ = blk * 4 + tt
                            add_branch(0, tt, g, T_)
                            add_branch(2, tt, g, T_)
                            add_branch(1, tt, g, T_)
                    for i_, (pre, S_, PV_) in enumerate(tiles):
                        if i_ == 0:
                            pre()
                            S_()
                        if i_ + 1 < len(tiles):
                            tiles[i_ + 1][0]()
                            tiles[i_ + 1][1]()
                        PV_()
                    for tt in range(4):
                        for h8 in range(2):
                            for j in range(8):
                                hh = h8 * 8 + j
                                sc.op(PE, lambda e, j=j, hh=hh: e.transpose(pbf.ap[:, j * 128:(j + 1) * 128],
                                                                            otm[:, tt, hh * 128:(hh + 1) * 128], identb.ap[:]),
                                      reads=C_hTc[4 * tt:4 * tt + 4] + [identb], writes=[pbf])
                            sc.op(ACT if h8 else DVE,
                                  (lambda e: e.activation(out=QT.ap[:, h8 * 8:(h8 + 1) * 8, tt * 128:(tt + 1) * 128],
                                                          in_=pbf.ap[:].rearrange("p (a b) -> p a b", a=8), func=AF.Copy)) if h8 else
                                  (lambda e: e.tensor_copy(out=QT.ap[:, h8 * 8:(h8 + 1) * 8, tt * 128:(tt + 1) * 128],
                                                           in_=pbf.ap[:].rearrange("p (a b) -> p a b", a=8))),
                                  reads=[pbf], writes=[QT])
                steps = []
                for og in range(4):
                    def fn(slot, og=og):
                        for oc in range(4):
                            c = og * 4 + oc
                            bk = nb()
                            for h in range(16):
                                sc.op(PE, lambda e, h=h, oc=oc: e.matmul(
                                    bk.ap[:], lhsT=slot.ap[:, h, oc * 128:(oc + 1) * 128], rhs=QT.ap[:, h, :],
                                    start=(h == 0), stop=(h == 15)), reads=[slot, QT], writes=[bk])
                            if oc == 0:
                                C.flush_sq()
                            C.resid_add(bk, c, modv["10"].ap[:, 32 + c, b:b + 1])
                    steps.append((panel_src(C.wo_d, 0, og * 512), fn))
                run_steps(steps, nxt=C.P_mlp[1])
                qs.close()
                if C.debug and blk == 0 and seq == 0:
                    for c in range(KC):
                        sc.dma(SP, C.dbg_d[1, :, c, :], xT[c].ap[:], reads=[xT[c]])
                if C.stop_after == "nsa0":
                    return
                C.mlp_phase(1, b, nxt=([] if (seq == NSEQ - 1 and blk == NBLK - 1) else C.P_q))
                C.norm_to_hT(None, b)
                with ExitStack() as ps:
                    ot = sb("ot", [128, 4, D], F32, ps)
                    ogf, _ = VO["gfin"]
                    for c in range(KC):
                        t_ = ntf()
                        sc.op(DVE, lambda e, c=c: e.scalar_tensor_tensor(out=t_.ap[:], in0=xT[c].ap[:],
                                                                         scalar=vecs.ap[:, ogf + c:ogf + c + 1],
                                                                         in1=C.rstd.ap[:], op0=ALU.mult, op1=ALU.mult),
                              reads=[xT[c], vecs, C.rstd], writes=[t_])
                        bk = nb()
                        for tt in range(4):
                            sc.op(PE, lambda e, tt=tt: e.transpose(bk.ap[:, tt * 128:(tt + 1) * 128],
                                                                   t_.ap[:, tt * 128:(tt + 1) * 128], C.identf.ap[:]),
                                  reads=[t_, C.identf], writes=[bk])
                        if c % 2:
                            sc.op(ACT, lambda e, c=c: e.activation(out=ot.ap[:, :, c * 128:(c + 1) * 128],
                                                                   in_=bk.ap[:].rearrange("p (a b) -> p a b", a=4), func=AF.Copy),
                                  reads=[bk], writes=[ot])
                        else:
                            sc.op(DVE, lambda e, c=c: e.tensor_copy(out=ot.ap[:, :, c * 128:(c + 1) * 128],
                                                                    in_=bk.ap[:].rearrange("p (a b) -> p a b", a=4)),
                                  reads=[bk], writes=[ot])
                    sc.dma(SP, C.out_d[seq, t0:t0 + TB, :].rearrange("(a p) f -> p a f", p=128), ot.ap[:], reads=[ot])


def host_tables():
    half = 64
    freqs = (10000.0 ** (-np.arange(half, dtype=np.float32) / half)).astype(np.float32)
    ang = np.arange(S, dtype=np.float32)[None, :] * freqs[:, None]
    cos = np.cos(ang).astype(np.float32)
    sin = np.sin(ang).astype(np.float32)
    cos2 = np.concatenate([cos, cos], 0)
    sinS = np.concatenate([-sin, sin], 0)
    return {"cos2": np.ascontiguousarray(cos2), "sinS": np.ascontiguousarray(sinS),
            "ident": np.eye(128, dtype=np.float32)}


def make_in_maps(inputs, ncores):
    f32 = lambda a: np.ascontiguousarray(np.asarray(a, np.float32))
    I = {k: np.asarray(v) for k, v in inputs.items()}
    vecs = np.zeros((128, NV), np.float32)

    def put(name, v):
        o, w = VOFF[name]
        vecs[:, o:o + w] = fm(v)
    for l in range(2):
        for s_ in range(2):
            put("mod_b%d%d" % (l, s_), I["mod_b"][l, s_])
            put("g%d%d" % (l, s_), I["norm_g"][l, s_])
    put("kv_mod_b", I["kv_mod_b"])
    put("gkv", I["kv_norm_g"])
    put("gfin", I["final_g"])
    put("b_in_u", I["a_b_in"][0, :GW])
    put("ln_g", I["a_ln_g"][0])
    put("ln_b", I["a_ln_b"][0])
    put("b_out", I["a_b_out"][0])
    shared = {
        "vecs": vecs,
        "mod_w": f32(I["mod_w"]), "kv_mod_w": f32(I["kv_mod_w"]),
        "mlp_w1": f32(I["mlp_w1"]), "mlp_w2": f32(I["mlp_w2"]),
        "a_w_in": f32(I["a_w_in"][0]), "a_w_out": f32(I["a_w_out"][0]),
        "b_in_v": f32(I["a_b_in"][0, GW:][None, :]),
        "w_sT": f32(np.transpose(I["a_w_s"][0], (2, 0, 1))),
        "b_s": f32(I["a_b_s"][0].reshape(1, 2048)),
        "w_kv": f32(I["w_kv"]),
        "cmp_posT": f32(np.stack([I["cmp_pos_k"].T, I["cmp_pos_v"].T])),
        "cmp_w1": f32(np.stack([I["cmp_w1_k"], I["cmp_w1_v"]])),
        "cmp_w2": f32(np.stack([I["cmp_w2_k"], I["cmp_w2_v"]])),
        "w_qg": f32(I["b_w_qg"][0]), "w_o": f32(I["b_w_o"][0]),
    }
    shared.update(host_tables())
    shared.update(host_masks())
    maps = []
    for i in range(ncores):
        m = dict(shared)
        m["x"] = f32(I["x"][NSEQ * i:NSEQ * (i + 1)])
        c = I["c"][NSEQ * i:NSEQ * (i + 1)]
        m["cT"] = f32(np.transpose(c.reshape(NSEQ, KC, 128), (2, 1, 0)))
        maps.append(m)
    return maps


def host_masks():
    n = np.arange(128)[:, None]
    m_cmp = np.zeros((128, 16, 128), np.float32)
    for T_ in range(16):
        t = T_ * 128 + np.arange(128)[None, :]
        m_cmp[:, T_, :] = np.where((16 * n + 31 <= t) & (n < 127), 0.0, NEGM)
    r = np.arange(128)[:, None]
    c = np.arange(128)[None, :]
    m_tri = np.zeros((128, 2, 128), np.float32)
    m_tri[:, 0, :] = np.where(r <= c, 0.0, NEGM)
    m_tri[:, 1, :] = np.where(r > c, 0.0, NEGM)
    addtab = np.zeros((128, 16, 32), np.float32)
    for T_ in range(16):
        t = T_ * 128 + np.arange(128)[:, None]
        j = np.arange(32)[None, :]
        cur = t // 64
        forced = (j == 0) | (j == cur) | (j == cur - 1)
        addtab[:, T_, :] = np.where(j * 64 <= t, np.where(forced, 1e6, 0.0), -1e30)
    emat = np.zeros((32, S), np.float32)
    emat[np.arange(S) // 64, np.arange(S)] = 1.0
    ci = np.arange(128)[:, None]
    sj = np.arange(32)[None, :]
    ovl = np.zeros((128, 33), np.float32)
    ovl[:, :32] = ((ci * 16 <= sj * 64 + 63) & (ci * 16 + 31 >= sj * 64) & (ci < 127)).astype(np.float32)
    ovl[:127, 32] = 1.0
    return {"m_cmp": m_cmp, "m_tri": m_tri, "addtab": addtab, "emat": emat, "ovl": ovl}


_NC_CACHE = {}


def kernel(**inputs):
    ncores = 8
    if "prog" not in _NC_CACHE:
        _NC_CACHE["prog"] = build_program()
    nc = _NC_CACHE["prog"]
    maps = make_in_maps(inputs, ncores)
    res = run_bass_kernel_spmd(nc, maps, core_ids=list(range(ncores)))
    out = np.concatenate([np.asarray(r["out"]) for r in res.results], axis=0)
    return out.astype(np.float32)
```

```python
import math
from contextlib import ExitStack

import numpy as np
import concourse.bass as bass
import concourse.mybir as mybir
from concourse.bass_utils import run_bass_kernel_spmd

F32 = mybir.dt.float32
BF16 = mybir.dt.bfloat16
AF = mybir.ActivationFunctionType
ALU = mybir.AluOpType
AX = mybir.AxisListType

D = 2048
S = 2048
NSEQ = 2
TB = 512
NBLK = S // TB
KC = D // 128
DFF = 8192
GW = 4096
EPS = 1e-6
NEGM = -30000.0


class Tok:
    __slots__ = ("sem", "val")

    def __init__(self, sem, val):
        self.sem = sem
        self.val = val


class T:
    def __init__(self, ap):
        self.ap = ap
        self.w = None
        self.r = {}

    def __getitem__(self, k):
        return self.ap[k]


class Eng:
    def __init__(self, nc, name, eng, stack):
        self.name = name
        self.eng = eng
        self.sem = stack.enter_context(nc.semaphore("s_" + name))
        self.cnt = 0
        self.waited = {}

    def wait(self, tok, raw):
        if tok is None:
            return
        if tok.sem is self.sem and not raw:
            return
        k = id(tok.sem)
        if self.waited.get(k, 0) >= tok.val:
            return
        self.eng.wait_ge(tok.sem, tok.val)
        self.waited[k] = tok.val


class Sched:
    NDMASEM = 16

    def __init__(self, nc, stack):
        self.nc = nc
        self.pe = Eng(nc, "pe", nc.tensor, stack)
        self.act = Eng(nc, "act", nc.scalar, stack)
        self.dve = Eng(nc, "dve", nc.vector, stack)
        self.pool = Eng(nc, "pool", nc.gpsimd, stack)
        self.sp = Eng(nc, "sp", nc.sync, stack)
        self.engs = [self.pe, self.act, self.dve, self.pool, self.sp]
        self.dsems = {id(q): [stack.enter_context(nc.semaphore("dma%s%d" % (q.name, i))) for i in range(self.NDMASEM)]
                      for q in (self.sp, self.pool)}
        self.dn = {id(q): 0 for q in (self.sp, self.pool)}
        self.dma_toks = {}

    def _deps(self, E, reads, writes):
        for t in reads:
            E.wait(t.w, True)
        for t in writes:
            E.wait(t.w, False)
            for tok in t.r.values():
                E.wait(tok, False)

    def _upd(self, tok, reads, writes):
        for t in writes:
            t.w = tok
            t.r = {}
        for t in reads:
            t.r[id(tok.sem)] = tok

    def op(self, E, inst_fn, reads=(), writes=()):
        self._deps(E, reads, writes)
        inst = inst_fn(E.eng)
        E.cnt += 1
        inst.then_inc(E.sem, 1)
        self._upd(Tok(E.sem, E.cnt), reads, writes)

    def dma(self, Q, out_ap, in_ap, reads=(), writes=()):
        n = self.dn[id(Q)]
        slot = n % self.NDMASEM
        sem = self.dsems[id(Q)][slot]
        val = 16 * (n // self.NDMASEM + 1)
        self.dn[id(Q)] = n + 1
        if val > 16:
            Q.wait(Tok(sem, val - 16), True)
        self._deps(Q, reads, writes)
        Q.eng.dma_start(out=out_ap, in_=in_ap).then_inc(sem, 16)
        tok = Tok(sem, val)
        self.dma_toks[(id(Q), slot)] = tok
        self._upd(tok, reads, writes)

    def barrier(self):
        toks = [Tok(e.sem, e.cnt) for e in self.engs if e.cnt > 0] + list(self.dma_toks.values())
        for e in self.engs:
            for tok in toks:
                e.wait(tok, True)

    def finish(self):
        for tok in self.dma_toks.values():
            self.sp.wait(tok, True)


VOFF = {}
_o = 0
for _n, _w in [("mod_b00", 48), ("mod_b01", 48), ("mod_b10", 48), ("mod_b11", 48), ("kv_mod_b", 32),
               ("g00", 16), ("g01", 16), ("g10", 16), ("g11", 16), ("gkv", 16), ("gfin", 16),
               ("b_in_u", 32), ("ln_g", 32), ("ln_b", 32), ("b_out", 16)]:
    VOFF[_n] = (_o, _w)
    _o += _w
NV = _o


def fm(v):
    v = np.asarray(v, np.float32)
    return np.ascontiguousarray(v.reshape(-1, 128).T)


class Ctx:
    pass


def build_program(debug=False, stop_after=None):
    nc = bass.Bass("TRN2", target_bir_lowering=False)
    C = Ctx()
    C.nc = nc

    def din(name, shape, dt=F32):
        return nc.dram_tensor(name, list(shape), dt, kind="ExternalInput").ap()

    dbgkind = "ExternalOutput" if debug else "Internal"
    x_d = din("x", [NSEQ, S, D])
    cT_d = din("cT", [128, KC, NSEQ])
    vecs_d = din("vecs", [128, NV])
    modw_d = din("mod_w", [2, 2, D, 3 * D])
    kvmodw_d = din("kv_mod_w", [D, 2 * D])
    w1_d = din("mlp_w1", [2, D, DFF])
    w2_d = din("mlp_w2", [2, DFF, D])
    win_d = din("a_w_in", [D, 2 * GW])
    wout_d = din("a_w_out", [GW, D])
    binv_d = din("b_in_v", [1, GW])
    wsT_d = din("w_sT", [128, 16, 128])
    bs_d = din("b_s", [1, 2048])
    wkv_d = din("w_kv", [D, 3072])
    cpos_d = din("cmp_posT", [2, 128, 32])
    cw1_d = din("cmp_w1", [2, 4096, 256])
    cw2_d = din("cmp_w2", [2, 256, 128])
    wqg_d = din("w_qg", [D, 2096])
    wo_d = din("w_o", [D, D])
    cos_d = din("cos2", [128, S])
    sin_d = din("sinS", [128, S])
    ident_d = din("ident", [128, 128])
    mcmp_d = din("m_cmp", [128, 16, 128])
    mtri_d = din("m_tri", [128, 2, 128])
    addtab_d = din("addtab", [128, 16, 32])
    emat_d = din("emat", [32, S])
    ovl_d = din("ovl", [128, 33])
    out_d = nc.dram_tensor("out", [NSEQ, S, D], F32, kind="ExternalOutput").ap()

    x2T_d = nc.dram_tensor("x2T", [NSEQ * NBLK, 128, KC, TB], F32, kind=dbgkind).ap()
    kT_d = nc.dram_tensor("kT", [4, NSEQ, 128, 4, S], BF16, kind=dbgkind).ap()
    vtm_d = nc.dram_tensor("vtm", [2, NSEQ, S, 512], BF16, kind=dbgkind).ap()
    dbg_d = nc.dram_tensor("dbg", [8, 128, KC, TB], F32, kind=dbgkind).ap()
    dbgc_d = nc.dram_tensor("dbgc", [NSEQ, 2, 128, 4, 128], BF16, kind=dbgkind).ap()

    with ExitStack() as st:
        sc = Sched(nc, st)
        C.sc = sc
        PE, ACT, DVE, POOL, SP = sc.pe, sc.act, sc.dve, sc.pool, sc.sp

        sbn = [0]
        grave = {}

        def retire(t):
            toks = list(t.r.values())
            if t.w is not None:
                toks.append(t.w)
            for tok in toks:
                k = id(tok.sem)
                if k not in grave or grave[k].val < tok.val:
                    grave[k] = tok

        def sb(name, shape, dt, stack=st):
            sbn[0] += 1
            t = T(stack.enter_context(nc.sbuf_tensor("sb%d_%s" % (sbn[0], name), list(shape), dt)))
            t.r = dict(grave)
            stack.callback(retire, t)
            return t

        banks = [T(st.enter_context(nc.psum_tensor("bank%d" % i, [128, 512], F32))) for i in range(7)]
        pbf = T(st.enter_context(nc.psum_tensor("pbf", [128, 1024], BF16)))
        bank_i = [0]

        def nb():
            b = banks[bank_i[0] % 6]
            bank_i[0] += 1
            return b

        xT = [sb("xT%d" % c, [128, TB], F32) for c in range(KC)]
        hT = sb("hT", [128, KC, TB], BF16)
        hTc = [T(hT.ap[:, c, :]) for c in range(KC)]
        slots = [sb("slot%d" % i, [128, 16, 512], BF16) for i in range(3)]
        slot_i = [0]
        vecs = sb("vecs", [128, NV], F32)
        identf = sb("identf", [128, 128], F32)
        identb = sb("identb", [128, 128], BF16)
        onesD = sb("onesD", [128, 128], BF16)
        ones1 = sb("ones1", [128, 128], BF16)
        condT = sb("condT", [128, KC, NSEQ], BF16)
        cTs = sb("cTs", [128, KC, NSEQ], F32)
        modv = {n: sb("modv_" + n, [128, w, NSEQ], F32) for n, w in
                [("00", 48), ("01", 48), ("10", 48), ("11", 48), ("kv", 32)]}
        Avec = {n: sb("A_" + n, [128, KC, NSEQ], F32) for n in ["00", "01", "10", "11", "kv"]}
        gbo = sb("gbo", [128, KC, NSEQ], F32)
        rstd = sb("rstd", [128, TB], F32)
        epsc = sb("epsc", [128, 1], F32)
        sqt = [sb("sqt%d" % i, [128, TB], BF16) for i in range(2)]
        tmpf = [sb("tmpf%d" % i, [128, TB], F32) for i in range(3)]
        KcT = [sb("KcT%d" % s_, [128, 4, 128], BF16) for s_ in range(NSEQ)]
        Vc = [sb("Vc%d" % s_, [128, 4, 128], BF16) for s_ in range(NSEQ)]
        rr = {"sq": 0, "tf": 0}

        def nsq():
            rr["sq"] += 1
            return sqt[rr["sq"] % 2]

        def ntf():
            rr["tf"] += 1
            return tmpf[rr["tf"] % 3]

        def vcol(name, c=None):
            o, w = VOFF[name]
            if c is None:
                return vecs.ap[:, o:o + w]
            return vecs.ap[:, o + c:o + c + 1]

        def panel_src(w2d, r0, c0):
            key = (str(w2d), r0, c0)
            return (key, w2d[r0:r0 + 2048, c0:c0 + 512].rearrange("(kc p) n -> p kc n", p=128))

        pref = []

        def load_panel(p):
            s_ = slots[slot_i[0] % 3]
            slot_i[0] += 1
            sc.dma(POOL, s_.ap[:], p[1], writes=[s_])
            return s_

        def run_steps(steps, nxt=()):
            n = len(steps)
            assigned = {}
            pidx = [i for i in range(n) if steps[i][0] is not None]
            pos = {i: j for j, i in enumerate(pidx)}

            def issue(j):
                if j < len(pidx):
                    i = pidx[j]
                    if pref:
                        key, s_ = pref.pop(0)
                        assert key == steps[i][0][0], (key, steps[i][0][0])
                        assigned[i] = s_
                    else:
                        assigned[i] = load_panel(steps[i][0])
            issue(0)
            issue(1)
            for i in range(n):
                if i in pos:
                    issue(pos[i] + 2)
                steps[i][1](assigned.get(i))
            assert not pref
            for p in nxt:
                pref.append((p[0], load_panel(p)))

        P_gmlp = [panel_src(win_d, 0, GW), panel_src(win_d, 0, GW + 512)]
        P_mlp = [[panel_src(w1_d[l], 0, 0), panel_src(w1_d[l], 0, 512)] for l in range(2)]
        P_kv = [panel_src(wkv_d, 0, 0), panel_src(wkv_d, 0, 512)]
        P_q = [panel_src(wqg_d, 0, 0), panel_src(wqg_d, 0, 512)]
        P_wo = [panel_src(wo_d, 0, 0), panel_src(wo_d, 0, 512)]

        sc.dma(SP, vecs.ap[:], vecs_d, writes=[vecs])
        sc.dma(SP, identf.ap[:], ident_d, writes=[identf])
        sc.dma(SP, cTs.ap[:], cT_d, writes=[cTs])
        sc.op(DVE, lambda e: e.tensor_copy(out=identb.ap[:], in_=identf.ap[:]), reads=[identf], writes=[identb])
        sc.op(DVE, lambda e: e.memset(onesD.ap[:], 1.0 / D), writes=[onesD])
        sc.op(DVE, lambda e: e.memset(ones1.ap[:], 1.0), writes=[ones1])
        sc.op(DVE, lambda e: e.memset(epsc.ap[:], EPS), writes=[epsc])
        sc.op(ACT, lambda e: e.activation(out=condT.ap[:], in_=cTs.ap[:], func=AF.Silu), reads=[cTs], writes=[condT])

        def mod_steps(name, w2d, ncols, bname):
            steps = []
            for p in range(ncols // 512):
                def fn(slot, p=p):
                    bk = nb()
                    for fc in range(4):
                        for k in range(KC):
                            sc.op(PE, lambda e, fc=fc, k=k: e.matmul(
                                bk.ap[:, fc * NSEQ:(fc + 1) * NSEQ], lhsT=slot.ap[:, k, fc * 128:(fc + 1) * 128],
                                rhs=condT.ap[:, k, :], start=(k == 0), stop=(k == KC - 1)),
                                reads=[slot, condT], writes=[bk])
                    o, _ = VOFF[bname]
                    sc.op(DVE, lambda e: e.tensor_tensor(
                        out=modv[name].ap[:, p * 4:(p + 1) * 4, :],
                        in0=bk.ap[:, 0:4 * NSEQ].rearrange("p (a b) -> p a b", b=NSEQ),
                        in1=vecs.ap[:, o + p * 4:o + (p + 1) * 4].unsqueeze(2).to_broadcast([128, 4, NSEQ]),
                        op=ALU.add), reads=[bk, vecs], writes=[modv[name]])
                steps.append((panel_src(w2d, 0, p * 512), fn))
            return steps

        def derive(name, gname, has_gate_bias=False):
            mv = modv[name]
            A = Avec[name]
            o, _ = VOFF[gname]
            sc.op(DVE, lambda e: e.tensor_scalar(out=A.ap[:], in0=mv.ap[:, 16:32, :], scalar1=1.0, scalar2=None,
                                                 op0=ALU.add), reads=[mv], writes=[A])
            sc.op(DVE, lambda e: e.tensor_tensor(out=A.ap[:], in0=A.ap[:],
                                                 in1=vecs.ap[:, o:o + 16].unsqueeze(2).to_broadcast([128, 16, NSEQ]),
                                                 op=ALU.mult), reads=[A, vecs], writes=[A])

        steps = []
        steps += mod_steps("00", modw_d[0, 0], 3 * D, "mod_b00")
        steps += mod_steps("01", modw_d[0, 1], 3 * D, "mod_b01")
        steps += mod_steps("kv", kvmodw_d, 2 * D, "kv_mod_b")
        steps += mod_steps("10", modw_d[1, 0], 3 * D, "mod_b10")
        steps += mod_steps("11", modw_d[1, 1], 3 * D, "mod_b11")
        run_steps(steps, nxt=P_gmlp)
        derive("00", "g00")
        derive("01", "g01")
        derive("kv", "gkv")
        derive("10", "g10")
        derive("11", "g11")
        ob, _ = VOFF["b_out"]
        sc.op(DVE, lambda e: e.tensor_tensor(out=gbo.ap[:], in0=modv["00"].ap[:, 32:48, :],
                                             in1=vecs.ap[:, ob:ob + 16].unsqueeze(2).to_broadcast([128, 16, NSEQ]),
                                             op=ALU.mult), reads=[modv["00"], vecs], writes=[gbo])

        sqst = {"done": 0, "pend": []}

        def flush_sq():
            bk = banks[6]
            for c in sqst["pend"]:
                assert c == sqst["done"]
                q = nsq()
                sc.op(ACT, lambda e, c=c, q=q: e.activation(out=q.ap[:], in_=xT[c].ap[:], func=AF.Square),
                      reads=[xT[c]], writes=[q])
                sc.op(PE, lambda e, c=c, q=q: e.matmul(bk.ap[:], lhsT=onesD.ap[:], rhs=q.ap[:],
                                                       start=(c == 0), stop=(c == KC - 1)),
                      reads=[q, onesD], writes=[bk])
                sqst["done"] += 1
            sqst["pend"] = []

        def x_final(c):
            sqst["pend"].append(c)

        def norm_to_hT(name, b):
            bk = banks[6]
            flush_sq()
            for c in range(sqst["done"], KC):
                sqst["pend"].append(c)
            flush_sq()
            assert sqst["done"] == KC
            sqst["done"] = 0
            sc.op(ACT, lambda e: e.activation(out=rstd.ap[:], in_=bk.ap[:], func=AF.Sqrt, bias=epsc.ap[:, 0:1]),
                  reads=[bk, epsc], writes=[rstd])
            sc.op(DVE, lambda e: e.reciprocal(out=rstd.ap[:], in_=rstd.ap[:]), reads=[rstd], writes=[rstd])
            if name is None:
                return
            A = Avec[name]
            mv = modv[name]
            for c in range(KC):
                t_ = ntf()
                sc.op(DVE, lambda e, c=c, t_=t_: e.tensor_tensor(out=t_.ap[:], in0=xT[c].ap[:], in1=rstd.ap[:],
                                                                 op=ALU.mult), reads=[xT[c], rstd], writes=[t_])
                sc.op(ACT, lambda e, c=c, t_=t_: e.activation(out=hT.ap[:, c, :], in_=t_.ap[:], func=AF.Identity,
                                                              bias=mv.ap[:, c, b:b + 1], scale=A.ap[:, c, b:b + 1]),
                      reads=[t_, A, mv], writes=[hTc[c]])

        def resid_add(bk, c, gate_ap, bias_ap=None):
            t_ = ntf()
            if bias_ap is None:
                sc.op(ACT, lambda e: e.activation(out=t_.ap[:], in_=bk.ap[:], func=AF.Copy, scale=gate_ap),
                      reads=[bk], writes=[t_])
            else:
                sc.op(ACT, lambda e: e.activation(out=t_.ap[:], in_=bk.ap[:], func=AF.Identity, scale=gate_ap,
                                                  bias=bias_ap), reads=[bk], writes=[t_])
            sc.op(DVE, lambda e: e.tensor_tensor(out=xT[c].ap[:], in0=xT[c].ap[:], in1=t_.ap[:], op=ALU.add),
                  reads=[xT[c], t_], writes=[xT[c]])
            x_final(c)

        def mlp_phase(l, b, nxt=()):
            name = "%d1" % l
            norm_to_hT(name, b)
            with ExitStack() as ps:
                hid = [sb("hid%d" % i, [128, TB], BF16, ps) for i in range(64)]
                steps = []
                for fp in range(16):
                    def fn(slot, fp=fp):
                        for j in range(4):
                            fc = fp * 4 + j
                            bk = nb()
                            for k in range(KC):
                                sc.op(PE, lambda e, k=k, j=j: e.matmul(
                                    bk.ap[:], lhsT=slot.ap[:, k, j * 128:(j + 1) * 128], rhs=hT.ap[:, k, :],
                                    start=(k == 0), stop=(k == KC - 1)), reads=[slot, hTc[k]], writes=[bk])
                            t_ = ntf()
                            sc.op(ACT, lambda e: e.activation(out=t_.ap[:], in_=bk.ap[:], func=AF.Relu),
                                  reads=[bk], writes=[t_])
                            sc.op(DVE, lambda e, fc=fc: e.tensor_tensor(out=hid[fc].ap[:], in0=t_.ap[:], in1=t_.ap[:],
                                                                        op=ALU.mult), reads=[t_], writes=[hid[fc]])
                    steps.append((panel_src(w1_d[l], 0, fp * 512), fn))
                acc = {}
                for og in range(4):
                    for fp in range(4):
                        def fn(slot, og=og, fp=fp):
                            if fp == 0:
                                acc[og] = [nb() for _ in range(4)]
                            for oc in range(4):
                                bk = acc[og][oc]
                                for k in range(16):
                                    sc.op(PE, lambda e, k=k, oc=oc: e.matmul(
                                        bk.ap[:], lhsT=slot.ap[:, k, oc * 128:(oc + 1) * 128],
                                        rhs=hid[fp * 16 + k].ap[:], start=(fp == 0 and k == 0),
                                        stop=(fp == 3 and k == 15)), reads=[slot, hid[fp * 16 + k]], writes=[bk])
                            if fp == 1:
                                flush_sq()
                            if fp == 3:
                                for oc in range(4):
                                    c = og * 4 + oc
                                    resid_add(acc[og][oc], c, modv[name].ap[:, 32 + c, b:b + 1])
                        steps.append((panel_src(w2_d[l], fp * 2048, og * 512), fn))
                run_steps(steps, nxt=nxt)

        C.__dict__.update(locals())
        passA(C)
        if stop_after != "A":
            passB(C)
        sc.barrier()
        sc.finish()
    return nc


def rope_from_bank(C, bk, out_t, out_ap, extra_reads=()):
    sc, ACT, DVE = C.sc, C.sc.act, C.sc.dve
    xs = C.ntf()
    sw = C.ntf()
    sc.op(ACT, lambda e: e.activation(out=xs.ap[:], in_=bk.ap[:], func=AF.Copy), reads=[bk], writes=[xs])
    sc.op(DVE, lambda e: e.tensor_copy(out=sw.ap[0:64, :], in_=xs.ap[64:128, :]), reads=[xs], writes=[sw])
    sc.op(DVE, lambda e: e.tensor_copy(out=sw.ap[64:128, :], in_=xs.ap[0:64, :]), reads=[xs], writes=[sw])
    sc.op(DVE, lambda e: e.tensor_tensor(out=xs.ap[:], in0=xs.ap[:], in1=C.cosb.ap[:], op=ALU.mult),
          reads=[xs, C.cosb, sw], writes=[xs])
    sc.op(DVE, lambda e: e.tensor_tensor(out=sw.ap[:], in0=sw.ap[:], in1=C.sinb.ap[:], op=ALU.mult),
          reads=[sw, C.sinb], writes=[sw])
    sc.op(DVE, lambda e: e.tensor_tensor(out=out_ap, in0=xs.ap[:], in1=sw.ap[:], op=ALU.add),
          reads=[xs, sw] + list(extra_reads), writes=[out_t])


def passA(C):
    nc, sc = C.nc, C.sc
    PE, ACT, DVE, POOL, SP = sc.pe, sc.act, sc.dve, sc.pool, sc.sp
    nb, ntf, nsq, xT, hT, vecs, VO = C.nb, C.ntf, C.nsq, C.xT, C.hT, C.vecs, VOFF
    sb, run_steps, panel_src = C.sb, C.run_steps, C.panel_src
    modv, Avec = C.modv, C.Avec
    C_hTc = C.hTc
    with ExitStack() as pa:
        WcT = sb("WcT", [128, 16, 128], BF16, pa)
        BiasT = sb("BiasT", [128, 32, 128], F32, pa)
        binv = sb("binv", [128, 1536], BF16, pa)
        s1 = sb("s1", [128, 32], F32, pa)
        s2 = sb("s2", [128, 32], F32, pa)
        st4 = sb("st4", [128, 4, 4], F32, pa)
        junk = sb("junk", [128, TB], BF16, pa)
        ko_i = [0]
        with ExitStack() as ps:
            rs_rep = sb("rs_rep", [128, 16, 128], F32, ps)
            bs_rep = sb("bs_rep", [128, 16, 128], F32, ps)
            sc.dma(POOL, WcT.ap[:], C.wsT_d, writes=[WcT])
            for vp in range(8):
                sc.dma(POOL, binv.ap[32 * (vp % 3):32 * (vp % 3) + 1, (vp // 3) * 512:(vp // 3 + 1) * 512],
                       C.binv_d[0:1, vp * 512:(vp + 1) * 512], writes=[binv])
            sc.dma(SP, bs_rep.ap[:].rearrange("p g t -> p (g t)"), C.bs_d.partition_broadcast(128), writes=[bs_rep])
            sc.op(POOL, lambda e: e.affine_select(out=WcT.ap[:], in_=WcT.ap[:], pattern=[[0, 16], [1, 128]],
                                                  compare_op=ALU.is_ge, fill=0.0, base=0, channel_multiplier=-1),
                  reads=[WcT], writes=[WcT])
            for q in range(4):
                bk = nb()
                sc.op(PE, lambda e, q=q: e.matmul(bk.ap[:], lhsT=C.ones1.ap[:],
                                                  rhs=WcT.ap[:, q * 4:(q + 1) * 4, :], start=True, stop=True),
                      reads=[WcT, C.ones1], writes=[bk])
                sc.op(ACT, lambda e, q=q: e.activation(out=rs_rep.ap[:, q * 4:(q + 1) * 4, :].rearrange("p a b -> p (a b)"),
                                                       in_=bk.ap[:], func=AF.Copy), reads=[bk], writes=[rs_rep])
            olb, _ = VO["ln_b"]
            for fc in range(32):
                g = fc // 2
                sc.op(DVE, lambda e, fc=fc, g=g: e.scalar_tensor_tensor(
                    out=BiasT.ap[:, fc, :], in0=rs_rep.ap[:, g, :], scalar=vecs.ap[:, olb + fc:olb + fc + 1],
                    in1=bs_rep.ap[:, g, :], op0=ALU.mult, op1=ALU.add), reads=[rs_rep, bs_rep, vecs], writes=[BiasT])

        olg, _ = VO["ln_g"]
        obu, _ = VO["b_in_u"]
        for seq in range(NSEQ):
            for blk in range(NBLK):
                t0 = blk * TB
                b = seq
                with ExitStack() as ps:
                    xtm = [sb("xtm%d" % i, [128, D], F32, ps) for i in range(4)]
                    for tt in range(4):
                        sc.dma(SP, xtm[tt].ap[:], C.x_d[seq, t0 + tt * 128:t0 + (tt + 1) * 128, :], writes=[xtm[tt]])
                    for c in range(KC):
                        bk = nb()
                        for tt in range(4):
                            sc.op(PE, lambda e, tt=tt, c=c: e.transpose(bk.ap[:, tt * 128:(tt + 1) * 128],
                                                                        xtm[tt].ap[:, c * 128:(c + 1) * 128],
                                                                        C.identf.ap[:]),
                                  reads=[xtm[tt], C.identf], writes=[bk])
                        if c >= 2:
                            C.flush_sq()
                        if c >= 1:
                            C.x_final(c - 1)
                        if c % 2:
                            sc.op(ACT, lambda e, c=c: e.activation(out=xT[c].ap[:], in_=bk.ap[:], func=AF.Copy),
                                  reads=[bk], writes=[xT[c]])
                        else:
                            sc.op(DVE, lambda e, c=c: e.tensor_copy(out=xT[c].ap[:], in_=bk.ap[:]),
                                  reads=[bk], writes=[xT[c]])
                    C.x_final(KC - 1)

                C.norm_to_hT("00", b)
                with ExitStack() as ps:
                    sT = [sb("sT%d" % i, [128, TB], BF16, ps) for i in range(32)]
                    v = [sb("v%d" % i, [128, GW], BF16, ps) for i in range(4)]
                    sc.op(DVE, lambda e: e.memset(s1.ap[:], 0.0), writes=[s1])
                    sc.op(DVE, lambda e: e.memset(s2.ap[:], 0.0), writes=[s2])
                    steps = []
                    for vp in range(8):
                        def fn(slot, vp=vp):
                            for tt in range(4):
                                bk = nb()
                                for k in range(KC):
                                    sc.op(PE, lambda e, k=k, tt=tt: e.matmul(
                                        bk.ap[:], lhsT=hT.ap[:, k, tt * 128:(tt + 1) * 128], rhs=slot.ap[:, k, :],
                                        start=(k == 0), stop=False), reads=[slot, C_hTc[k]], writes=[bk])
                                pb = 32 * (vp % 3)
                                sc.op(PE, lambda e: e.matmul(bk.ap[:], lhsT=C.ones1.ap[pb:pb + 1, :],
                                                             rhs=binv.ap[pb:pb + 1, (vp // 3) * 512:(vp // 3 + 1) * 512],
                                                             start=False, stop=True),
                                      reads=[C.ones1, binv], writes=[bk])
                                col = tt * 8 + vp
                                sc.op(ACT, lambda e, tt=tt, col=col: e.activation(
                                    out=v[tt].ap[:, vp * 512:(vp + 1) * 512], in_=bk.ap[:], func=AF.Gelu,
                                    accum_out=s1.ap[:, col:col + 1]), reads=[bk], writes=[v[tt], s1])
                                sc.op(ACT, lambda e, tt=tt, col=col: e.activation(
                                    out=junk.ap[:], in_=v[tt].ap[:, vp * 512:(vp + 1) * 512], func=AF.Square,
                                    accum_out=s2.ap[:, col:col + 1]), reads=[v[tt]], writes=[junk, s2])
                        steps.append((panel_src(C.win_d, 0, GW + vp * 512), fn))

                    def ln_and_spatial(slot):
                        sc.op(DVE, lambda e: e.tensor_reduce(out=st4.ap[:, 0, :], in_=s1.ap[:].rearrange("p (a b) -> p a b", b=8),
                                                             axis=AX.X, op=ALU.add), reads=[s1], writes=[st4])
                        sc.op(DVE, lambda e: e.tensor_reduce(out=st4.ap[:, 3, :], in_=s2.ap[:].rearrange("p (a b) -> p a b", b=8),
                                                             axis=AX.X, op=ALU.add), reads=[s2, st4], writes=[st4])
                        sc.op(DVE, lambda e: e.tensor_scalar(out=st4.ap[:, 0, :], in0=st4.ap[:, 0, :], scalar1=1.0 / GW,
                                                             scalar2=None, op0=ALU.mult), reads=[st4], writes=[st4])
                        sc.op(DVE, lambda e: e.tensor_tensor(out=st4.ap[:, 1, :], in0=st4.ap[:, 0, :], in1=st4.ap[:, 0, :],
                                                             op=ALU.mult), reads=[st4], writes=[st4])
                        sc.op(DVE, lambda e: e.scalar_tensor_tensor(out=st4.ap[:, 1, :], in0=st4.ap[:, 3, :], scalar=1.0 / GW,
                                                                    in1=st4.ap[:, 1, :], op0=ALU.mult, op1=ALU.subtract),
                              reads=[st4], writes=[st4])
                        sc.op(ACT, lambda e: e.activation(out=st4.ap[:, 2, :], in_=st4.ap[:, 1, :], func=AF.Sqrt,
                                                          bias=C.epsc.ap[:, 0:1]), reads=[st4, C.epsc], writes=[st4])
                        sc.op(DVE, lambda e: e.reciprocal(out=st4.ap[:, 2, :], in_=st4.ap[:, 2, :]), reads=[st4], writes=[st4])
                        for tt in range(4):
                            sc.op(DVE, lambda e, tt=tt: e.tensor_scalar(
                                out=v[tt].ap[:], in0=v[tt].ap[:], scalar1=st4.ap[:, 0, tt:tt + 1],
                                scalar2=st4.ap[:, 2, tt:tt + 1], op0=ALU.subtract, op1=ALU.mult),
                                reads=[v[tt], st4], writes=[v[tt]])
                        for tt in range(4):
                            for f4 in range(8):
                                bk = nb()
                                for j in range(4):
                                    fc = f4 * 4 + j
                                    sc.op(PE, lambda e, j=j, fc=fc, tt=tt: e.matmul(
                                        bk.ap[:, j * 128:(j + 1) * 128], lhsT=v[tt].ap[:, fc * 128:(fc + 1) * 128],
                                        rhs=WcT.ap[:, fc // 2, :], start=True, stop=True),
                                        reads=[v[tt], WcT], writes=[bk])
                                for j in range(4):
                                    fc = f4 * 4 + j
                                    sc.op(DVE, lambda e, j=j, fc=fc, tt=tt: e.scalar_tensor_tensor(
                                        out=sT[fc].ap[:, tt * 128:(tt + 1) * 128], in0=bk.ap[:, j * 128:(j + 1) * 128],
                                        scalar=vecs.ap[:, olg + fc:olg + fc + 1], in1=BiasT.ap[:, fc, :],
                                        op0=ALU.mult, op1=ALU.add), reads=[bk, vecs, BiasT], writes=[sT[fc]])
                    steps.append((None, ln_and_spatial))
                    for up in range(8):
                        def fn(slot, up=up):
                            for j in range(4):
                                fc = up * 4 + j
                                bk = nb()
                                for k in range(KC):
                                    sc.op(PE, lambda e, k=k, j=j: e.matmul(
                                        bk.ap[:], lhsT=slot.ap[:, k, j * 128:(j + 1) * 128], rhs=hT.ap[:, k, :],
                                        start=(k == 0), stop=(k == KC - 1)), reads=[slot, C_hTc[k]], writes=[bk])
                                t_ = ntf()
                                sc.op(ACT, lambda e, fc=fc: e.activation(out=t_.ap[:], in_=bk.ap[:], func=AF.Gelu,
                                                                         bias=vecs.ap[:, obu + fc:obu + fc + 1]),
                                      reads=[bk, vecs], writes=[t_])
                                sc.op(DVE, lambda e, fc=fc: e.tensor_tensor(out=sT[fc].ap[:], in0=t_.ap[:], in1=sT[fc].ap[:],
                                                                            op=ALU.mult), reads=[t_, sT[fc]], writes=[sT[fc]])
                        steps.append((panel_src(C.win_d, 0, up * 512), fn))
                    acc = {}
                    for og in range(4):
                        for fp in range(2):
                            def fn(slot, og=og, fp=fp):
                                if fp == 0:
                                    acc[og] = [nb() for _ in range(4)]
                                for oc in range(4):
                                    bk = acc[og][oc]
                                    for k in range(16):
                                        sc.op(PE, lambda e, k=k, oc=oc: e.matmul(
                                            bk.ap[:], lhsT=slot.ap[:, k, oc * 128:(oc + 1) * 128],
                                            rhs=sT[fp * 16 + k].ap[:], start=(fp == 0 and k == 0),
                                            stop=(fp == 1 and k == 15)), reads=[slot, sT[fp * 16 + k]], writes=[bk])
                                if fp == 1:
                                    C.flush_sq()
                                if fp == 1:
                                    for oc in range(4):
                                        c = og * 4 + oc
                                        C.resid_add(acc[og][oc], c, modv["00"].ap[:, 32 + c, b:b + 1],
                                                    C.gbo.ap[:, c, b:b + 1])
                            steps.append((panel_src(C.wout_d, fp * 2048, og * 512), fn))
                    run_steps(steps, nxt=C.P_mlp[0])
                if C.debug and seq == 0 and blk == 0:
                    for c in range(KC):
                        sc.dma(SP, C.dbg_d[0, :, c, :], xT[c].ap[:], reads=[xT[c]])
                if C.stop_after == "gmlp":
                    return

                C.mlp_phase(0, b, nxt=C.P_kv)

                for c in range(KC):
                    sc.dma(SP, C.x2T_d[seq * NBLK + blk, :, c, :], xT[c].ap[:], reads=[xT[c]])
                C.norm_to_hT("kv", b)
                kvs = ExitStack()
                kout = [sb("kout%d" % i, [128, TB], BF16, kvs) for i in range(4)]
                C.cosb = sb("cosb", [128, TB], F32, kvs)
                C.sinb = sb("sinb", [128, TB], F32, kvs)
                sc.dma(SP, C.cosb.ap[:], C.cos_d[:, t0:t0 + TB], writes=[C.cosb])
                sc.dma(SP, C.sinb.ap[:], C.sin_d[:, t0:t0 + TB], writes=[C.sinb])
                steps = []
                for br in range(6):
                    def fn(slot, br=br):
                        if br in (0, 1, 2, 4):
                            ki = {0: 0, 1: 1, 2: 2, 4: 3}[br]
                            for g in range(4):
                                bk = nb()
                                for k in range(KC):
                                    sc.op(PE, lambda e, k=k, g=g: e.matmul(
                                        bk.ap[:], lhsT=slot.ap[:, k, g * 128:(g + 1) * 128], rhs=hT.ap[:, k, :],
                                        start=(k == 0), stop=(k == KC - 1)), reads=[slot, C_hTc[k]], writes=[bk])
                                ko = kout[ko_i[0] % 4]
                                ko_i[0] += 1
                                if br == 1:
                                    sc.op(ACT, lambda e: e.activation(out=ko.ap[:], in_=bk.ap[:], func=AF.Copy),
                                          reads=[bk], writes=[ko])
                                else:
                                    rope_from_bank(C, bk, ko, ko.ap[:])
                                sc.dma(SP, C.kT_d[ki, seq, :, g, t0:t0 + TB], ko.ap[:], reads=[ko])
                        else:
                            vi = {3: 0, 5: 1}[br]
                            for tt in range(4):
                                bk = nb()
                                for k in range(KC):
                                    sc.op(PE, lambda e, k=k, tt=tt: e.matmul(
                                        bk.ap[:], lhsT=hT.ap[:, k, tt * 128:(tt + 1) * 128], rhs=slot.ap[:, k, :],
                                        start=(k == 0), stop=(k == KC - 1)), reads=[slot, C_hTc[k]], writes=[bk])
                                ko = kout[ko_i[0] % 4]
                                ko_i[0] += 1
                                sc.op(ACT, lambda e: e.activation(out=ko.ap[:], in_=bk.ap[:], func=AF.Copy),
                                      reads=[bk], writes=[ko])
                                sc.dma(SP, C.vtm_d[vi, seq, t0 + tt * 128:t0 + (tt + 1) * 128, :], ko.ap[:], reads=[ko])
                    steps.append((panel_src(C.wkv_d, 0, br * 512), fn))
                run_steps(steps, nxt=(C.P_q if (seq == NSEQ - 1 and blk == NBLK - 1) else C.P_gmlp))
                kvs.close()
                if C.stop_after == "blk0":
                    return
            sc.barrier()
            compress_seq(C, seq)


def compress_seq(C, seq):
    nc, sc = C.nc, C.sc
    PE, ACT, DVE, POOL, SP = sc.pe, sc.act, sc.dve, sc.pool, sc.sp
    nb, sb = C.nb, C.sb
    with ExitStack() as ps:
        w1s = sb("cw1s", [128, 32, 256], BF16, ps)
        w2s = sb("cw2s", [128, 2, 128], BF16, ps)
        posT = sb("cposT", [128, 32], BF16, ps)
        cvec = sb("ccvec", [128, 2], F32, ps)
        src = sb("csrc", [128, S], BF16, ps)
        zde = sb("czde", [128, 16, 128], BF16, ps)
        hidc = sb("chid", [128, 2, 128], BF16, ps)
        for kv in range(2):
            sc.dma(POOL, w1s.ap[:], C.cw1_d[kv].rearrange("(r p) j -> p r j", p=128), writes=[w1s])
            sc.dma(POOL, w2s.ap[:], C.cw2_d[kv].rearrange("(a p) d -> p a d", p=128), writes=[w2s])
            sc.dma(POOL, posT.ap[:], C.cpos_d[kv], writes=[posT])
            bk = nb()
            for jc in range(2):
                for r in range(32):
                    sc.op(PE, lambda e, jc=jc, r=r: e.matmul(bk.ap[:, jc:jc + 1], lhsT=w1s.ap[:, r, jc * 128:(jc + 1) * 128],
                                                             rhs=posT.ap[:, r:r + 1], start=(r == 0), stop=(r == 31)),
                          reads=[w1s, posT], writes=[bk])
            sc.op(DVE, lambda e: e.tensor_copy(out=cvec.ap[:], in_=bk.ap[:, 0:2]), reads=[bk], writes=[cvec])
            for g in range(4):
                sc.dma(SP, src.ap[:], C.kT_d[kv, seq, :, g, :], writes=[src])
                sc.op(DVE, lambda e: e.tensor_copy(out=zde.ap[:], in_=src.ap[:].rearrange("p (m rr) -> p rr m", rr=16)),
                      reads=[src], writes=[zde])
                bk = nb()
                for jc in range(2):
                    for r in range(32):
                        sc.op(PE, lambda e, jc=jc, r=r: e.matmul(
                            bk.ap[:, jc * 128:jc * 128 + 127], lhsT=w1s.ap[:, r, jc * 128:(jc + 1) * 128],
                            rhs=zde.ap[:, r % 16, (r // 16):(r // 16) + 127], start=(r == 0), stop=(r == 31)),
                            reads=[w1s, zde], writes=[bk])
                for jc in range(2):
                    sc.op(ACT, lambda e, jc=jc: e.activation(out=hidc.ap[:, jc, 0:127], in_=bk.ap[:, jc * 128:jc * 128 + 127],
                                                             func=AF.Gelu, bias=cvec.ap[:, jc:jc + 1]),
                          reads=[bk, cvec], writes=[hidc])
                bk2 = nb()
                if kv == 0:
                    for jc in range(2):
                        sc.op(PE, lambda e, jc=jc: e.matmul(bk2.ap[:, 0:127], lhsT=w2s.ap[:, jc, :], rhs=hidc.ap[:, jc, 0:127],
                                                            start=(jc == 0), stop=(jc == 1)), reads=[w2s, hidc], writes=[bk2])
                    sc.op(DVE, lambda e, g=g: e.tensor_copy(out=C.KcT[seq].ap[:, g, 0:127], in_=bk2.ap[:, 0:127]),
                          reads=[bk2], writes=[C.KcT[seq]])
                else:
                    for jc in range(2):
                        sc.op(PE, lambda e, jc=jc: e.matmul(bk2.ap[0:127, 0:128], lhsT=hidc.ap[:, jc, 0:127], rhs=w2s.ap[:, jc, :],
                                                            start=(jc == 0), stop=(jc == 1)), reads=[w2s, hidc], writes=[bk2])
                    sc.op(DVE, lambda e, g=g: e.tensor_copy(out=C.Vc[seq].ap[0:127, g, :], in_=bk2.ap[0:127, 0:128]),
                          reads=[bk2], writes=[C.Vc[seq]])
        sc.barrier()
    if C.debug:
        sc.dma(SP, C.dbgc_d[seq, 0], C.KcT[seq].ap[:], reads=[C.KcT[seq]])
        sc.dma(SP, C.dbgc_d[seq, 1], C.Vc[seq].ap[:], reads=[C.Vc[seq]])


def passB(C):
    nc, sc = C.nc, C.sc
    PE, ACT, DVE, POOL, SP = sc.pe, sc.act, sc.dve, sc.pool, sc.sp
    nb, ntf, nsq, xT, hT, vecs, VO = C.nb, C.ntf, C.nsq, C.xT, C.hT, C.vecs, VOFF
    sb, run_steps, panel_src = C.sb, C.run_steps, C.panel_src
    modv, banks, pbf, identb, ones1 = C.modv, C.banks, C.pbf, C.identb, C.ones1
    SCL = 1.0 / math.sqrt(128.0)
    C_hTc = C.hTc
    with ExitStack() as pb:
        m_cmp = sb("m_cmp", [128, 16, 128], BF16, pb)
        m_tri = sb("m_tri", [128, 2, 128], BF16, pb)
        addtab = sb("addtab", [128, 16, 32], F32, pb)
        emat = sb("emat", [32, S], BF16, pb)
        ovl = sb("ovl", [128, 33], BF16, pb)
        wg = sb("wg", [128, KC, 48], BF16, pb)
        gates = sb("gates", [128, 4, 48], F32, pb)
        Pt = [sb("Pt%d" % i, [128, 512], BF16, pb) for i in range(3)]
        oacc = sb("oacc", [128, 4, 128], F32, pb)
        otmp = sb("otmp", [128, 4, 128], F32, pb)
        sm = sb("sm", [128, 64], F32, pb)
        tmpi = sb("tmpi", [128, 4, 32], F32, pb)
        selb = sb("selb", [128, 32], F32, pb)
        selbb = sb("selbb", [128, 32], BF16, pb)
        selT = sb("selT", [32, 128], BF16, pb)
        sc.dma(POOL, m_cmp.ap[:], C.mcmp_d, writes=[m_cmp])
        sc.dma(POOL, m_tri.ap[:], C.mtri_d, writes=[m_tri])
        sc.dma(SP, addtab.ap[:], C.addtab_d, writes=[addtab])
        sc.dma(POOL, emat.ap[:], C.emat_d, writes=[emat])
        sc.dma(POOL, ovl.ap[:], C.ovl_d, writes=[ovl])
        sc.dma(POOL, wg.ap[:], C.wqg_d[:, 2048:2096].rearrange("(kc p) n -> p kc n", p=128), writes=[wg])
        pt_i = [0]
        sb_i = [0]
        o_i = [0]
        otm = hT.ap[:].rearrange("p c t -> p (c t)").rearrange("p (a f) -> p a f", a=4)

        def npt():
            pt_i[0] += 1
            return Pt[pt_i[0] % 3]

        def nbs():
            sb_i[0] += 1
            return banks[sb_i[0] % 3]

        def nbo():
            o_i[0] += 1
            return banks[4 + o_i[0] % 2]
        bkD = banks[6]

        def bc4(ap2d, k):
            return ap2d.unsqueeze(1).to_broadcast([k, 4, 128])

        def v3(ap2d):
            return ap2d.rearrange("p (a b) -> p a b", a=4)

        for seq in range(NSEQ):
            for blk in range(NBLK):
                t0 = blk * TB
                b = seq
                for c in range(KC):
                    sc.dma(SP, xT[c].ap[:], C.x2T_d[seq * NBLK + blk, :, c, :], writes=[xT[c]])
                C.norm_to_hT("10", b)
                qs = ExitStack()
                QT = sb("QT", [128, 16, TB], BF16, qs)
                nt = blk * 4 + 4
                wlo = max(0, blk * 4 - 4)
                Ks = sb("Ks", [128, 4, S], BF16, qs)
                Vs = sb("Vs", [128, 16, 512], BF16, qs)
                Kw = sb("Kw", [128, 4, 1024], BF16, qs)
                Vw = sb("Vw", [128, 8, 512], BF16, qs)
                te = nt * 128
                Ksg = [T(Ks.ap[:, g, :]) for g in range(4)]
                Kwg = [T(Kw.ap[:, g, :]) for g in range(4)]
                for t_ in Ksg:
                    t_.r = dict(Ks.r)
                    qs.callback(C.retire, t_)
                for t_ in Kwg:
                    t_.r = dict(Kw.r)
                    qs.callback(C.retire, t_)
                for g in range(4):
                    sc.dma(SP, Ks.ap[:, g, 0:te], C.kT_d[2, seq, :, g, 0:te], writes=[Ksg[g]])
                    sc.dma(SP, Kw.ap[:, g, 0:te - wlo * 128], C.kT_d[3, seq, :, g, wlo * 128:te], writes=[Kwg[g]])
                sc.dma(SP, Vs.ap[:, 0:nt, :], C.vtm_d[0, seq, 0:te, :].rearrange("(n p) c -> p n c", p=128), writes=[Vs])
                sc.dma(SP, Vw.ap[:, 0:nt - wlo, :], C.vtm_d[1, seq, wlo * 128:te, :].rearrange("(n p) c -> p n c", p=128),
                       writes=[Vw])

                with ExitStack() as ps:
                    C.cosb = sb("cosb", [128, TB], F32, ps)
                    C.sinb = sb("sinb", [128, TB], F32, ps)
                    sc.dma(SP, C.cosb.ap[:], C.cos_d[:, t0:t0 + TB], writes=[C.cosb])
                    sc.dma(SP, C.sinb.ap[:], C.sin_d[:, t0:t0 + TB], writes=[C.sinb])
                    steps = []
                    for g in range(4):
                        def fn(slot, g=g):
                            for j in range(4):
                                h = g * 4 + j
                                bk = nb()
                                for k in range(KC):
                                    sc.op(PE, lambda e, k=k, j=j: e.matmul(
                                        bk.ap[:], lhsT=slot.ap[:, k, j * 128:(j + 1) * 128], rhs=hT.ap[:, k, :],
                                        start=(k == 0), stop=(k == KC - 1)), reads=[slot, C_hTc[k]], writes=[bk])
                                rope_from_bank(C, bk, QT, QT.ap[:, h, :])
                        steps.append((panel_src(C.wqg_d, 0, g * 512), fn))

                    def gate_fn(slot):
                        for tt in range(4):
                            bk = nb()
                            for k in range(KC):
                                sc.op(PE, lambda e, k=k, tt=tt: e.matmul(
                                    bk.ap[:, 0:48], lhsT=hT.ap[:, k, tt * 128:(tt + 1) * 128], rhs=wg.ap[:, k, :],
                                    start=(k == 0), stop=(k == KC - 1)), reads=[wg, C_hTc[k]], writes=[bk])
                            sc.op(ACT, lambda e, tt=tt: e.activation(out=gates.ap[:, tt, :], in_=bk.ap[:, 0:48],
                                                                     func=AF.Sigmoid), reads=[bk], writes=[gates])
                    steps.append((None, gate_fn))
                    run_steps(steps, nxt=C.P_wo)

                with ExitStack() as ps:
                    def finish_branch(bkO, bkDv, den_ap, br, tt, g, first, last):
                        sc.op(DVE, lambda e: e.tensor_scalar(out=sm.ap[:, 0:4], in0=den_ap, scalar1=1e-30, scalar2=None,
                                                             op0=ALU.max), reads=[bkDv], writes=[sm])
                        sc.op(DVE, lambda e: e.reciprocal(out=sm.ap[:, 0:4], in_=sm.ap[:, 0:4]), reads=[sm], writes=[sm])
                        gsl = gates.ap[:, tt, 12 * g:12 * g + 12].rearrange("p (h r) -> p h r", r=3)[:, :, br]
                        sc.op(DVE, lambda e: e.tensor_tensor(out=sm.ap[:, 4:8], in0=sm.ap[:, 0:4], in1=gsl, op=ALU.mult),
                              reads=[sm, gates], writes=[sm])
                        fb = sm.ap[:, 4:8].unsqueeze(2).to_broadcast([128, 4, 128])
                        if first:
                            sc.op(DVE, lambda e: e.tensor_tensor(out=oacc.ap[:], in0=v3(bkO.ap[:]), in1=fb, op=ALU.mult),
                                  reads=[bkO, sm], writes=[oacc])
                        else:
                            sc.op(DVE, lambda e: e.tensor_tensor(out=otmp.ap[:], in0=v3(bkO.ap[:]), in1=fb, op=ALU.mult),
                                  reads=[bkO, sm], writes=[otmp])
                            dst = v3(otm[:, tt, g * 512:(g + 1) * 512]) if last else oacc.ap[:]
                            sc.op(DVE, lambda e: e.tensor_tensor(out=dst, in0=oacc.ap[:], in1=otmp.ap[:], op=ALU.add),
                                  reads=[oacc, otmp], writes=(C_hTc[4 * tt:4 * tt + 4] if last else [oacc]))

                    tiles = []
                    brn = [0]

                    def add_branch(kind, tt, g, T_):
                        q0 = tt * 128
                        Qg = QT.ap[:, 4 * g:4 * g + 4, q0:q0 + 128]
                        st_ = {}
                        bi = brn[0]
                        brn[0] += 1
                        if kind == 0:
                            kts = [0]
                        elif kind == 2:
                            kts = list(range(max(0, T_ - 4), T_ + 1))
                        else:
                            kts = list(range(T_ + 1))
                        for idx, kt in enumerate(kts):
                            first = (idx == 0)
                            lastk = (idx == len(kts) - 1)
                            tl = {}

                            def S(kt=kt, tl=tl):
                                bkS = nbs()
                                P = npt()
                                tl["P"] = P
                                if kind == 0:
                                    sc.op(PE, lambda e: e.matmul(v3(bkS.ap[0:127, :]), lhsT=C.KcT[seq].ap[:, g, 0:127], rhs=Qg,
                                                                 start=True, stop=False), reads=[C.KcT[seq], QT], writes=[bkS])
                                    sc.op(PE, lambda e: e.matmul(v3(bkS.ap[0:127, :]), lhsT=identb.ap[0:127, 0:127],
                                                                 rhs=bc4(m_cmp.ap[0:127, T_, :], 127), start=False, stop=True),
                                          reads=[identb, m_cmp], writes=[bkS])
                                    sc.op(ACT, lambda e: e.activation(out=P.ap[0:127, :], in_=bkS.ap[0:127, :], func=AF.Exp,
                                                                      scale=SCL), reads=[bkS], writes=[P])
                                    return
                                diag = (kt == T_)
                                if kind == 1:
                                    sc.op(PE, lambda e: e.matmul(v3(bkS.ap[:]), lhsT=Ks.ap[:, g, kt * 128:(kt + 1) * 128], rhs=Qg,
                                                                 start=True, stop=False), reads=[Ksg[g], QT], writes=[bkS])
                                    sc.op(PE, lambda e: e.matmul(v3(bkS.ap[:]), lhsT=emat.ap[0:32, kt * 128:(kt + 1) * 128],
                                                                 rhs=bc4(selT.ap[0:32, :], 32), start=False, stop=not diag),
                                          reads=[emat, selT], writes=[bkS])
                                    if diag:
                                        sc.op(PE, lambda e: e.matmul(v3(bkS.ap[:]), lhsT=identb.ap[:],
                                                                     rhs=bc4(m_tri.ap[:, 0, :], 128), start=False, stop=True),
                                              reads=[identb, m_tri], writes=[bkS])
                                else:
                                    lw = kt - wlo
                                    lower = (kt == T_ - 4)
                                    sc.op(PE, lambda e: e.matmul(v3(bkS.ap[:]), lhsT=Kw.ap[:, g, lw * 128:(lw + 1) * 128], rhs=Qg,
                                                                 start=True, stop=not (lower or diag)), reads=[Kwg[g], QT], writes=[bkS])
                                    if lower or diag:
                                        mi = 0 if diag else 1
                                        sc.op(PE, lambda e: e.matmul(v3(bkS.ap[:]), lhsT=identb.ap[:],
                                                                     rhs=bc4(m_tri.ap[:, mi, :], 128), start=False, stop=True),
                                              reads=[identb, m_tri], writes=[bkS])
                                sc.op(ACT, lambda e: e.activation(out=P.ap[:], in_=bkS.ap[:], func=AF.Exp, scale=SCL),
                                      reads=[bkS], writes=[P])

                            def PV(kt=kt, tl=tl, first=first, lastk=lastk):
                                if first:
                                    st_["O"] = banks[3 + bi % 2]
                                    st_["D"] = banks[5 + bi % 2]
                                bkO, bkDv, P = st_["O"], st_["D"], tl["P"]
                                for h in range(4):
                                    s0 = (first and h == 0)
                                    if kind == 0:
                                        sc.op(PE, lambda e, h=h: e.matmul(bkO.ap[:, h * 128:(h + 1) * 128],
                                                                          lhsT=P.ap[0:127, h * 128:(h + 1) * 128],
                                                                          rhs=C.Vc[seq].ap[0:127, g, :], start=s0, stop=True),
                                              reads=[P, C.Vc[seq]], writes=[bkO])
                                        sc.op(PE, lambda e, h=h: e.matmul(bkDv.ap[:, h * 33:(h + 1) * 33],
                                                                          lhsT=P.ap[0:127, h * 128:(h + 1) * 128],
                                                                          rhs=ovl.ap[0:127, :], start=s0, stop=True),
                                              reads=[P, ovl], writes=[bkDv])
                                    else:
                                        if kind == 1:
                                            vsrc, vt = Vs.ap[:, kt, g * 128:(g + 1) * 128], Vs
                                        else:
                                            vsrc, vt = Vw.ap[:, kt - wlo, g * 128:(g + 1) * 128], Vw
                                        sc.op(PE, lambda e, h=h: e.matmul(bkO.ap[:, h * 128:(h + 1) * 128],
                                                                          lhsT=P.ap[:, h * 128:(h + 1) * 128], rhs=vsrc,
                                                                          start=s0, stop=lastk), reads=[P, vt], writes=[bkO])
                                        sc.op(PE, lambda e, h=h: e.matmul(bkDv.ap[:, h:h + 1],
                                                                          lhsT=P.ap[:, h * 128:(h + 1) * 128], rhs=ones1.ap[:, 0:1],
                                                                          start=s0, stop=lastk), reads=[P, ones1], writes=[bkDv])
                                if not lastk:
                                    return
                                if kind == 0:
                                    bd3 = bkDv.ap[:, 0:132].rearrange("p (h c) -> p h c", c=33)
                                    finish_branch(bkO, bkDv, bd3[:, :, 32], 0, tt, g, True, False)
                                    sc.op(DVE, lambda e: e.tensor_tensor(out=tmpi.ap[:], in0=bd3[:, :, 0:32],
                                                                         in1=sm.ap[:, 0:4].unsqueeze(2).to_broadcast([128, 4, 32]),
                                                                         op=ALU.mult), reads=[bkDv, sm], writes=[tmpi])
                                    sc.op(DVE, lambda e: e.tensor_reduce(out=sm.ap[:, 16:48],
                                                                         in_=tmpi.ap[:].rearrange("p h j -> p j h"),
                                                                         axis=AX.X, op=ALU.add), reads=[tmpi], writes=[sm])
                                    sc.op(DVE, lambda e: e.tensor_tensor(out=sm.ap[:, 16:48], in0=sm.ap[:, 16:48],
                                                                         in1=addtab.ap[:, T_, :], op=ALU.add),
                                          reads=[sm, addtab], writes=[sm])
                                    sc.op(DVE, lambda e: e.max(out=sm.ap[:, 8:16], in_=sm.ap[:, 16:48]), reads=[sm], writes=[sm])
                                    sc.op(DVE, lambda e: e.tensor_scalar(out=selb.ap[:], in0=sm.ap[:, 16:48],
                                                                         scalar1=sm.ap[:, 15:16], scalar2=None, op0=ALU.is_ge),
                                          reads=[sm], writes=[selb])
                                    sc.op(DVE, lambda e: e.tensor_scalar(out=selbb.ap[:], in0=selb.ap[:], scalar1=-NEGM,
                                                                         scalar2=NEGM, op0=ALU.mult, op1=ALU.add),
                                          reads=[selb], writes=[selbb])
                                elif kind == 2:
                                    finish_branch(bkO, bkDv, bkDv.ap[:, 0:4], 2, tt, g, False, False)
                                else:
                                    finish_branch(bkO, bkDv, bkDv.ap[:, 0:4], 1, tt, g, False, True)

                            def pre(first=first):
                                if kind == 1 and first:
                                    sc.op(PE, lambda e: e.transpose(pbf.ap[0:32, 0:128], selbb.ap[:], identb.ap[:]),
                                          reads=[selbb, identb], writes=[pbf])
                                    sc.op(ACT, lambda e: e.activation(out=selT.ap[:], in_=pbf.ap[0:32, 0:128], func=AF.Copy),
                                          reads=[pbf], writes=[selT])
                            tiles.append((pre, S, PV))

                    for tt in range(4):
                        for g in range(4):
                            T_ = blk * 4 + tt
                            add_branch(0, tt, g, T_)
                            add_branch(2, tt, g, T_)
                            add_branch(1, tt, g, T_)
                    for i_, (pre, S_, PV_) in enumerate(tiles):
                        if i_ == 0:
                            pre()
                            S_()
                        if i_ + 1 < len(tiles):
                            tiles[i_ + 1][0]()
                            tiles[i_ + 1][1]()
                        PV_()
                    for tt in range(4):
                        for h8 in range(2):
                            for j in range(8):
                                hh = h8 * 8 + j
                                sc.op(PE, lambda e, j=j, hh=hh: e.transpose(pbf.ap[:, j * 128:(j + 1) * 128],
                                                                            otm[:, tt, hh * 128:(hh + 1) * 128], identb.ap[:]),
                                      reads=C_hTc[4 * tt:4 * tt + 4] + [identb], writes=[pbf])
                            sc.op(ACT if h8 else DVE,
                                  (lambda e: e.activation(out=QT.ap[:, h8 * 8:(h8 + 1) * 8, tt * 128:(tt + 1) * 128],
                                                          in_=pbf.ap[:].rearrange("p (a b) -> p a b", a=8), func=AF.Copy)) if h8 else
                                  (lambda e: e.tensor_copy(out=QT.ap[:, h8 * 8:(h8 + 1) * 8, tt * 128:(tt + 1) * 128],
                                                           in_=pbf.ap[:].rearrange("p (a b) -> p a b", a=8))),
                                  reads=[pbf], writes=[QT])
                steps = []
                for og in range(4):
                    def fn(slot, og=og):
                        for oc in range(4):
                            c = og * 4 + oc
                            bk = nb()
                            for h in range(16):
                                sc.op(PE, lambda e, h=h, oc=oc: e.matmul(
                                    bk.ap[:], lhsT=slot.ap[:, h, oc * 128:(oc + 1) * 128], rhs=QT.ap[:, h, :],
                                    start=(h == 0), stop=(h == 15)), reads=[slot, QT], writes=[bk])
                            if oc == 0:
                                C.flush_sq()
                            C.resid_add(bk, c, modv["10"].ap[:, 32 + c, b:b + 1])
                    steps.append((panel_src(C.wo_d, 0, og * 512), fn))
                run_steps(steps, nxt=C.P_mlp[1])
                qs.close()
                if C.debug and blk == 0 and seq == 0:
                    for c in range(KC):
                        sc.dma(SP, C.dbg_d[1, :, c, :], xT[c].ap[:], reads=[xT[c]])
                if C.stop_after == "nsa0":
                    return
                C.mlp_phase(1, b, nxt=([] if (seq == NSEQ - 1 and blk == NBLK - 1) else C.P_q))
                C.norm_to_hT(None, b)
                with ExitStack() as ps:
                    ot = sb("ot", [128, 4, D], F32, ps)
                    ogf, _ = VO["gfin"]
                    for c in range(KC):
                        t_ = ntf()
                        sc.op(DVE, lambda e, c=c: e.scalar_tensor_tensor(out=t_.ap[:], in0=xT[c].ap[:],
                                                                         scalar=vecs.ap[:, ogf + c:ogf + c + 1],
                                                                         in1=C.rstd.ap[:], op0=ALU.mult, op1=ALU.mult),
                              reads=[xT[c], vecs, C.rstd], writes=[t_])
                        bk = nb()
                        for tt in range(4):
                            sc.op(PE, lambda e, tt=tt: e.transpose(bk.ap[:, tt * 128:(tt + 1) * 128],
                                                                   t_.ap[:, tt * 128:(tt + 1) * 128], C.identf.ap[:]),
                                  reads=[t_, C.identf], writes=[bk])
                        if c % 2:
                            sc.op(ACT, lambda e, c=c: e.activation(out=ot.ap[:, :, c * 128:(c + 1) * 128],
                                                                   in_=bk.ap[:].rearrange("p (a b) -> p a b", a=4), func=AF.Copy),
                                  reads=[bk], writes=[ot])
                        else:
                            sc.op(DVE, lambda e, c=c: e.tensor_copy(out=ot.ap[:, :, c * 128:(c + 1) * 128],
                                                                    in_=bk.ap[:].rearrange("p (a b) -> p a b", a=4)),
                                  reads=[bk], writes=[ot])
                    sc.dma(SP, C.out_d[seq, t0:t0 + TB, :].rearrange("(a p) f -> p a f", p=128), ot.ap[:], reads=[ot])


def host_tables():
    half = 64
    freqs = (10000.0 ** (-np.arange(half, dtype=np.float32) / half)).astype(np.float32)
    ang = np.arange(S, dtype=np.float32)[None, :] * freqs[:, None]
    cos = np.cos(ang).astype(np.float32)
    sin = np.sin(ang).astype(np.float32)
    cos2 = np.concatenate([cos, cos], 0)
    sinS = np.concatenate([-sin, sin], 0)
    return {"cos2": np.ascontiguousarray(cos2), "sinS": np.ascontiguousarray(sinS),
            "ident": np.eye(128, dtype=np.float32)}


def make_in_maps(inputs, ncores):
    f32 = lambda a: np.ascontiguousarray(np.asarray(a, np.float32))
    I = {k: np.asarray(v) for k, v in inputs.items()}
    vecs = np.zeros((128, NV), np.float32)

    def put(name, v):
        o, w = VOFF[name]
        vecs[:, o:o + w] = fm(v)
    for l in range(2):
        for s_ in range(2):
            put("mod_b%d%d" % (l, s_), I["mod_b"][l, s_])
            put("g%d%d" % (l, s_), I["norm_g"][l, s_])
    put("kv_mod_b", I["kv_mod_b"])
    put("gkv", I["kv_norm_g"])
    put("gfin", I["final_g"])
    put("b_in_u", I["a_b_in"][0, :GW])
    put("ln_g", I["a_ln_g"][0])
    put("ln_b", I["a_ln_b"][0])
    put("b_out", I["a_b_out"][0])
    shared = {
        "vecs": vecs,
        "mod_w": f32(I["mod_w"]), "kv_mod_w": f32(I["kv_mod_w"]),
        "mlp_w1": f32(I["mlp_w1"]), "mlp_w2": f32(I["mlp_w2"]),
        "a_w_in": f32(I["a_w_in"][0]), "a_w_out": f32(I["a_w_out"][0]),
        "b_in_v": f32(I["a_b_in"][0, GW:][None, :]),
        "w_sT": f32(np.transpose(I["a_w_s"][0], (2, 0, 1))),
        "b_s": f32(I["a_b_s"][0].reshape(1, 2048)),
        "w_kv": f32(I["w_kv"]),
        "cmp_posT": f32(np.stack([I["cmp_pos_k"].T, I["cmp_pos_v"].T])),
        "cmp_w1": f32(np.stack([I["cmp_w1_k"], I["cmp_w1_v"]])),
        "cmp_w2": f32(np.stack([I["cmp_w2_k"], I["cmp_w2_v"]])),
        "w_qg": f32(I["b_w_qg"][0]), "w_o": f32(I["b_w_o"][0]),
    }
    shared.update(host_tables())
    shared.update(host_masks())
    maps = []
    for i in range(ncores):
        m = dict(shared)
        m["x"] = f32(I["x"][NSEQ * i:NSEQ * (i + 1)])
        c = I["c"][NSEQ * i:NSEQ * (i + 1)]
        m["cT"] = f32(np.transpose(c.reshape(NSEQ, KC, 128), (2, 1, 0)))
        maps.append(m)
    return maps


def host_masks():
    n = np.arange(128)[:, None]
    m_cmp = np.zeros((128, 16, 128), np.float32)
    for T_ in range(16):
        t = T_ * 128 + np.arange(128)[None, :]
        m_cmp[:, T_, :] = np.where((16 * n + 31 <= t) & (n < 127), 0.0, NEGM)
    r = np.arange(128)[:, None]
    c = np.arange(128)[None, :]
    m_tri = np.zeros((128, 2, 128), np.float32)
    m_tri[:, 0, :] = np.where(r <= c, 0.0, NEGM)
    m_tri[:, 1, :] = np.where(r > c, 0.0, NEGM)
    addtab = np.zeros((128, 16, 32), np.float32)
    for T_ in range(16):
        t = T_ * 128 + np.arange(128)[:, None]
        j = np.arange(32)[None, :]
        cur = t // 64
        forced = (j == 0) | (j == cur) | (j == cur - 1)
        addtab[:, T_, :] = np.where(j * 64 <= t, np.where(forced, 1e6, 0.0), -1e30)
    emat = np.zeros((32, S), np.float32)
    emat[np.arange(S) // 64, np.arange(S)] = 1.0
    ci = np.arange(128)[:, None]
    sj = np.arange(32)[None, :]
    ovl = np.zeros((128, 33), np.float32)
    ovl[:, :32] = ((ci * 16 <= sj * 64 + 63) & (ci * 16 + 31 >= sj * 64) & (ci < 127)).astype(np.float32)
    ovl[:127, 32] = 1.0
    return {"m_cmp": m_cmp, "m_tri": m_tri, "addtab": addtab, "emat": emat, "ovl": ovl}


_NC_CACHE = {}


def kernel(**inputs):
    ncores = 8
    if "prog" not in _NC_CACHE:
        _NC_CACHE["prog"] = build_program()
    nc = _NC_CACHE["prog"]
    maps = make_in_maps(inputs, ncores)
    res = run_bass_kernel_spmd(nc, maps, core_ids=list(range(ncores)))
    out = np.concatenate([np.asarray(r["out"]) for r in res.results], axis=0)
    return out.astype(np.float32)
```

```python
import math
from contextlib import ExitStack

import numpy as np
import concourse.bass as bass
import concourse.mybir as mybir
from concourse.bass_utils import run_bass_kernel_spmd

F32 = mybir.dt.float32
BF16 = mybir.dt.bfloat16
AF = mybir.ActivationFunctionType
ALU = mybir.AluOpType
AX = mybir.AxisListType

D = 2048
S = 2048
NSEQ = 2
TB = 512
NBLK = S // TB
KC = D // 128
DFF = 8192
GW = 4096
EPS = 1e-6
NEGM = -30000.0


class Tok:
    __slots__ = ("sem", "val")

    def __init__(self, sem, val):
        self.sem = sem
        self.val = val


class T:
    def __init__(self, ap):
        self.ap = ap
        self.w = None
        self.r = {}

    def __getitem__(self, k):
        return self.ap[k]


class Eng:
    def __init__(self, nc, name, eng, stack):
        self.name = name
        self.eng = eng
        self.sem = stack.enter_context(nc.semaphore("s_" + name))
        self.cnt = 0
        self.waited = {}

    def wait(self, tok, raw):
        if tok is None:
            return
        if tok.sem is self.sem and not raw:
            return
        k = id(tok.sem)
        if self.waited.get(k, 0) >= tok.val:
            return
        self.eng.wait_ge(tok.sem, tok.val)
        self.waited[k] = tok.val


class Sched:
    NDMASEM = 16

    def __init__(self, nc, stack):
        self.nc = nc
        self.pe = Eng(nc, "pe", nc.tensor, stack)
        self.act = Eng(nc, "act", nc.scalar, stack)
        self.dve = Eng(nc, "dve", nc.vector, stack)
        self.pool = Eng(nc, "pool", nc.gpsimd, stack)
        self.sp = Eng(nc, "sp", nc.sync, stack)
        self.engs = [self.pe, self.act, self.dve, self.pool, self.sp]
        self.dsems = {id(q): [stack.enter_context(nc.semaphore("dma%s%d" % (q.name, i))) for i in range(self.NDMASEM)]
                      for q in (self.sp, self.pool)}
        self.dn = {id(q): 0 for q in (self.sp, self.pool)}
        self.dma_toks = {}

    def _deps(self, E, reads, writes):
        for t in reads:
            E.wait(t.w, True)
        for t in writes:
            E.wait(t.w, False)
            for tok in t.r.values():
                E.wait(tok, False)

    def _upd(self, tok, reads, writes):
        for t in writes:
            t.w = tok
            t.r = {}
        for t in reads:
            t.r[id(tok.sem)] = tok

    def op(self, E, inst_fn, reads=(), writes=()):
        self._deps(E, reads, writes)
        inst = inst_fn(E.eng)
        E.cnt += 1
        inst.then_inc(E.sem, 1)
        self._upd(Tok(E.sem, E.cnt), reads, writes)

    def dma(self, Q, out_ap, in_ap, reads=(), writes=()):
        n = self.dn[id(Q)]
        slot = n % self.NDMASEM
        sem = self.dsems[id(Q)][slot]
        val = 16 * (n // self.NDMASEM + 1)
        self.dn[id(Q)] = n + 1
        if val > 16:
            Q.wait(Tok(sem, val - 16), True)
        self._deps(Q, reads, writes)
        Q.eng.dma_start(out=out_ap, in_=in_ap).then_inc(sem, 16)
        tok = Tok(sem, val)
        self.dma_toks[(id(Q), slot)] = tok
        self._upd(tok, reads, writes)

    def barrier(self):
        toks = [Tok(e.sem, e.cnt) for e in self.engs if e.cnt > 0] + list(self.dma_toks.values())
        for e in self.engs:
            for tok in toks:
                e.wait(tok, True)

    def finish(self):
        for tok in self.dma_toks.values():
            self.sp.wait(tok, True)


VOFF = {}
_o = 0
for _n, _w in [("mod_b00", 48), ("mod_b01", 48), ("mod_b10", 48), ("mod_b11", 48), ("kv_mod_b", 32),
               ("g00", 16), ("g01", 16), ("g10", 16), ("g11", 16), ("gkv", 16), ("gfin", 16),
               ("b_in_u", 32), ("ln_g", 32), ("ln_b", 32), ("b_out", 16)]:
    VOFF[_n] = (_o, _w)
    _o += _w
NV = _o


def fm(v):
    v = np.asarray(v, np.float32)
    return np.ascontiguousarray(v.reshape(-1, 128).T)


class Ctx:
    pass


def build_program(debug=False, stop_after=None):
    nc = bass.Bass("TRN2", target_bir_lowering=False)
    C = Ctx()
    C.nc = nc

    def din(name, shape, dt=F32):
        return nc.dram_tensor(name, list(shape), dt, kind="ExternalInput").ap()

    dbgkind = "ExternalOutput" if debug else "Internal"
    x_d = din("x", [NSEQ, S, D])
    cT_d = din("cT", [128, KC, NSEQ])
    vecs_d = din("vecs", [128, NV])
    modw_d = din("mod_w", [2, 2, D, 3 * D])
    kvmodw_d = din("kv_mod_w", [D, 2 * D])
    w1_d = din("mlp_w1", [2, D, DFF])
    w2_d = din("mlp_w2", [2, DFF, D])
    win_d = din("a_w_in", [D, 2 * GW])
    wout_d = din("a_w_out", [GW, D])
    binv_d = din("b_in_v", [1, GW])
    wsT_d = din("w_sT", [128, 16, 128])
    bs_d = din("b_s", [1, 2048])
    wkv_d = din("w_kv", [D, 3072])
    cpos_d = din("cmp_posT", [2, 128, 32])
    cw1_d = din("cmp_w1", [2, 4096, 256])
    cw2_d = din("cmp_w2", [2, 256, 128])
    wqg_d = din("w_qg", [D, 2096])
    wo_d = din("w_o", [D, D])
    cos_d = din("cos2", [128, S])
    sin_d = din("sinS", [128, S])
    ident_d = din("ident", [128, 128])
    mcmp_d = din("m_cmp", [128, 16, 128])
    mtri_d = din("m_tri", [128, 2, 128])
    addtab_d = din("addtab", [128, 16, 32])
    emat_d = din("emat", [32, S])
    ovl_d = din("ovl", [128, 33])
    out_d = nc.dram_tensor("out", [NSEQ, S, D], F32, kind="ExternalOutput").ap()

    x2T_d = nc.dram_tensor("x2T", [NSEQ * NBLK, 128, KC, TB], F32, kind=dbgkind).ap()
    kT_d = nc.dram_tensor("kT", [4, NSEQ, 128, 4, S], BF16, kind=dbgkind).ap()
    vtm_d = nc.dram_tensor("vtm", [2, NSEQ, S, 512], BF16, kind=dbgkind).ap()
    dbg_d = nc.dram_tensor("dbg", [8, 128, KC, TB], F32, kind=dbgkind).ap()
    dbgc_d = nc.dram_tensor("dbgc", [NSEQ, 2, 128, 4, 128], BF16, kind=dbgkind).ap()

    with ExitStack() as st:
        sc = Sched(nc, st)
        C.sc = sc
        PE, ACT, DVE, POOL, SP = sc.pe, sc.act, sc.dve, sc.pool, sc.sp

        sbn = [0]
        grave = {}

        def retire(t):
            toks = list(t.r.values())
            if t.w is not None:
                toks.append(t.w)
            for tok in toks:
                k = id(tok.sem)
                if k not in grave or grave[k].val < tok.val:
                    grave[k] = tok

        def sb(name, shape, dt, stack=st):
            sbn[0] += 1
            t = T(stack.enter_context(nc.sbuf_tensor("sb%d_%s" % (sbn[0], name), list(shape), dt)))
            t.r = dict(grave)
            stack.callback(retire, t)
            return t

        banks = [T(st.enter_context(nc.psum_tensor("bank%d" % i, [128, 512], F32))) for i in range(7)]
        pbf = T(st.enter_context(nc.psum_tensor("pbf", [128, 1024], BF16)))
        bank_i = [0]

        def nb():
            b = banks[bank_i[0] % 6]
            bank_i[0] += 1
            return b

        xT = [sb("xT%d" % c, [128, TB], F32) for c in range(KC)]
        hT = sb("hT", [128, KC, TB], BF16)
        hTc = [T(hT.ap[:, c, :]) for c in range(KC)]
        slots = [sb("slot%d" % i, [128, 16, 512], BF16) for i in range(3)]
        slot_i = [0]
        vecs = sb("vecs", [128, NV], F32)
        identf = sb("identf", [128, 128], F32)
        identb = sb("identb", [128, 128], BF16)
        onesD = sb("onesD", [128, 128], BF16)
        ones1 = sb("ones1", [128, 128], BF16)
        condT = sb("condT", [128, KC, NSEQ], BF16)
        cTs = sb("cTs", [128, KC, NSEQ], F32)
        modv = {n: sb("modv_" + n, [128, w, NSEQ], F32) for n, w in
                [("00", 48), ("01", 48), ("10", 48), ("11", 48), ("kv", 32)]}
        Avec = {n: sb("A_" + n, [128, KC, NSEQ], F32) for n in ["00", "01", "10", "11", "kv"]}
        gbo = sb("gbo", [128, KC, NSEQ], F32)
        rstd = sb("rstd", [128, TB], F32)
        epsc = sb("epsc", [128, 1], F32)
        sqt = [sb("sqt%d" % i, [128, TB], BF16) for i in range(2)]
        tmpf = [sb("tmpf%d" % i, [128, TB], F32) for i in range(3)]
        KcT = [sb("KcT%d" % s_, [128, 4, 128], BF16) for s_ in range(NSEQ)]
        Vc = [sb("Vc%d" % s_, [128, 4, 128], BF16) for s_ in range(NSEQ)]
        rr = {"sq": 0, "tf": 0}

        def nsq():
            rr["sq"] += 1
            return sqt[rr["sq"] % 2]

        def ntf():
            rr["tf"] += 1
            return tmpf[rr["tf"] % 3]

        def vcol(name, c=None):
            o, w = VOFF[name]
            if c is None:
                return vecs.ap[:, o:o + w]
            return vecs.ap[:, o + c:o + c + 1]

        def panel_src(w2d, r0, c0):
            key = (str(w2d), r0, c0)
            return (key, w2d[r0:r0 + 2048, c0:c0 + 512].rearrange("(kc p) n -> p kc n", p=128))

        pref = []

        def load_panel(p):
            s_ = slots[slot_i[0] % 3]
            slot_i[0] += 1
            sc.dma(POOL, s_.ap[:], p[1], writes=[s_])
            return s_

        def run_steps(steps, nxt=()):
            n = len(steps)
            assigned = {}
            pidx = [i for i in range(n) if steps[i][0] is not None]
            pos = {i: j for j, i in enumerate(pidx)}

            def issue(j):
                if j < len(pidx):
                    i = pidx[j]
                    if pref:
                        key, s_ = pref.pop(0)
                        assert key == steps[i][0][0], (key, steps[i][0][0])
                        assigned[i] = s_
                    else:
                        assigned[i] = load_panel(steps[i][0])
            issue(0)
            issue(1)
            for i in range(n):
                if i in pos:
                    issue(pos[i] + 2)
                steps[i][1](assigned.get(i))
            assert not pref
            for p in nxt:
                pref.append((p[0], load_panel(p)))

        P_gmlp = [panel_src(win_d, 0, GW), panel_src(win_d, 0, GW + 512)]
        P_mlp = [[panel_src(w1_d[l], 0, 0), panel_src(w1_d[l], 0, 512)] for l in range(2)]
        P_kv = [panel_src(wkv_d, 0, 0), panel_src(wkv_d, 0, 512)]
        P_q = [panel_src(wqg_d, 0, 0), panel_src(wqg_d, 0, 512)]
        P_wo = [panel_src(wo_d, 0, 0), panel_src(wo_d, 0, 512)]

        sc.dma(SP, vecs.ap[:], vecs_d, writes=[vecs])
        sc.dma(SP, identf.ap[:], ident_d, writes=[identf])
        sc.dma(SP, cTs.ap[:], cT_d, writes=[cTs])
        sc.op(DVE, lambda e: e.tensor_copy(out=identb.ap[:], in_=identf.ap[:]), reads=[identf], writes=[identb])
        sc.op(DVE, lambda e: e.memset(onesD.ap[:], 1.0 / D), writes=[onesD])
        sc.op(DVE, lambda e: e.memset(ones1.ap[:], 1.0), writes=[ones1])
        sc.op(DVE, lambda e: e.memset(epsc.ap[:], EPS), writes=[epsc])
        sc.op(ACT, lambda e: e.activation(out=condT.ap[:], in_=cTs.ap[:], func=AF.Silu), reads=[cTs], writes=[condT])

        def mod_steps(name, w2d, ncols, bname):
            steps = []
            for p in range(ncols // 512):
                def fn(slot, p=p):
                    bk = nb()
                    for fc in range(4):
                        for k in range(KC):
                            sc.op(PE, lambda e, fc=fc, k=k: e.matmul(
                                bk.ap[:, fc * NSEQ:(fc + 1) * NSEQ], lhsT=slot.ap[:, k, fc * 128:(fc + 1) * 128],
                                rhs=condT.ap[:, k, :], start=(k == 0), stop=(k == KC - 1)),
                                reads=[slot, condT], writes=[bk])
                    o, _ = VOFF[bname]
                    sc.op(DVE, lambda e: e.tensor_tensor(
                        out=modv[name].ap[:, p * 4:(p + 1) * 4, :],
                        in0=bk.ap[:, 0:4 * NSEQ].rearrange("p (a b) -> p a b", b=NSEQ),
                        in1=vecs.ap[:, o + p * 4:o + (p + 1) * 4].unsqueeze(2).to_broadcast([128, 4, NSEQ]),
                        op=ALU.add), reads=[bk, vecs], writes=[modv[name]])
                steps.append((panel_src(w2d, 0, p * 512), fn))
            return steps

        def derive(name, gname, has_gate_bias=False):
            mv = modv[name]
            A = Avec[name]
            o, _ = VOFF[gname]
            sc.op(DVE, lambda e: e.tensor_scalar(out=A.ap[:], in0=mv.ap[:, 16:32, :], scalar1=1.0, scalar2=None,
                                                 op0=ALU.add), reads=[mv], writes=[A])
            sc.op(DVE, lambda e: e.tensor_tensor(out=A.ap[:], in0=A.ap[:],
                                                 in1=vecs.ap[:, o:o + 16].unsqueeze(2).to_broadcast([128, 16, NSEQ]),
                                                 op=ALU.mult), reads=[A, vecs], writes=[A])

        steps = []
        steps += mod_steps("00", modw_d[0, 0], 3 * D, "mod_b00")
        steps += mod_steps("01", modw_d[0, 1], 3 * D, "mod_b01")
        steps += mod_steps("kv", kvmodw_d, 2 * D, "kv_mod_b")
        steps += mod_steps("10", modw_d[1, 0], 3 * D, "mod_b10")
        steps += mod_steps("11", modw_d[1, 1], 3 * D, "mod_b11")
        run_steps(steps, nxt=P_gmlp)
        derive("00", "g00")
        derive("01", "g01")
        derive("kv", "gkv")
        derive("10", "g10")
        derive("11", "g11")
        ob, _ = VOFF["b_out"]
        sc.op(DVE, lambda e: e.tensor_tensor(out=gbo.ap[:], in0=modv["00"].ap[:, 32:48, :],
                                             in1=vecs.ap[:, ob:ob + 16].unsqueeze(2).to_broadcast([128, 16, NSEQ]),
                                             op=ALU.mult), reads=[modv["00"], vecs], writes=[gbo])

        sqst = {"done": 0, "pend": []}

        def flush_sq():
            bk = banks[6]
            for c in sqst["pend"]:
                assert c == sqst["done"]
                q = nsq()
                sc.op(ACT, lambda e, c=c, q=q: e.activation(out=q.ap[:], in_=xT[c].ap[:], func=AF.Square),
                      reads=[xT[c]], writes=[q])
                sc.op(PE, lambda e, c=c, q=q: e.matmul(bk.ap[:], lhsT=onesD.ap[:], rhs=q.ap[:],
                                                       start=(c == 0), stop=(c == KC - 1)),
                      reads=[q, onesD], writes=[bk])
                sqst["done"] += 1
            sqst["pend"] = []

        def x_final(c):
            sqst["pend"].append(c)

        def norm_to_hT(name, b):
            bk = banks[6]
            flush_sq()
            for c in range(sqst["done"], KC):
                sqst["pend"].append(c)
            flush_sq()
            assert sqst["done"] == KC
            sqst["done"] = 0
            sc.op(ACT, lambda e: e.activation(out=rstd.ap[:], in_=bk.ap[:], func=AF.Sqrt, bias=epsc.ap[:, 0:1]),
                  reads=[bk, epsc], writes=[rstd])
            sc.op(DVE, lambda e: e.reciprocal(out=rstd.ap[:], in_=rstd.ap[:]), reads=[rstd], writes=[rstd])
            if name is None:
                return
            A = Avec[name]
            mv = modv[name]
            for c in range(KC):
                t_ = ntf()
                sc.op(DVE, lambda e, c=c, t_=t_: e.tensor_tensor(out=t_.ap[:], in0=xT[c].ap[:], in1=rstd.ap[:],
                                                                 op=ALU.mult), reads=[xT[c], rstd], writes=[t_])
                sc.op(ACT, lambda e, c=c, t_=t_: e.activation(out=hT.ap[:, c, :], in_=t_.ap[:], func=AF.Identity,
                                                              bias=mv.ap[:, c, b:b + 1], scale=A.ap[:, c, b:b + 1]),
                      reads=[t_, A, mv], writes=[hTc[c]])

        def resid_add(bk, c, gate_ap, bias_ap=None):
            t_ = ntf()
            if bias_ap is None:
                sc.op(ACT, lambda e: e.activation(out=t_.ap[:], in_=bk.ap[:], func=AF.Copy, scale=gate_ap),
                      reads=[bk], writes=[t_])
            else:
                sc.op(ACT, lambda e: e.activation(out=t_.ap[:], in_=bk.ap[:], func=AF.Identity, scale=gate_ap,
                                                  bias=bias_ap), reads=[bk], writes=[t_])
            sc.op(DVE, lambda e: e.tensor_tensor(out=xT[c].ap[:], in0=xT[c].ap[:], in1=t_.ap[:], op=ALU.add),
                  reads=[xT[c], t_], writes=[xT[c]])
            x_final(c)

        def mlp_phase(l, b, nxt=()):
            name = "%d1" % l
            norm_to_hT(name, b)
            with ExitStack() as ps:
                hid = [sb("hid%d" % i, [128, TB], BF16, ps) for i in range(64)]
                steps = []
                for fp in range(16):
                    def fn(slot, fp=fp):
                        for j in range(4):
                            fc = fp * 4 + j
                            bk = nb()
                            for k in range(KC):
                                sc.op(PE, lambda e, k=k, j=j: e.matmul(
                                    bk.ap[:], lhsT=slot.ap[:, k, j * 128:(j + 1) * 128], rhs=hT.ap[:, k, :],
                                    start=(k == 0), stop=(k == KC - 1)), reads=[slot, hTc[k]], writes=[bk])
                            t_ = ntf()
                            sc.op(ACT, lambda e: e.activation(out=t_.ap[:], in_=bk.ap[:], func=AF.Relu),
                                  reads=[bk], writes=[t_])
                            sc.op(DVE, lambda e, fc=fc: e.tensor_tensor(out=hid[fc].ap[:], in0=t_.ap[:], in1=t_.ap[:],
                                                                        op=ALU.mult), reads=[t_], writes=[hid[fc]])
                    steps.append((panel_src(w1_d[l], 0, fp * 512), fn))
                acc = {}
                for og in range(4):
                    for fp in range(4):
                        def fn(slot, og=og, fp=fp):
                            if fp == 0:
                                acc[og] = [nb() for _ in range(4)]
                            for oc in range(4):
                                bk = acc[og][oc]
                                for k in range(16):
                                    sc.op(PE, lambda e, k=k, oc=oc: e.matmul(
                                        bk.ap[:], lhsT=slot.ap[:, k, oc * 128:(oc + 1) * 128],
                                        rhs=hid[fp * 16 + k].ap[:], start=(fp == 0 and k == 0),
                                        stop=(fp == 3 and k == 15)), reads=[slot, hid[fp * 16 + k]], writes=[bk])
                            if fp == 1:
                                flush_sq()
                            if fp == 3:
                                for oc in range(4):
                                    c = og * 4 + oc
                                    resid_add(acc[og][oc], c, modv[name].ap[:, 32 + c, b:b + 1])
                        steps.append((panel_src(w2_d[l], fp * 2048, og * 512), fn))
                run_steps(steps, nxt=nxt)

        C.__dict__.update(locals())
        passA(C)
        if stop_after != "A":
            passB(C)
        sc.barrier()
        sc.finish()
    return nc


def rope_from_bank(C, bk, out_t, out_ap, extra_reads=()):
    sc, ACT, DVE = C.sc, C.sc.act, C.sc.dve
    xs = C.ntf()
    sw = C.ntf()
    sc.op(ACT, lambda e: e.activation(out=xs.ap[:], in_=bk.ap[:], func=AF.Copy), reads=[bk], writes=[xs])
    sc.op(DVE, lambda e: e.tensor_copy(out=sw.ap[0:64, :], in_=xs.ap[64:128, :]), reads=[xs], writes=[sw])
    sc.op(DVE, lambda e: e.tensor_copy(out=sw.ap[64:128, :], in_=xs.ap[0:64, :]), reads=[xs], writes=[sw])
    sc.op(DVE, lambda e: e.tensor_tensor(out=xs.ap[:], in0=xs.ap[:], in1=C.cosb.ap[:], op=ALU.mult),
          reads=[xs, C.cosb, sw], writes=[xs])
    sc.op(DVE, lambda e: e.tensor_tensor(out=sw.ap[:], in0=sw.ap[:], in1=C.sinb.ap[:], op=ALU.mult),
          reads=[sw, C.sinb], writes=[sw])
    sc.op(DVE, lambda e: e.tensor_tensor(out=out_ap, in0=xs.ap[:], in1=sw.ap[:], op=ALU.add),
          reads=[xs, sw] + list(extra_reads), writes=[out_t])


def passA(C):
    nc, sc = C.nc, C.sc
    PE, ACT, DVE, POOL, SP = sc.pe, sc.act, sc.dve, sc.pool, sc.sp
    nb, ntf, nsq, xT, hT, vecs, VO = C.nb, C.ntf, C.nsq, C.xT, C.hT, C.vecs, VOFF
    sb, run_steps, panel_src = C.sb, C.run_steps, C.panel_src
    modv, Avec = C.modv, C.Avec
    C_hTc = C.hTc
    with ExitStack() as pa:
        WcT = sb("WcT", [128, 16, 128], BF16, pa)
        BiasT = sb("BiasT", [128, 32, 128], F32, pa)
        binv = sb("binv", [128, 1536], BF16, pa)
        s1 = sb("s1", [128, 32], F32, pa)
        s2 = sb("s2", [128, 32], F32, pa)
        st4 = sb("st4", [128, 4, 4], F32, pa)
        junk = sb("junk", [128, TB], BF16, pa)
        ko_i = [0]
        with ExitStack() as ps:
            rs_rep = sb("rs_rep", [128, 16, 128], F32, ps)
            bs_rep = sb("bs_rep", [128, 16, 128], F32, ps)
            sc.dma(POOL, WcT.ap[:], C.wsT_d, writes=[WcT])
            for vp in range(8):
                sc.dma(POOL, binv.ap[32 * (vp % 3):32 * (vp % 3) + 1, (vp // 3) * 512:(vp // 3 + 1) * 512],
                       C.binv_d[0:1, vp * 512:(vp + 1) * 512], writes=[binv])
            sc.dma(SP, bs_rep.ap[:].rearrange("p g t -> p (g t)"), C.bs_d.partition_broadcast(128), writes=[bs_rep])
            sc.op(POOL, lambda e: e.affine_select(out=WcT.ap[:], in_=WcT.ap[:], pattern=[[0, 16], [1, 128]],
                                                  compare_op=ALU.is_ge, fill=0.0, base=0, channel_multiplier=-1),
                  reads=[WcT], writes=[WcT])
            for q in range(4):
                bk = nb()
                sc.op(PE, lambda e, q=q: e.matmul(bk.ap[:], lhsT=C.ones1.ap[:],
                                                  rhs=WcT.ap[:, q * 4:(q + 1) * 4, :], start=True, stop=True),
                      reads=[WcT, C.ones1], writes=[bk])
                sc.op(ACT, lambda e, q=q: e.activation(out=rs_rep.ap[:, q * 4:(q + 1) * 4, :].rearrange("p a b -> p (a b)"),
                                                       in_=bk.ap[:], func=AF.Copy), reads=[bk], writes=[rs_rep])
            olb, _ = VO["ln_b"]
            for fc in range(32):
                g = fc // 2
                sc.op(DVE, lambda e, fc=fc, g=g: e.scalar_tensor_tensor(
                    out=BiasT.ap[:, fc, :], in0=rs_rep.ap[:, g, :], scalar=vecs.ap[:, olb + fc:olb + fc + 1],
                    in1=bs_rep.ap[:, g, :], op0=ALU.mult, op1=ALU.add), reads=[rs_rep, bs_rep, vecs], writes=[BiasT])

        olg, _ = VO["ln_g"]
        obu, _ = VO["b_in_u"]
        for seq in range(NSEQ):
            for blk in range(NBLK):
                t0 = blk * TB
                b = seq
                with ExitStack() as ps:
                    xtm = [sb("xtm%d" % i, [128, D], F32, ps) for i in range(4)]
                    for tt in range(4):
                        sc.dma(SP, xtm[tt].ap[:], C.x_d[seq, t0 + tt * 128:t0 + (tt + 1) * 128, :], writes=[xtm[tt]])
                    for c in range(KC):
                        bk = nb()
                        for tt in range(4):
                            sc.op(PE, lambda e, tt=tt, c=c: e.transpose(bk.ap[:, tt * 128:(tt + 1) * 128],
                                                                        xtm[tt].ap[:, c * 128:(c + 1) * 128],
                                                                        C.identf.ap[:]),
                                  reads=[xtm[tt], C.identf], writes=[bk])
                        if c >= 2:
                            C.flush_sq()
                        if c >= 1:
                            C.x_final(c - 1)
                        if c % 2:
                            sc.op(ACT, lambda e, c=c: e.activation(out=xT[c].ap[:], in_=bk.ap[:], func=AF.Copy),
                                  reads=[bk], writes=[xT[c]])
                        else:
                            sc.op(DVE, lambda e, c=c: e.tensor_copy(out=xT[c].ap[:], in_=bk.ap[:]),
                                  reads=[bk], writes=[xT[c]])
                    C.x_final(KC - 1)

                C.norm_to_hT("00", b)
                with ExitStack() as ps:
                    sT = [sb("sT%d" % i, [128, TB], BF16, ps) for i in range(32)]
                    v = [sb("v%d" % i, [128, GW], BF16, ps) for i in range(4)]
                    sc.op(DVE, lambda e: e.memset(s1.ap[:], 0.0), writes=[s1])
                    sc.op(DVE, lambda e: e.memset(s2.ap[:], 0.0), writes=[s2])
                    steps = []
                    for vp in range(8):
                        def fn(slot, vp=vp):
                            for tt in range(4):
                                bk = nb()
                                for k in range(KC):
                                    sc.op(PE, lambda e, k=k, tt=tt: e.matmul(
                                        bk.ap[:], lhsT=hT.ap[:, k, tt * 128:(tt + 1) * 128], rhs=slot.ap[:, k, :],
                                        start=(k == 0), stop=False), reads=[slot, C_hTc[k]], writes=[bk])
                                pb = 32 * (vp % 3)
                                sc.op(PE, lambda e: e.matmul(bk.ap[:], lhsT=C.ones1.ap[pb:pb + 1, :],
                                                             rhs=binv.ap[pb:pb + 1, (vp // 3) * 512:(vp // 3 + 1) * 512],
                                                             start=False, stop=True),
                                      reads=[C.ones1, binv], writes=[bk])
                                col = tt * 8 + vp
                                sc.op(ACT, lambda e, tt=tt, col=col: e.activation(
                                    out=v[tt].ap[:, vp * 512:(vp + 1) * 512], in_=bk.ap[:], func=AF.Gelu,
                                    accum_out=s1.ap[:, col:col + 1]), reads=[bk], writes=[v[tt], s1])
                                sc.op(ACT, lambda e, tt=tt, col=col: e.activation(
                                    out=junk.ap[:], in_=v[tt].ap[:, vp * 512:(vp + 1) * 512], func=AF.Square,
                                    accum_out=s2.ap[:, col:col + 1]), reads=[v[tt]], writes=[junk, s2])
                        steps.append((panel_src(C.win_d, 0, GW + vp * 512), fn))

                    def ln_and_spatial(slot):
                        sc.op(DVE, lambda e: e.tensor_reduce(out=st4.ap[:, 0, :], in_=s1.ap[:].rearrange("p (a b) -> p a b", b=8),
                                                             axis=AX.X, op=ALU.add), reads=[s1], writes=[st4])
                        sc.op(DVE, lambda e: e.tensor_reduce(out=st4.ap[:, 3, :], in_=s2.ap[:].rearrange("p (a b) -> p a b", b=8),
                                                             axis=AX.X, op=ALU.add), reads=[s2, st4], writes=[st4])
                        sc.op(DVE, lambda e: e.tensor_scalar(out=st4.ap[:, 0, :], in0=st4.ap[:, 0, :], scalar1=1.0 / GW,
                                                             scalar2=None, op0=ALU.mult), reads=[st4], writes=[st4])
                        sc.op(DVE, lambda e: e.tensor_tensor(out=st4.ap[:, 1, :], in0=st4.ap[:, 0, :], in1=st4.ap[:, 0, :],
                                                             op=ALU.mult), reads=[st4], writes=[st4])
                        sc.op(DVE, lambda e: e.scalar_tensor_tensor(out=st4.ap[:, 1, :], in0=st4.ap[:, 3, :], scalar=1.0 / GW,
                                                                    in1=st4.ap[:, 1, :], op0=ALU.mult, op1=ALU.subtract),
                              reads=[st4], writes=[st4])
                        sc.op(ACT, lambda e: e.activation(out=st4.ap[:, 2, :], in_=st4.ap[:, 1, :], func=AF.Sqrt,
                                                          bias=C.epsc.ap[:, 0:1]), reads=[st4, C.epsc], writes=[st4])
                        sc.op(DVE, lambda e: e.reciprocal(out=st4.ap[:, 2, :], in_=st4.ap[:, 2, :]), reads=[st4], writes=[st4])
                        for tt in range(4):
                            sc.op(DVE, lambda e, tt=tt: e.tensor_scalar(
                                out=v[tt].ap[:], in0=v[tt].ap[:], scalar1=st4.ap[:, 0, tt:tt + 1],
                                scalar2=st4.ap[:, 2, tt:tt + 1], op0=ALU.subtract, op1=ALU.mult),
                                reads=[v[tt], st4], writes=[v[tt]])
                        for tt in range(4):
                            for f4 in range(8):
                                bk = nb()
                                for j in range(4):
                                    fc = f4 * 4 + j
                                    sc.op(PE, lambda e, j=j, fc=fc, tt=tt: e.matmul(
                                        bk.ap[:, j * 128:(j + 1) * 128], lhsT=v[tt].ap[:, fc * 128:(fc + 1) * 128],
                                        rhs=WcT.ap[:, fc // 2, :], start=True, stop=True),
                                        reads=[v[tt], WcT], writes=[bk])
                                for j in range(4):
                                    fc = f4 * 4 + j
                                    sc.op(DVE, lambda e, j=j, fc=fc, tt=tt: e.scalar_tensor_tensor(
                                        out=sT[fc].ap[:, tt * 128:(tt + 1) * 128], in0=bk.ap[:, j * 128:(j + 1) * 128],
                                        scalar=vecs.ap[:, olg + fc:olg + fc + 1], in1=BiasT.ap[:, fc, :],
                                        op0=ALU.mult, op1=ALU.add), reads=[bk, vecs, BiasT], writes=[sT[fc]])
                    steps.append((None, ln_and_spatial))
                    for up in range(8):
                        def fn(slot, up=up):
                            for j in range(4):
                                fc = up * 4 + j
                                bk = nb()
                                for k in range(KC):
                                    sc.op(PE, lambda e, k=k, j=j: e.matmul(
                                        bk.ap[:], lhsT=slot.ap[:, k, j * 128:(j + 1) * 128], rhs=hT.ap[:, k, :],
                                        start=(k == 0), stop=(k == KC - 1)), reads=[slot, C_hTc[k]], writes=[bk])
                                t_ = ntf()
                                sc.op(ACT, lambda e, fc=fc: e.activation(out=t_.ap[:], in_=bk.ap[:], func=AF.Gelu,
                                                                         bias=vecs.ap[:, obu + fc:obu + fc + 1]),
                                      reads=[bk, vecs], writes=[t_])
                                sc.op(DVE, lambda e, fc=fc: e.tensor_tensor(out=sT[fc].ap[:], in0=t_.ap[:], in1=sT[fc].ap[:],
                                                                            op=ALU.mult), reads=[t_, sT[fc]], writes=[sT[fc]])
                        steps.append((panel_src(C.win_d, 0, up * 512), fn))
                    acc = {}
                    for og in range(4):
                        for fp in range(2):
                            def fn(slot, og=og, fp=fp):
                                if fp == 0:
                                    acc[og] = [nb() for _ in range(4)]
                                for oc in range(4):
                                    bk = acc[og][oc]
                                    for k in range(16):
                                        sc.op(PE, lambda e, k=k, oc=oc: e.matmul(
                                            bk.ap[:], lhsT=slot.ap[:, k, oc * 128:(oc + 1) * 128],
                                            rhs=sT[fp * 16 + k].ap[:], start=(fp == 0 and k == 0),
                                            stop=(fp == 1 and k == 15)), reads=[slot, sT[fp * 16 + k]], writes=[bk])
                                if fp == 1:
                                    C.flush_sq()
                                if fp == 1:
                                    for oc in range(4):
                                        c = og * 4 + oc
                                        C.resid_add(acc[og][oc], c, modv["00"].ap[:, 32 + c, b:b + 1],
                                                    C.gbo.ap[:, c, b:b + 1])
                            steps.append((panel_src(C.wout_d, fp * 2048, og * 512), fn))
                    run_steps(steps, nxt=C.P_mlp[0])
                if C.debug and seq == 0 and blk == 0:
                    for c in range(KC):
                        sc.dma(SP, C.dbg_d[0, :, c, :], xT[c].ap[:], reads=[xT[c]])
                if C.stop_after == "gmlp":
                    return

                C.mlp_phase(0, b, nxt=C.P_kv)

                for c in range(KC):
                    sc.dma(SP, C.x2T_d[seq * NBLK + blk, :, c, :], xT[c].ap[:], reads=[xT[c]])
                C.norm_to_hT("kv", b)
                kvs = ExitStack()
                kout = [sb("kout%d" % i, [128, TB], BF16, kvs) for i in range(4)]
                C.cosb = sb("cosb", [128, TB], F32, kvs)
                C.sinb = sb("sinb", [128, TB], F32, kvs)
                sc.dma(SP, C.cosb.ap[:], C.cos_d[:, t0:t0 + TB], writes=[C.cosb])
                sc.dma(SP, C.sinb.ap[:], C.sin_d[:, t0:t0 + TB], writes=[C.sinb])
                steps = []
                for br in range(6):
                    def fn(slot, br=br):
                        if br in (0, 1, 2, 4):
                            ki = {0: 0, 1: 1, 2: 2, 4: 3}[br]
                            for g in range(4):
                                bk = nb()
                                for k in range(KC):
                                    sc.op(PE, lambda e, k=k, g=g: e.matmul(
                                        bk.ap[:], lhsT=slot.ap[:, k, g * 128:(g + 1) * 128], rhs=hT.ap[:, k, :],
                                        start=(k == 0), stop=(k == KC - 1)), reads=[slot, C_hTc[k]], writes=[bk])
                                ko = kout[ko_i[0] % 4]
                                ko_i[0] += 1
                                if br == 1:
                                    sc.op(ACT, lambda e: e.activation(out=ko.ap[:], in_=bk.ap[:], func=AF.Copy),
                                          reads=[bk], writes=[ko])
                                else:
                                    rope_from_bank(C, bk, ko, ko.ap[:])
                                sc.dma(SP, C.kT_d[ki, seq, :, g, t0:t0 + TB], ko.ap[:], reads=[ko])
                        else:
                            vi = {3: 0, 5: 1}[br]
                            for tt in range(4):
                                bk = nb()
                                for k in range(KC):
                                    sc.op(PE, lambda e, k=k, tt=tt: e.matmul(
                                        bk.ap[:], lhsT=hT.ap[:, k, tt * 128:(tt + 1) * 128], rhs=slot.ap[:, k, :],
                                        start=(k == 0), stop=(k == KC - 1)), reads=[slot, C_hTc[k]], writes=[bk])
                                ko = kout[ko_i[0] % 4]
                                ko_i[0] += 1
                                sc.op(ACT, lambda e: e.activation(out=ko.ap[:], in_=bk.ap[:], func=AF.Copy),
                                      reads=[bk], writes=[ko])
                                sc.dma(SP, C.vtm_d[vi, seq, t0 + tt * 128:t0 + (tt + 1) * 128, :], ko.ap[:], reads=[ko])
                    steps.append((panel_src(C.wkv_d, 0, br * 512), fn))
                run_steps(steps, nxt=(C.P_q if (seq == NSEQ - 1 and blk == NBLK - 1) else C.P_gmlp))
                kvs.close()
                if C.stop_after == "blk0":
                    return
            sc.barrier()
            compress_seq(C, seq)


def compress_seq(C, seq):
    nc, sc = C.nc, C.sc
    PE, ACT, DVE, POOL, SP = sc.pe, sc.act, sc.dve, sc.pool, sc.sp
    nb, sb = C.nb, C.sb
    with ExitStack() as ps:
        w1s_l = [sb("cw1s%d" % i, [128, 32, 256], BF16, ps) for i in range(2)]
        w2s_l = [sb("cw2s%d" % i, [128, 2, 128], BF16, ps) for i in range(2)]
        posT_l = [sb("cposT%d" % i, [128, 32], BF16, ps) for i in range(2)]
        cvec_l = [sb("ccvec%d" % i, [128, 2], F32, ps) for i in range(2)]
        src_l = [sb("csrc%d" % i, [128, S], BF16, ps) for i in range(2)]
        zde_l = [sb("czde%d" % i, [128, 16, 128], BF16, ps) for i in range(2)]
        hidc_l = [sb("chid%d" % i, [128, 2, 128], BF16, ps) for i in range(2)]
        for kv in range(2):
            sc.dma(POOL, w1s_l[kv].ap[:], C.cw1_d[kv].rearrange("(r p) j -> p r j", p=128), writes=[w1s_l[kv]])
            sc.dma(POOL, w2s_l[kv].ap[:], C.cw2_d[kv].rearrange("(a p) d -> p a d", p=128), writes=[w2s_l[kv]])
            sc.dma(POOL, posT_l[kv].ap[:], C.cpos_d[kv], writes=[posT_l[kv]])
        for u in range(8):
            sc.dma(SP, src_l[u % 2].ap[:], C.kT_d[u // 4, seq, :, u % 4, :], writes=[src_l[u % 2]]) if u < 2 else None
        for kv in range(2):
            w1s, w2s, posT, cvec = w1s_l[kv], w2s_l[kv], posT_l[kv], cvec_l[kv]
            bk = nb()
            for jc in range(2):
                for r in range(32):
                    sc.op(PE, lambda e, jc=jc, r=r: e.matmul(bk.ap[:, jc:jc + 1], lhsT=w1s.ap[:, r, jc * 128:(jc + 1) * 128],
                                                             rhs=posT.ap[:, r:r + 1], start=(r == 0), stop=(r == 31)),
                          reads=[w1s, posT], writes=[bk])
            sc.op(DVE, lambda e: e.tensor_copy(out=cvec.ap[:], in_=bk.ap[:, 0:2]), reads=[bk], writes=[cvec])
            for g in range(4):
                u = kv * 4 + g
                src, zde, hidc = src_l[u % 2], zde_l[u % 2], hidc_l[u % 2]
                sc.op(DVE, lambda e: e.tensor_copy(out=zde.ap[:], in_=src.ap[:].rearrange("p (m rr) -> p rr m", rr=16)),
                      reads=[src], writes=[zde])
                if u + 2 < 8:
                    sc.dma(SP, src.ap[:], C.kT_d[(u + 2) // 4, seq, :, (u + 2) % 4, :], writes=[src])
                bk = nb()
                for jc in range(2):
                    for r in range(32):
                        sc.op(PE, lambda e, jc=jc, r=r: e.matmul(
                            bk.ap[:, jc * 128:jc * 128 + 127], lhsT=w1s.ap[:, r, jc * 128:(jc + 1) * 128],
                            rhs=zde.ap[:, r % 16, (r // 16):(r // 16) + 127], start=(r == 0), stop=(r == 31)),
                            reads=[w1s, zde], writes=[bk])
                for jc in range(2):
                    sc.op(ACT, lambda e, jc=jc: e.activation(out=hidc.ap[:, jc, 0:127], in_=bk.ap[:, jc * 128:jc * 128 + 127],
                                                             func=AF.Gelu, bias=cvec.ap[:, jc:jc + 1]),
                          reads=[bk, cvec], writes=[hidc])
                bk2 = nb()
                if kv == 0:
                    for jc in range(2):
                        sc.op(PE, lambda e, jc=jc: e.matmul(bk2.ap[:, 0:127], lhsT=w2s.ap[:, jc, :], rhs=hidc.ap[:, jc, 0:127],
                                                            start=(jc == 0), stop=(jc == 1)), reads=[w2s, hidc], writes=[bk2])
                    sc.op(DVE, lambda e, g=g: e.tensor_copy(out=C.KcT[seq].ap[:, g, 0:127], in_=bk2.ap[:, 0:127]),
                          reads=[bk2], writes=[C.KcT[seq]])
                else:
                    for jc in range(2):
                        sc.op(PE, lambda e, jc=jc: e.matmul(bk2.ap[0:127, 0:128], lhsT=hidc.ap[:, jc, 0:127], rhs=w2s.ap[:, jc, :],
                                                            start=(jc == 0), stop=(jc == 1)), reads=[w2s, hidc], writes=[bk2])
                    sc.op(DVE, lambda e, g=g: e.tensor_copy(out=C.Vc[seq].ap[0:127, g, :], in_=bk2.ap[0:127, 0:128]),
                          reads=[bk2], writes=[C.Vc[seq]])
        sc.barrier()
    if C.debug:
        sc.dma(SP, C.dbgc_d[seq, 0], C.KcT[seq].ap[:], reads=[C.KcT[seq]])
        sc.dma(SP, C.dbgc_d[seq, 1], C.Vc[seq].ap[:], reads=[C.Vc[seq]])


def passB(C):
    nc, sc = C.nc, C.sc
    PE, ACT, DVE, POOL, SP = sc.pe, sc.act, sc.dve, sc.pool, sc.sp
    nb, ntf, nsq, xT, hT, vecs, VO = C.nb, C.ntf, C.nsq, C.xT, C.hT, C.vecs, VOFF
    sb, run_steps, panel_src = C.sb, C.run_steps, C.panel_src
    modv, banks, pbf, identb, ones1 = C.modv, C.banks, C.pbf, C.identb, C.ones1
    SCL = 1.0 / math.sqrt(128.0)
    C_hTc = C.hTc
    with ExitStack() as pb:
        m_cmp = sb("m_cmp", [128, 16, 128], BF16, pb)
        m_tri = sb("m_tri", [128, 2, 128], BF16, pb)
        addtab = sb("addtab", [128, 16, 32], F32, pb)
        emat = sb("emat", [32, S], BF16, pb)
        ovl = sb("ovl", [128, 33], BF16, pb)
        wg = sb("wg", [128, KC, 48], BF16, pb)
        gates = sb("gates", [128, 4, 48], F32, pb)
        Pt = [sb("Pt%d" % i, [128, 512], BF16, pb) for i in range(3)]
        oacc = sb("oacc", [128, 4, 128], F32, pb)
        otmp = sb("otmp", [128, 4, 128], F32, pb)
        sm = sb("sm", [128, 64], F32, pb)
        tmpi = sb("tmpi", [128, 4, 32], F32, pb)
        selb = sb("selb", [128, 32], F32, pb)
        selbb = sb("selbb", [128, 32], BF16, pb)
        selT = sb("selT", [32, 128], BF16, pb)
        sc.dma(POOL, m_cmp.ap[:], C.mcmp_d, writes=[m_cmp])
        sc.dma(POOL, m_tri.ap[:], C.mtri_d, writes=[m_tri])
        sc.dma(SP, addtab.ap[:], C.addtab_d, writes=[addtab])
        sc.dma(POOL, emat.ap[:], C.emat_d, writes=[emat])
        sc.dma(POOL, ovl.ap[:], C.ovl_d, writes=[ovl])
        sc.dma(POOL, wg.ap[:], C.wqg_d[:, 2048:2096].rearrange("(kc p) n -> p kc n", p=128), writes=[wg])
        pt_i = [0]
        sb_i = [0]
        o_i = [0]
        otm = hT.ap[:].rearrange("p c t -> p (c t)").rearrange("p (a f) -> p a f", a=4)

        def npt():
            pt_i[0] += 1
            return Pt[pt_i[0] % 3]

        def nbs():
            sb_i[0] += 1
            return banks[sb_i[0] % 3]

        def nbo():
            o_i[0] += 1
            return banks[4 + o_i[0] % 2]
        bkD = banks[6]

        def bc4(ap2d, k):
            return ap2d.unsqueeze(1).to_broadcast([k, 4, 128])

        def v3(ap2d):
            return ap2d.rearrange("p (a b) -> p a b", a=4)

        for seq in range(NSEQ):
            for blk in range(NBLK):
                t0 = blk * TB
                b = seq
                for c in range(KC):
                    sc.dma(SP, xT[c].ap[:], C.x2T_d[seq * NBLK + blk, :, c, :], writes=[xT[c]])
                C.norm_to_hT("10", b)
                qs = ExitStack()
                QT = sb("QT", [128, 16, TB], BF16, qs)
                nt = blk * 4 + 4
                wlo = max(0, blk * 4 - 4)
                Ks = sb("Ks", [128, 4, S], BF16, qs)
                Vs = sb("Vs", [128, 16, 512], BF16, qs)
                Kw = sb("Kw", [128, 4, 1024], BF16, qs)
                Vw = sb("Vw", [128, 8, 512], BF16, qs)
                te = nt * 128
                Ksg = [T(Ks.ap[:, g, :]) for g in range(4)]
                Kwg = [T(Kw.ap[:, g, :]) for g in range(4)]
                for t_ in Ksg:
                    t_.r = dict(Ks.r)
                    qs.callback(C.retire, t_)
                for t_ in Kwg:
                    t_.r = dict(Kw.r)
                    qs.callback(C.retire, t_)
                for g in range(4):
                    sc.dma(SP, Ks.ap[:, g, 0:te], C.kT_d[2, seq, :, g, 0:te], writes=[Ksg[g]])
                    sc.dma(SP, Kw.ap[:, g, 0:te - wlo * 128], C.kT_d[3, seq, :, g, wlo * 128:te], writes=[Kwg[g]])
                sc.dma(SP, Vs.ap[:, 0:nt, :], C.vtm_d[0, seq, 0:te, :].rearrange("(n p) c -> p n c", p=128), writes=[Vs])
                sc.dma(SP, Vw.ap[:, 0:nt - wlo, :], C.vtm_d[1, seq, wlo * 128:te, :].rearrange("(n p) c -> p n c", p=128),
                       writes=[Vw])

                with ExitStack() as ps:
                    C.cosb = sb("cosb", [128, TB], F32, ps)
                    C.sinb = sb("sinb", [128, TB], F32, ps)
                    sc.dma(SP, C.cosb.ap[:], C.cos_d[:, t0:t0 + TB], writes=[C.cosb])
                    sc.dma(SP, C.sinb.ap[:], C.sin_d[:, t0:t0 + TB], writes=[C.sinb])
                    steps = []
                    for g in range(4):
                        def fn(slot, g=g):
                            for j in range(4):
                                h = g * 4 + j
                                bk = nb()
                                for k in range(KC):
                                    sc.op(PE, lambda e, k=k, j=j: e.matmul(
                                        bk.ap[:], lhsT=slot.ap[:, k, j * 128:(j + 1) * 128], rhs=hT.ap[:, k, :],
                                        start=(k == 0), stop=(k == KC - 1)), reads=[slot, C_hTc[k]], writes=[bk])
                                rope_from_bank(C, bk, QT, QT.ap[:, h, :])
                        steps.append((panel_src(C.wqg_d, 0, g * 512), fn))

                    def gate_fn(slot):
                        for tt in range(4):
                            bk = nb()
                            for k in range(KC):
                                sc.op(PE, lambda e, k=k, tt=tt: e.matmul(
                                    bk.ap[:, 0:48], lhsT=hT.ap[:, k, tt * 128:(tt + 1) * 128], rhs=wg.ap[:, k, :],
                                    start=(k == 0), stop=(k == KC - 1)), reads=[wg, C_hTc[k]], writes=[bk])
                            sc.op(ACT, lambda e, tt=tt: e.activation(out=gates.ap[:, tt, :], in_=bk.ap[:, 0:48],
                                                                     func=AF.Sigmoid), reads=[bk], writes=[gates])
                    steps.append((None, gate_fn))
                    run_steps(steps, nxt=C.P_wo)

                with ExitStack() as ps:
                    def finish_branch(bkO, bkDv, den_ap, br, tt, g, first, last):
                        sc.op(DVE, lambda e: e.tensor_scalar(out=sm.ap[:, 0:4], in0=den_ap, scalar1=1e-30, scalar2=None,
                                                             op0=ALU.max), reads=[bkDv], writes=[sm])
                        sc.op(DVE, lambda e: e.reciprocal(out=sm.ap[:, 0:4], in_=sm.ap[:, 0:4]), reads=[sm], writes=[sm])
                        gsl = gates.ap[:, tt, 12 * g:12 * g + 12].rearrange("p (h r) -> p h r", r=3)[:, :, br]
                        sc.op(DVE, lambda e: e.tensor_tensor(out=sm.ap[:, 4:8], in0=sm.ap[:, 0:4], in1=gsl, op=ALU.mult),
                              reads=[sm, gates], writes=[sm])
                        fb = sm.ap[:, 4:8].unsqueeze(2).to_broadcast([128, 4, 128])
                        if first:
                            sc.op(DVE, lambda e: e.tensor_tensor(out=oacc.ap[:], in0=v3(bkO.ap[:]), in1=fb, op=ALU.mult),
                                  reads=[bkO, sm], writes=[oacc])
                        else:
                            sc.op(DVE, lambda e: e.tensor_tensor(out=otmp.ap[:], in0=v3(bkO.ap[:]), in1=fb, op=ALU.mult),
                                  reads=[bkO, sm], writes=[otmp])
                            dst = v3(otm[:, tt, g * 512:(g + 1) * 512]) if last else oacc.ap[:]
                            sc.op(DVE, lambda e: e.tensor_tensor(out=dst, in0=oacc.ap[:], in1=otmp.ap[:], op=ALU.add),
                                  reads=[oacc, otmp], writes=(C_hTc[4 * tt:4 * tt + 4] if last else [oacc]))

                    tiles = []
                    brn = [0]

                    def add_branch(kind, tt, g, T_):
                        q0 = tt * 128
                        Qg = QT.ap[:, 4 * g:4 * g + 4, q0:q0 + 128]
                        st_ = {}
                        bi = brn[0]
                        brn[0] += 1
                        if kind == 0:
                            kts = [0]
                        elif kind == 2:
                            kts = list(range(max(0, T_ - 4), T_ + 1))
                        else:
                            kts = list(range(T_ + 1))
                        for idx, kt in enumerate(kts):
                            first = (idx == 0)
                            lastk = (idx == len(kts) - 1)
                            tl = {}

                            def S(kt=kt, tl=tl):
                                bkS = nbs()
                                P = npt()
                                tl["P"] = P
                                if kind == 0:
                                    sc.op(PE, lambda e: e.matmul(v3(bkS.ap[0:127, :]), lhsT=C.KcT[seq].ap[:, g, 0:127], rhs=Qg,
                                                                 start=True, stop=False), reads=[C.KcT[seq], QT], writes=[bkS])
                                    sc.op(PE, lambda e: e.matmul(v3(bkS.ap[0:127, :]), lhsT=identb.ap[0:127, 0:127],
                                                                 rhs=bc4(m_cmp.ap[0:127, T_, :], 127), start=False, stop=True),
                                          reads=[identb, m_cmp], writes=[bkS])
                                    sc.op(ACT, lambda e: e.activation(out=P.ap[0:127, :], in_=bkS.ap[0:127, :], func=AF.Exp,
                                                                      scale=SCL), reads=[bkS], writes=[P])
                                    return
                                diag = (kt == T_)
                                if kind == 1:
                                    sc.op(PE, lambda e: e.matmul(v3(bkS.ap[:]), lhsT=Ks.ap[:, g, kt * 128:(kt + 1) * 128], rhs=Qg,
                                                                 start=True, stop=False), reads=[Ksg[g], QT], writes=[bkS])
                                    sc.op(PE, lambda e: e.matmul(v3(bkS.ap[:]), lhsT=emat.ap[0:32, kt * 128:(kt + 1) * 128],
                                                                 rhs=bc4(selT.ap[0:32, :], 32), start=False, stop=not diag),
                                          reads=[emat, selT], writes=[bkS])
                                    if diag:
                                        sc.op(PE, lambda e: e.matmul(v3(bkS.ap[:]), lhsT=identb.ap[:],
                                                                     rhs=bc4(m_tri.ap[:, 0, :], 128), start=False, stop=True),
                                              reads=[identb, m_tri], writes=[bkS])
                                else:
                                    lw = kt - wlo
                                    lower = (kt == T_ - 4)
                                    sc.op(PE, lambda e: e.matmul(v3(bkS.ap[:]), lhsT=Kw.ap[:, g, lw * 128:(lw + 1) * 128], rhs=Qg,
                                                                 start=True, stop=not (lower or diag)), reads=[Kwg[g], QT], writes=[bkS])
                                    if lower or diag:
                                        mi = 0 if diag else 1
                                        sc.op(PE, lambda e: e.matmul(v3(bkS.ap[:]), lhsT=identb.ap[:],
                                                                     rhs=bc4(m_tri.ap[:, mi, :], 128), start=False, stop=True),
                                              reads=[identb, m_tri], writes=[bkS])
                                sc.op(ACT, lambda e: e.activation(out=P.ap[:], in_=bkS.ap[:], func=AF.Exp, scale=SCL),
                                      reads=[bkS], writes=[P])

                            def PV(kt=kt, tl=tl, first=first, lastk=lastk):
                                if first:
                                    st_["O"] = banks[3 + bi % 2]
                                    st_["D"] = banks[5 + bi % 2]
                                bkO, bkDv, P = st_["O"], st_["D"], tl["P"]
                                for h in range(4):
                                    s0 = (first and h == 0)
                                    if kind == 0:
                                        sc.op(PE, lambda e, h=h: e.matmul(bkO.ap[:, h * 128:(h + 1) * 128],
                                                                          lhsT=P.ap[0:127, h * 128:(h + 1) * 128],
                                                                          rhs=C.Vc[seq].ap[0:127, g, :], start=s0, stop=True),
                                              reads=[P, C.Vc[seq]], writes=[bkO])
                                        sc.op(PE, lambda e, h=h: e.matmul(bkDv.ap[:, h * 33:(h + 1) * 33],
                                                                          lhsT=P.ap[0:127, h * 128:(h + 1) * 128],
                                                                          rhs=ovl.ap[0:127, :], start=s0, stop=True),
                                              reads=[P, ovl], writes=[bkDv])
                                    else:
                                        if kind == 1:
                                            vsrc, vt = Vs.ap[:, kt, g * 128:(g + 1) * 128], Vs
                                        else:
                                            vsrc, vt = Vw.ap[:, kt - wlo, g * 128:(g + 1) * 128], Vw
                                        sc.op(PE, lambda e, h=h: e.matmul(bkO.ap[:, h * 128:(h + 1) * 128],
                                                                          lhsT=P.ap[:, h * 128:(h + 1) * 128], rhs=vsrc,
                                                                          start=s0, stop=lastk), reads=[P, vt], writes=[bkO])
                                        sc.op(PE, lambda e, h=h: e.matmul(bkDv.ap[:, h:h + 1],
                                                                          lhsT=P.ap[:, h * 128:(h + 1) * 128], rhs=ones1.ap[:, 0:1],
                                                                          start=s0, stop=lastk), reads=[P, ones1], writes=[bkDv])
                                if not lastk:
                                    return
                                if kind == 0:
                                    bd3 = bkDv.ap[:, 0:132].rearrange("p (h c) -> p h c", c=33)
                                    finish_branch(bkO, bkDv, bd3[:, :, 32], 0, tt, g, True, False)
                                    sc.op(DVE, lambda e: e.tensor_tensor(out=tmpi.ap[:], in0=bd3[:, :, 0:32],
                                                                         in1=sm.ap[:, 0:4].unsqueeze(2).to_broadcast([128, 4, 32]),
                                                                         op=ALU.mult), reads=[bkDv, sm], writes=[tmpi])
                                    sc.op(DVE, lambda e: e.tensor_reduce(out=sm.ap[:, 16:48],
                                                                         in_=tmpi.ap[:].rearrange("p h j -> p j h"),
                                                                         axis=AX.X, op=ALU.add), reads=[tmpi], writes=[sm])
                                    sc.op(DVE, lambda e: e.tensor_tensor(out=sm.ap[:, 16:48], in0=sm.ap[:, 16:48],
                                                                         in1=addtab.ap[:, T_, :], op=ALU.add),
                                          reads=[sm, addtab], writes=[sm])
                                    sc.op(DVE, lambda e: e.max(out=sm.ap[:, 8:16], in_=sm.ap[:, 16:48]), reads=[sm], writes=[sm])
                                    sc.op(DVE, lambda e: e.tensor_scalar(out=selb.ap[:], in0=sm.ap[:, 16:48],
                                                                         scalar1=sm.ap[:, 15:16], scalar2=None, op0=ALU.is_ge),
                                          reads=[sm], writes=[selb])
                                    sc.op(DVE, lambda e: e.tensor_scalar(out=selbb.ap[:], in0=selb.ap[:], scalar1=-NEGM,
                                                                         scalar2=NEGM, op0=ALU.mult, op1=ALU.add),
                                          reads=[selb], writes=[selbb])
                                elif kind == 2:
                                    finish_branch(bkO, bkDv, bkDv.ap[:, 0:4], 2, tt, g, False, False)
                                else:
                                    finish_branch(bkO, bkDv, bkDv.ap[:, 0:4], 1, tt, g, False, True)

                            def pre(first=first):
                                if kind == 1 and first:
                                    sc.op(PE, lambda e: e.transpose(pbf.ap[0:32, 0:128], selbb.ap[:], identb.ap[:]),
                                          reads=[selbb, identb], writes=[pbf])
                                    sc.op(ACT, lambda e: e.activation(out=selT.ap[:], in_=pbf.ap[0:32, 0:128], func=AF.Copy),
                                          reads=[pbf], writes=[selT])
                            tiles.append((pre, S, PV))

                    for tt in range(4):
                        for g in range(4):
                            T_ = blk * 4 + tt
                            add_branch(0, tt, g, T_)
                            add_branch(2, tt, g, T_)
                            add_branch(1, tt, g, T_)
                    for i_, (pre, S_, PV_) in enumerate(tiles):
                        if i_ == 0:
                            pre()
                            S_()
                        if i_ + 1 < len(tiles):
                            tiles[i_ + 1][0]()
                            tiles[i_ + 1][1]()
                        PV_()
                    for tt in range(4):
                        for h8 in range(2):
                            for j in range(8):
                                hh = h8 * 8 + j
                                sc.op(PE, lambda e, j=j, hh=hh: e.transpose(pbf.ap[:, j * 128:(j + 1) * 128],
                                                                            otm[:, tt, hh * 128:(hh + 1) * 128], identb.ap[:]),
                                      reads=C_hTc[4 * tt:4 * tt + 4] + [identb], writes=[pbf])
                            sc.op(ACT if h8 else DVE,
                                  (lambda e: e.activation(out=QT.ap[:, h8 * 8:(h8 + 1) * 8, tt * 128:(tt + 1) * 128],
                                                          in_=pbf.ap[:].rearrange("p (a b) -> p a b", a=8), func=AF.Copy)) if h8 else
                                  (lambda e: e.tensor_copy(out=QT.ap[:, h8 * 8:(h8 + 1) * 8, tt * 128:(tt + 1) * 128],
                                                           in_=pbf.ap[:].rearrange("p (a b) -> p a b", a=8))),
                                  reads=[pbf], writes=[QT])
                steps = []
                for og in range(4):
                    def fn(slot, og=og):
                        for oc in range(4):
                            c = og * 4 + oc
                            bk = nb()
                            for h in range(16):
                                sc.op(PE, lambda e, h=h, oc=oc: e.matmul(
                                    bk.ap[:], lhsT=slot.ap[:, h, oc * 128:(oc + 1) * 128], rhs=QT.ap[:, h, :],
                                    start=(h == 0), stop=(h == 15)), reads=[slot, QT], writes=[bk])
                            if oc == 0:
                                C.flush_sq()
                            C.resid_add(bk, c, modv["10"].ap[:, 32 + c, b:b + 1])
                    steps.append((panel_src(C.wo_d, 0, og * 512), fn))
                run_steps(steps, nxt=C.P_mlp[1])
                qs.close()
                if C.debug and blk == 0 and seq == 0:
                    for c in range(KC):
                        sc.dma(SP, C.dbg_d[1, :, c, :], xT[c].ap[:], reads=[xT[c]])
                if C.stop_after == "nsa0":
                    return
                C.mlp_phase(1, b, nxt=([] if (seq == NSEQ - 1 and blk == NBLK - 1) else C.P_q))
                C.norm_to_hT(None, b)
                with ExitStack() as ps:
                    ot = sb("ot", [128, 4, D], F32, ps)
                    ogf, _ = VO["gfin"]
                    for c in range(KC):
                        t_ = ntf()
                        sc.op(DVE, lambda e, c=c: e.scalar_tensor_tensor(out=t_.ap[:], in0=xT[c].ap[:],
                                                                         scalar=vecs.ap[:, ogf + c:ogf + c + 1],
                                                                         in1=C.rstd.ap[:], op0=ALU.mult, op1=ALU.mult),
                              reads=[xT[c], vecs, C.rstd], writes=[t_])
                        bk = nb()
                        for tt in range(4):
                            sc.op(PE, lambda e, tt=tt: e.transpose(bk.ap[:, tt * 128:(tt + 1) * 128],
                                                                   t_.ap[:, tt * 128:(tt + 1) * 128], C.identf.ap[:]),
                                  reads=[t_, C.identf], writes=[bk])
                        if c % 2:
                            sc.op(ACT, lambda e, c=c: e.activation(out=ot.ap[:, :, c * 128:(c + 1) * 128],
                                                                   in_=bk.ap[:].rearrange("p (a b) -> p a b", a=4), func=AF.Copy),
                                  reads=[bk], writes=[ot])
                        else:
                            sc.op(DVE, lambda e, c=c: e.tensor_copy(out=ot.ap[:, :, c * 128:(c + 1) * 128],
                                                                    in_=bk.ap[:].rearrange("p (a b) -> p a b", a=4)),
                                  reads=[bk], writes=[ot])
                    sc.dma(SP, C.out_d[seq, t0:t0 + TB, :].rearrange("(a p) f -> p a f", p=128), ot.ap[:], reads=[ot])


def host_tables():
    half = 64
    freqs = (10000.0 ** (-np.arange(half, dtype=np.float32) / half)).astype(np.float32)
    ang = np.arange(S, dtype=np.float32)[None, :] * freqs[:, None]
    cos = np.cos(ang).astype(np.float32)
    sin = np.sin(ang).astype(np.float32)
    cos2 = np.concatenate([cos, cos], 0)
    sinS = np.concatenate([-sin, sin], 0)
    return {"cos2": np.ascontiguousarray(cos2), "sinS": np.ascontiguousarray(sinS),
            "ident": np.eye(128, dtype=np.float32)}


def make_in_maps(inputs, ncores):
    f32 = lambda a: np.ascontiguousarray(np.asarray(a, np.float32))
    I = {k: np.asarray(v) for k, v in inputs.items()}
    vecs = np.zeros((128, NV), np.float32)

    def put(name, v):
        o, w = VOFF[name]
        vecs[:, o:o + w] = fm(v)
    for l in range(2):
        for s_ in range(2):
            put("mod_b%d%d" % (l, s_), I["mod_b"][l, s_])
            put("g%d%d" % (l, s_), I["norm_g"][l, s_])
    put("kv_mod_b", I["kv_mod_b"])
    put("gkv", I["kv_norm_g"])
    put("gfin", I["final_g"])
    put("b_in_u", I["a_b_in"][0, :GW])
    put("ln_g", I["a_ln_g"][0])
    put("ln_b", I["a_ln_b"][0])
    put("b_out", I["a_b_out"][0])
    shared = {
        "vecs": vecs,
        "mod_w": f32(I["mod_w"]), "kv_mod_w": f32(I["kv_mod_w"]),
        "mlp_w1": f32(I["mlp_w1"]), "mlp_w2": f32(I["mlp_w2"]),
        "a_w_in": f32(I["a_w_in"][0]), "a_w_out": f32(I["a_w_out"][0]),
        "b_in_v": f32(I["a_b_in"][0, GW:][None, :]),
        "w_sT": f32(np.transpose(I["a_w_s"][0], (2, 0, 1))),
        "b_s": f32(I["a_b_s"][0].reshape(1, 2048)),
        "w_kv": f32(I["w_kv"]),
        "cmp_posT": f32(np.stack([I["cmp_pos_k"].T, I["cmp_pos_v"].T])),
        "cmp_w1": f32(np.stack([I["cmp_w1_k"], I["cmp_w1_v"]])),
        "cmp_w2": f32(np.stack([I["cmp_w2_k"], I["cmp_w2_v"]])),
        "w_qg": f32(I["b_w_qg"][0]), "w_o": f32(I["b_w_o"][0]),
    }
    shared.update(host_tables())
    shared.update(host_masks())
    maps = []
    for i in range(ncores):
        m = dict(shared)
        m["x"] = f32(I["x"][NSEQ * i:NSEQ * (i + 1)])
        c = I["c"][NSEQ * i:NSEQ * (i + 1)]
        m["cT"] = f32(np.transpose(c.reshape(NSEQ, KC, 128), (2, 1, 0)))
        maps.append(m)
    return maps


def host_masks():
    n = np.arange(128)[:, None]
    m_cmp = np.zeros((128, 16, 128), np.float32)
    for T_ in range(16):
        t = T_ * 128 + np.arange(128)[None, :]
        m_cmp[:, T_, :] = np.where((16 * n + 31 <= t) & (n < 127), 0.0, NEGM)
    r = np.arange(128)[:, None]
    c = np.arange(128)[None, :]
    m_tri = np.zeros((128, 2, 128), np.float32)
    m_tri[:, 0, :] = np.where(r <= c, 0.0, NEGM)
    m_tri[:, 1, :] = np.where(r > c, 0.0, NEGM)
    addtab = np.zeros((128, 16, 32), np.float32)
    for T_ in range(16):
        t = T_ * 128 + np.arange(128)[:, None]
        j = np.arange(32)[None, :]
        cur = t // 64
        forced = (j == 0) | (j == cur) | (j == cur - 1)
        addtab[:, T_, :] = np.where(j * 64 <= t, np.where(forced, 1e6, 0.0), -1e30)
    emat = np.zeros((32, S), np.float32)
    emat[np.arange(S) // 64, np.arange(S)] = 1.0
    ci = np.arange(128)[:, None]
    sj = np.arange(32)[None, :]
    ovl = np.zeros((128, 33), np.float32)
    ovl[:, :32] = ((ci * 16 <= sj * 64 + 63) & (ci * 16 + 31 >= sj * 64) & (ci < 127)).astype(np.float32)
    ovl[:127, 32] = 1.0
    return {"m_cmp": m_cmp, "m_tri": m_tri, "addtab": addtab, "emat": emat, "ovl": ovl}


_NC_CACHE = {}


def kernel(**inputs):
    ncores = 8
    if "prog" not in _NC_CACHE:
        _NC_CACHE["prog"] = build_program()
    nc = _NC_CACHE["prog"]
    maps = make_in_maps(inputs, ncores)
    res = run_bass_kernel_spmd(nc, maps, core_ids=list(range(ncores)))
    out = np.concatenate([np.asarray(r["out"]) for r in res.results], axis=0)
    return out.astype(np.float32)
```
